# Optimizing a Trainium2 kernel written in Bass

```python
import jax, jax.numpy as jnp
from jax import lax
import numpy as np

D_MODEL = 1024
BATCH = 8
SEQ = 2048
DEPTH = 1
DEC_BATCH = 128
DEC_SEQ = 4
PAST_LEN = 16384
PAGE_SIZE = 128

GLA_HEADS = 4
GLA_DK = 64
GLA_DV = 128
GLA_RANK = 16
GLA_TEMP = 16.0
RET_HEADS = 4
RET_DK = 64
RET_DV = 128
ROPE_BASE = 10000.0
CHUNK = 32
N_MEM = 256
XA_HEADS = 4
XA_DH = D_MODEL // XA_HEADS
PEER_HEADS = 8
PEER_NKEYS = 128
PEER_N = PEER_NKEYS * PEER_NKEYS
PEER_DQ = 256
PEER_TOPK = 16
PEER_BLOCK = 256
EPS = 1e-6

GLA_QK = GLA_HEADS * GLA_DK
GLA_V = GLA_HEADS * GLA_DV
RET_QK = RET_HEADS * RET_DK
RET_V = RET_HEADS * RET_DV
IN_SIZES = (GLA_QK, GLA_QK, GLA_V, GLA_RANK, GLA_V, RET_QK, RET_QK, RET_V, RET_V, D_MODEL, D_MODEL)
IN_WIDTH = 3 * GLA_QK + 2 * GLA_V + GLA_RANK - GLA_QK + 2 * RET_QK + 2 * RET_V + 2 * D_MODEL

kernel_name = 'gla_retnet_peer_hybrid_step'


def rms_norm(x, g):
    xf = x.astype(jnp.float32)
    y = xf * lax.rsqrt(jnp.mean(xf * xf, axis=-1, keepdims=True) + EPS)
    return (y * g.astype(jnp.float32)).astype(x.dtype)


def rotary(x, pos):
    half = x.shape[-1] // 2
    inv = ROPE_BASE ** (-jnp.arange(half, dtype=jnp.float32) / half)
    ang = pos.astype(jnp.float32)[:, None] * inv[None, :]
    cos = jnp.cos(ang)[None, :, None, :]
    sin = jnp.sin(ang)[None, :, None, :]
    xf = x.astype(jnp.float32)
    x1, x2 = xf[..., :half], xf[..., half:]
    return jnp.concatenate([x1 * cos - x2 * sin, x1 * sin + x2 * cos], axis=-1).astype(x.dtype)


def gated_linear_chunked(q, k, v, log_a, s0):
    B, S, H, dk = q.shape
    dv = v.shape[-1]
    C = CHUNK if S % CHUNK == 0 else S
    n = S // C

    def to_chunks(t):
        return t.astype(jnp.float32).reshape(B, n, C, H, t.shape[-1]).transpose(1, 0, 2, 3, 4)

    mask = jnp.tril(jnp.ones((C, C), dtype=bool))

    def step(state, inp):
        qc, kc, vc, lc = inp
        b = jnp.cumsum(lc, axis=1)
        b_last = b[:, -1]
        qe = qc * jnp.exp(b)
        ke = kc * jnp.exp(-b)
        inter = jnp.einsum('bchk,bhkv->bchv', qe, state)
        att = jnp.where(mask, jnp.einsum('bchk,bshk->bhcs', qe, ke), 0.0)
        intra = jnp.einsum('bhcs,bshv->bchv', att, vc)
        kd = kc * jnp.exp(b_last[:, None] - b)
        new_state = jnp.exp(b_last)[..., None] * state + jnp.einsum('bshk,bshv->bhkv', kd, vc)
        return new_state, inter + intra

    s_final, o = lax.scan(step, s0.astype(jnp.float32),
                          (to_chunks(q), to_chunks(k), to_chunks(v), to_chunks(log_a)))
    o = o.transpose(1, 0, 2, 3, 4).reshape(B, S, H, dv)
    return o.astype(v.dtype), s_final.astype(s0.dtype)


def token_mixers(xn, pos, s_gla0, s_ret0, w_in, w_a2, b_a, gla_head_norm, ret_head_norm, w_pa, w_pb, w_o):
    B, S, _ = xn.shape
    proj = xn @ w_in
    splits = np.cumsum(np.array(IN_SIZES))[:-1].tolist()
    gq, gk, gv, glr, gr, rq, rk, rv, rg, za, zb = jnp.split(proj, splits, axis=-1)

    def heads(t, h):
        return t.reshape(B, S, h, t.shape[-1] // h)

    log_a = jax.nn.log_sigmoid((glr @ w_a2 + b_a).astype(jnp.float32)) / GLA_TEMP
    o_a, s_gla = gated_linear_chunked(heads(gq, GLA_HEADS) * GLA_DK ** -0.5, heads(gk, GLA_HEADS),
                                      heads(gv, GLA_HEADS), heads(log_a, GLA_HEADS), s_gla0)
    o_a = (rms_norm(o_a, gla_head_norm) * jax.nn.silu(heads(gr, GLA_HEADS))).reshape(B, S, GLA_V)

    log_gamma = jnp.log1p(-(2.0 ** (-5.0 - jnp.arange(RET_HEADS, dtype=jnp.float32))))
    log_g = jnp.broadcast_to(log_gamma[None, None, :, None], (B, S, RET_HEADS, RET_DK))
    q_r = rotary(heads(rq, RET_HEADS), pos)
    k_r = rotary(heads(rk, RET_HEADS), pos) * RET_DK ** -0.5
    o_b, s_ret = gated_linear_chunked(q_r, k_r, heads(rv, RET_HEADS), log_g, s_ret0)
    o_b = (rms_norm(o_b, ret_head_norm) * jax.nn.silu(heads(rg, RET_HEADS))).reshape(B, S, RET_V)

    merged = jax.nn.sigmoid(za) * (o_a @ w_pa) + jax.nn.sigmoid(zb) * (o_b @ w_pb)
    return merged @ w_o, s_gla, s_ret


def memory_kv(mem, norm_mem, w_xk, w_xv):
    B, M, _ = mem.shape
    mn = rms_norm(mem, norm_mem)
    return (mn @ w_xk).reshape(B, M, XA_HEADS, XA_DH), (mn @ w_xv).reshape(B, M, XA_HEADS, XA_DH)


def cross_attn(hn, mk, mv, w_xq, w_xo):
    B, S, D = hn.shape
    q = (hn @ w_xq).reshape(B, S, XA_HEADS, XA_DH)
    s = jnp.einsum('bshd,bmhd->bhsm', q.astype(jnp.float32), mk.astype(jnp.float32)) * XA_DH ** -0.5
    p = jax.nn.softmax(s, axis=-1)
    o = jnp.einsum('bhsm,bmhd->bshd', p, mv.astype(jnp.float32)).astype(hn.dtype).reshape(B, S, D)
    return o @ w_xo


def peer(xn, peer_wq, peer_subkeys, peer_u, peer_v):
    B, S, D = xn.shape
    T = B * S
    xf = xn.reshape(T, D)
    q = (xf @ peer_wq).astype(jnp.float32).reshape(T, PEER_HEADS, PEER_DQ)
    half = PEER_DQ // 2
    sk = peer_subkeys.astype(jnp.float32)
    s1 = jnp.einsum('thd,hkd->thk', q[..., :half], sk[:, 0])
    s2 = jnp.einsum('thd,hkd->thk', q[..., half:], sk[:, 1])
    v1, i1 = lax.top_k(s1, PEER_TOPK)
    v2, i2 = lax.top_k(s2, PEER_TOPK)
    n_cand = PEER_TOPK * PEER_TOPK
    cand_s = (v1[..., :, None] + v2[..., None, :]).reshape(T, PEER_HEADS, n_cand)
    cand_i = (i1[..., :, None] * PEER_NKEYS + i2[..., None, :]).reshape(T, PEER_HEADS, n_cand)
    top_s, top_p = lax.top_k(cand_s, PEER_TOPK)
    idx = jnp.take_along_axis(cand_i, top_p, axis=-1).reshape(T, PEER_HEADS * PEER_TOPK)
    gate = jax.nn.softmax(top_s, axis=-1).reshape(T, PEER_HEADS * PEER_TOPK)
    blk = PEER_BLOCK if T % PEER_BLOCK == 0 else T
    nb = T // blk

    def expert_block(args):
        xb, ib, gb = args
        ue = jnp.take(peer_u, ib, axis=0)
        a = jnp.einsum('td,tkd->tk', xb, ue).astype(jnp.float32)
        w = (jax.nn.gelu(a, approximate=False) * gb).astype(xb.dtype)
        ve = jnp.take(peer_v, ib, axis=0)
        return jnp.einsum('tk,tkd->td', w, ve)

    out = lax.map(expert_block, (xf.reshape(nb, blk, D), idx.reshape(nb, blk, -1), gate.reshape(nb, blk, -1)))
    return out.reshape(B, S, D)


def layer(x, pos, mk, mv, s_gla0, s_ret0, norm_mix, w_in, w_a2, b_a, gla_head_norm, ret_head_norm,
          w_pa, w_pb, w_o, norm_xattn, w_xq, w_xo, norm_ffn, peer_wq, peer_subkeys, peer_u, peer_v):
    mix, s_gla, s_ret = token_mixers(rms_norm(x, norm_mix), pos, s_gla0, s_ret0, w_in, w_a2, b_a,
                                     gla_head_norm, ret_head_norm, w_pa, w_pb, w_o)
    h = x + mix
    h = h + cross_attn(rms_norm(h, norm_xattn), mk, mv, w_xq, w_xo)
    h = h + peer(rms_norm(h, norm_ffn), peer_wq, peer_subkeys, peer_u, peer_v)
    return h, s_gla, s_ret


def setup_inputs(seed: int = 0) -> dict:
    key = jax.random.key(seed)
    ks = iter(jax.random.split(key, 40))
    f32 = jnp.float32

    def nrm(shape, scale):
        return scale * jax.random.normal(next(ks), shape, f32)

    def gain(shape):
        return 1.0 + 0.05 * jax.random.normal(next(ks), shape, f32)

    L, D = DEPTH, D_MODEL
    return {
        'x_prompt': nrm((BATCH, SEQ, D), 1.0),
        'x_sample': nrm((DEC_BATCH, DEC_SEQ, D), 1.0),
        'mem_prompt': nrm((BATCH, N_MEM, D), 1.0),
        'state_gla': nrm((L, DEC_BATCH, GLA_HEADS, GLA_DK, GLA_DV), 0.5),
        'state_ret': nrm((L, DEC_BATCH, RET_HEADS, RET_DK, RET_DV), 0.5),
        'cache_mem_k': nrm((L, DEC_BATCH, N_MEM, XA_HEADS, XA_DH), 1.0),
        'cache_mem_v': nrm((L, DEC_BATCH, N_MEM, XA_HEADS, XA_DH), 1.0),
        'norm_mix': gain((L, D)),
        'w_in': nrm((L, D, IN_WIDTH), D ** -0.5),
        'w_a2': nrm((L, GLA_RANK, GLA_QK), GLA_RANK ** -0.5),
        'b_a': nrm((L, GLA_QK), 0.1),
        'gla_head_norm': gain((L, GLA_HEADS, GLA_DV)),
        'ret_head_norm': gain((L, RET_HEADS, RET_DV)),
        'w_pa': nrm((L, GLA_V, D), GLA_V ** -0.5),
        'w_pb': nrm((L, RET_V, D), RET_V ** -0.5),
        'w_o': nrm((L, D, D), D ** -0.5),
        'norm_xattn': gain((L, D)),
        'norm_mem': gain((L, D)),
        'w_xq': nrm((L, D, D), D ** -0.5),
        'w_xk': nrm((L, D, D), D ** -0.5),
        'w_xv': nrm((L, D, D), D ** -0.5),
        'w_xo': nrm((L, D, D), D ** -0.5),
        'norm_ffn': gain((L, D)),
        'peer_wq': nrm((L, D, PEER_HEADS * PEER_DQ), D ** -0.5),
        'peer_subkeys': nrm((L, PEER_HEADS, 2, PEER_NKEYS, PEER_DQ // 2), (PEER_DQ // 2) ** -0.5),
        'peer_u': nrm((L, PEER_N, D), D ** -0.5),
        'peer_v': nrm((L, PEER_N, D), 0.2),
        'norm_final': gain((D,)),
    }


def reference(x_prompt, x_sample, mem_prompt, state_gla, state_ret, cache_mem_k, cache_mem_v,
              norm_mix, w_in, w_a2, b_a, gla_head_norm, ret_head_norm, w_pa, w_pb, w_o,
              norm_xattn, norm_mem, w_xq, w_xk, w_xv, w_xo,
              norm_ffn, peer_wq, peer_subkeys, peer_u, peer_v, norm_final):
    Bp, Sp, _ = x_prompt.shape
    pos_p = jnp.arange(Sp, dtype=jnp.int32)
    pos_s = PAST_LEN + jnp.arange(x_sample.shape[1], dtype=jnp.int32)
    hp, hs = x_prompt, x_sample
    gla_p, ret_p, mk_p, mv_p, gla_s, ret_s = [], [], [], [], [], []
    for l in range(DEPTH):
        lw = (norm_mix[l], w_in[l], w_a2[l], b_a[l], gla_head_norm[l], ret_head_norm[l],
              w_pa[l], w_pb[l], w_o[l], norm_xattn[l], w_xq[l], w_xo[l],
              norm_ffn[l], peer_wq[l], peer_subkeys[l], peer_u[l], peer_v[l])
        mk, mv = memory_kv(mem_prompt, norm_mem[l], w_xk[l], w_xv[l])
        z_gla = jnp.zeros((Bp, GLA_HEADS, GLA_DK, GLA_DV), x_prompt.dtype)
        z_ret = jnp.zeros((Bp, RET_HEADS, RET_DK, RET_DV), x_prompt.dtype)
        hp, sg, sr = layer(hp, pos_p, mk, mv, z_gla, z_ret, *lw)
        gla_p.append(sg)
        ret_p.append(sr)
        mk_p.append(mk)
        mv_p.append(mv)
        hs, sg, sr = layer(hs, pos_s, cache_mem_k[l], cache_mem_v[l], state_gla[l], state_ret[l], *lw)
        gla_s.append(sg)
        ret_s.append(sr)
    y_prompt = rms_norm(hp, norm_final)
    y_sample = rms_norm(hs, norm_final)
    return (y_prompt, y_sample, jnp.stack(gla_p), jnp.stack(ret_p), jnp.stack(mk_p), jnp.stack(mv_p),
            jnp.stack(gla_s), jnp.stack(ret_s))
```

```python
import numpy as np
import ml_dtypes
import concourse.bass as bass
import concourse.mybir as mybir
from concourse.bass_utils import run_bass_kernel_spmd

F32 = mybir.dt.float32
BF16 = mybir.dt.bfloat16
I32 = mybir.dt.int32
U32 = mybir.dt.uint32
AF = mybir.ActivationFunctionType
ALU = mybir.AluOpType
AX = mybir.AxisListType

D = 1024
SEQ = 2048
NB = 8
DEC_B = 128
DEC_S = 4
PAST = 16384
INW = 5136
NMEM = 256
EPS = 1e-6
NEXP = 16384
SB_PER_CORE = DEC_B // 8
PS = SB_PER_CORE * DEC_S
C_GQ, C_GK, C_GV, C_GLR, C_GR, C_RQ, C_RK, C_RV, C_RG, C_ZA, C_ZB = (
    0, 256, 512, 1024, 1040, 1552, 1808, 2064, 2576, 3088, 4112)


class Buf:
    __slots__ = ("w", "r", "name")

    def __init__(self, name=""):
        self.w = None
        self.r = {}
        self.name = name


class T:
    def __init__(self, t, buf):
        self.t = t
        self.b = buf

    def __getitem__(self, k):
        return self.t[k]


class K:
    def __init__(self, nc):
        self.nc = nc
        self.E = {"pe": nc.tensor, "act": nc.scalar, "dve": nc.vector, "pool": nc.gpsimd, "sp": nc.sync}
        self.sems = {e: nc.alloc_semaphore("s_" + e) for e in self.E}
        self.cnt = {e: 0 for e in self.E}
        self.seen = {e: {} for e in self.E}
        self.slots = {}
        for q, n in (("sp", 12), ("pool", 8), ("act", 4)):
            self.slots[q] = []
            for i in range(n):
                nm = "d_%s%d" % (q, i)
                self.sems[nm] = nc.alloc_semaphore(nm)
                self.slots[q].append([nm, 0])
        self.slot_i = {q: 0 for q in self.slots}
        self.ncnt = 0
        self.rot = {}

    ARENA_WORDS = 47104

    def sb(self, name, shape, dt):
        if not hasattr(self, "arena"):
            self.arena = self.nc.alloc_sbuf_tensor("arena", [128, self.ARENA_WORDS], F32)
            self.top = 0
        n = 1
        for x in shape[1:]:
            n *= x
        esz = 2 if dt == BF16 else 4
        words = (n * esz + 3) // 4
        words = (words + 7) // 8 * 8
        assert self.top + words <= self.ARENA_WORDS, "arena overflow at %s (%d + %d)" % (name, self.top, words)
        ap = self.arena[0:shape[0], self.top:self.top + words]
        self.top += words
        if dt != F32:
            ap = ap.bitcast(dt)
        ap = ap[:, 0:n]
        if len(shape) > 2:
            names = " ".join("d%d" % i for i in range(len(shape) - 1))
            kw = {"d%d" % i: shape[i + 1] for i in range(len(shape) - 1)}
            ap = ap.rearrange("p (%s) -> p %s" % (names, names), **kw)
        return T(ap, Buf(name))

    def mark(self):
        return self.top

    def phase_reset(self, mark):
        tags = {}
        for e in self.E:
            if self.cnt[e] > 0:
                tags[e] = self.cnt[e]
        for q in self.slots:
            for nm, v in self.slots[q]:
                if v > 0:
                    tags[nm] = v
        for e in self.E:
            self._wait(e, {s_: v for s_, v in tags.items() if s_ != e})
        self.top = mark
        self.rot = {}

    def ring(self, name, shape, dt, n=2):
        if name not in self.rot:
            self.rot[name] = [[self.sb("%s_%d" % (name, i), shape, dt) for i in range(n)], 0]
        lst = self.rot[name]
        t = lst[0][lst[1] % len(lst[0])]
        lst[1] += 1
        return t

    def _wait(self, eng, tags):
        e = self.E[eng]
        for s, v in tags.items():
            if eng == "pe" and s == "pe":
                continue
            if s == eng and eng == "sp":
                continue
            if self.seen[eng].get(s, 0) < v:
                e.wait_ge(self.sems[s], v)
                self.seen[eng][s] = v

    @staticmethod
    def _need(reads, writes):
        tags = {}

        def upd(tag):
            if tag is None:
                return
            s, v = tag
            if tags.get(s, 0) < v:
                tags[s] = v

        for b in reads:
            upd(b.w)
        for b in writes:
            upd(b.w)
            for s, v in b.r.items():
                upd((s, v))
        return tags

    @staticmethod
    def _bufs(xs):
        out = []
        for x in xs:
            if isinstance(x, T):
                out.append(x.b)
            elif isinstance(x, Buf):
                out.append(x)
            elif isinstance(x, (list, tuple)):
                out.extend(K._bufs(x))
            elif x is None:
                pass
            else:
                raise TypeError(type(x))
        return out

    def _done(self, tag, reads, writes):
        s, v = tag
        for b in writes:
            b.w = tag
            b.r = {}
        for b in reads:
            if b in writes:
                continue
            if b.r.get(s, 0) < v:
                b.r[s] = v

    def op(self, eng, fn, reads=(), writes=()):
        reads = self._bufs(reads)
        writes = self._bufs(writes)
        self._wait(eng, self._need(reads, writes))
        ins = fn(self.E[eng])
        self.cnt[eng] += 1
        ins.then_inc(self.sems[eng], 1)
        self._done((eng, self.cnt[eng]), reads, writes)
        self.ncnt += 1
        return ins

    def dma(self, q, out, in_, reads=(), writes=(), fn=None):
        reads = self._bufs(reads)
        writes = self._bufs(writes)
        need = self._need(reads, writes)
        i = self.slot_i[q]
        self.slot_i[q] = (i + 1) % len(self.slots[q])
        slot = self.slots[q][i]
        if slot[1] > 0:
            if need.get(slot[0], 0) < slot[1]:
                need[slot[0]] = slot[1]
        self._wait(q, need)
        if fn is None:
            ins = self.E[q].dma_start(out=out, in_=in_)
        else:
            ins = fn(self.E[q])
        slot[1] += 16
        ins.then_inc(self.sems[slot[0]], 16)
        self._done((slot[0], slot[1]), reads, writes)
        self.ncnt += 1
        return ins

    def finish(self, bufs):
        self._wait("sp", self._need(self._bufs(bufs), []))


def bc(ap, shape, axis):
    return ap.unsqueeze(axis).broadcast_to(list(shape))


def build(n_ptiles=16, debug=False, peer_mode=1, cut=99):
    nc = bass.Bass("TRN2", target_bir_lowering=False)
    k = K(nc)
    NT = n_ptiles + 1
    TP = n_ptiles * 128
    TT = TP + PS

    def din(name, shape, dt=F32):
        return nc.dram_tensor(name, list(shape), dt, kind="ExternalInput").ap()

    def dout(name, shape, dt=F32):
        return nc.dram_tensor(name, list(shape), dt, kind="ExternalOutput").ap()

    x_all = din("x_all", [TT, D])
    mem = din("mem", [NMEM, D])
    st_gla = din("st_gla", [SB_PER_CORE, 4, 64, 128])
    st_ret = din("st_ret", [SB_PER_CORE, 4, 64, 128])
    ck_in = din("ck", [SB_PER_CORE, NMEM, D])
    cv_in = din("cv", [SB_PER_CORE, NMEM, D])
    w_in = din("w_in", [D, INW])
    w_a2 = din("w_a2", [16, 256])
    b_a = din("b_a", [1, 256])
    gains = din("gains", [5, D])
    hgains = din("hgains", [2, 512])
    w_pa = din("w_pa", [512, D])
    w_pb = din("w_pb", [512, D])
    w_o = din("w_o", [D, D])
    w_xq = din("w_xq", [D, D])
    w_xk = din("w_xk", [D, D])
    w_xv = din("w_xv", [D, D])
    w_xo = din("w_xo", [D, D])
    peer_wq = din("peer_wq", [D, 2048])
    peer_sk = din("peer_sk", [16, 128, 128])
    peer_u = din("peer_u", [NEXP, D])
    peer_v = din("peer_v", [NEXP, D])
    c_ident = din("c_ident", [128, 128])
    c_tri = din("c_tri", [2, 128, 128])
    c_utri = din("c_utri", [2, 128, 128])
    c_seqind = din("c_seqind", [2, 128, 16])
    c_seqindT = din("c_seqindT", [64, 16 * 64], BF16)
    c_rot = din("c_rot", [TT, 6 * 128])
    c_decret = din("c_decret", [2, 64, 4])
    c_iota16 = din("c_iota16", [128, 16])
    c_identsel = din("c_identsel", [128, 128 * 128], BF16)

    y_all = dout("y_all", [TT, D])
    o_gla_p = dout("o_gla_p", [4, 64, 128])
    o_ret_p = dout("o_ret_p", [4, 64, 128])
    o_mk = dout("o_mk", [NMEM, D])
    o_mv = dout("o_mv", [NMEM, D])
    o_gla_s = dout("o_gla_s", [SB_PER_CORE, 4, 64, 128])
    o_ret_s = dout("o_ret_s", [SB_PER_CORE, 4, 64, 128])
    out_bufs = [Buf("o%d" % i) for i in range(8)]
    if debug:
        dbg_h1 = dout("dbg_h1", [TT, D])
        dbg_h2 = dout("dbg_h2", [TT, D])
        dbg_h3 = dout("dbg_h3", [TT, D])
        dbg_buf = Buf("dbg")

    dumps = {}

    def dump(name, ap, dep, ti_=0, only=0):
        if not debug or ti_ != only or name in dumps:
            return
        shp = list(ap.shape)
        dtn = dout("dd_" + name, shp, ap.dtype)
        dumps[name] = Buf("dd_" + name)
        k.dma("sp", dtn, ap, reads=[dep], writes=[dumps[name]])

    h1_d = nc.dram_tensor("h1_scratch", [TT, D], F32, kind="Internal").ap()
    h1_bufs = [Buf("h1_%d" % i) for i in range(NT)]

    pp = [nc.alloc_psum_tensor("pp%d" % i, [128, 1024], F32) for i in range(4)]
    pbuf = [Buf("bank%d" % i) for i in range(8)]
    bank_i = [0]

    def bank(i):
        return pp[i // 2][:, (i % 2) * 512:(i % 2) * 512 + 512]

    held = set()

    def nb():
        while True:
            i = bank_i[0] % 8
            bank_i[0] += 1
            if i not in held:
                return i

    def hold(i):
        held.add(i)

    def release(i):
        held.discard(i)

    def nb2():
        while True:
            if bank_i[0] % 2:
                bank_i[0] += 1
            i = bank_i[0] % 8
            bank_i[0] += 2
            if i not in held and (i + 1) not in held:
                return i

    ident_f = k.sb("ident_f", [128, 128], F32)
    ident = k.sb("ident", [128, 128], BF16)
    tri = k.sb("tri", [128, 2, 128], F32)
    utri = k.sb("utri", [128, 2, 128], F32)
    seqind = k.sb("seqind", [128, 2, 16], F32)
    decret = k.sb("decret", [64, 2, 4], F32)
    ones_row = k.sb("ones_row", [1, 128], F32)
    hgain_t = k.sb("hgain_t", [128, 2, 512], F32)
    wa2_t = k.sb("wa2_t", [16, 256], F32)
    ba_t = k.sb("ba_t", [1, 256], F32)

    k.dma("sp", ident_f[:], c_ident, writes=[ident_f])
    k.dma("sp", tri[:], c_tri.rearrange("a p t -> p a t"), writes=[tri])
    k.dma("sp", utri[:], c_utri.rearrange("a p t -> p a t"), writes=[utri])
    k.dma("sp", seqind[:], c_seqind.rearrange("a p t -> p a t"), writes=[seqind])
    k.dma("sp", decret[:], c_decret.rearrange("a p t -> p a t"), writes=[decret])
    k.dma("sp", wa2_t[:], w_a2, writes=[wa2_t])
    k.dma("sp", ba_t[:], b_a, writes=[ba_t])
    gslot = {}
    gain_box = [None]

    def load_gains(pairs):
        gain_box[0] = k.sb("gain_t", [128, len(pairs), D], F32)
        gain_t = gain_box[0]
        for gi, slot in pairs:
            gslot[gi] = slot
            k.dma("sp", gain_t[:, slot, :], gains[gi].partition_broadcast(128), writes=[gain_t])
    for i in range(2):
        k.dma("sp", hgain_t[:, i, :], hgains[i].partition_broadcast(128), writes=[hgain_t])
    k.op("dve", lambda e: e.tensor_copy(out=ident[:], in_=ident_f[:]), [ident_f], [ident])
    k.op("pool", lambda e: e.memset(ones_row[:], 1.0), [], [ones_row])
    eps_t = k.sb("eps_t", [128, 1], F32)
    k.op("pool", lambda e: e.memset(eps_t[:], EPS), [], [eps_t])
    one_t = k.sb("one_t", [128, 1], F32)
    k.op("pool", lambda e: e.memset(one_t[:], 1.0), [], [one_t])

    wl_pending = []

    def weights_ready():
        tags = {nm: v for nm, v in k.slots["pool"] if v > 0}
        for e in ("pe", "act", "dve", "pool", "sp"):
            k._wait(e, dict(tags))
        del wl_pending[:]

    def load_weight_bf(dst, row0, src, ncols, col0=0, chunk=512):
        kcs = src.shape[0] // 128
        for kc in range(kcs):
            for c0 in range(0, ncols, 2048):
                cw = min(2048, ncols - c0)
                k.dma("pool", dst[:, row0 + kc, col0 + c0:col0 + c0 + cw], src[kc * 128:(kc + 1) * 128, c0:c0 + cw],
                      writes=[Buf("wld")])
        wl_pending.append(dst)

    def rmsnorm_bf(xt, P, gi, out_bf, out_f=None):
        junk = out_bf if out_bf is not None else k.ring("rms_junk", [128, D], BF16, n=1)
        ssq = k.ring("rms_ssq", [128, 1], F32, n=2)
        k.op("act", lambda e: e.activation(out=junk[:P], in_=xt[:P], func=AF.Square, accum_out=ssq[:P]),
             [xt], [junk, ssq])
        rstd = k.ring("rms_rstd", [128, 1], F32, n=2)
        k.op("act", lambda e: e.activation(out=rstd[:P], in_=ssq[:P], func=AF.Sqrt, scale=1.0 / D, bias=eps_t[:P, 0:1]),
             [ssq, eps_t], [rstd])
        rstd2 = k.ring("rms_rstd2", [128, 1], F32, n=2)
        k.op("dve", lambda e: e.reciprocal(out=rstd2[:P], in_=rstd[:P]), [rstd], [rstd2])
        if out_f is not None:
            k.op("dve", lambda e: e.scalar_tensor_tensor(out=out_f[:P], in0=xt[:P], scalar=rstd2[:P, 0:1],
                                                         in1=gain_box[0][:P, gslot[gi], :], op0=ALU.mult, op1=ALU.mult),
                 [xt, rstd2, gain_box[0]], [out_f])
            if out_bf is not None:
                k.op("pool", lambda e: e.tensor_copy(out=out_bf[:P], in_=out_f[:P]), [out_f], [out_bf])
        else:
            k.op("dve", lambda e: e.scalar_tensor_tensor(out=out_bf[:P], in0=xt[:P], scalar=rstd2[:P, 0:1],
                                                         in1=gain_box[0][:P, gslot[gi], :], op0=ALU.mult, op1=ALU.mult),
                 [xt, rstd2, gain_box[0]], [out_bf])

    def transpose_to(dstT, src_bf, P, nchunks, evac="act", src_col0=0, dcol0=0, deps=None):
        for c0 in range(0, nchunks, 8):
            n = min(8, nchunks - c0)
            bi = nb()
            pv = bank(bi).bitcast(BF16).rearrange("p (c t) -> p c t", c=8)
            for c in range(n):
                k.op("pe", lambda e, c=c: e.transpose(out=pv[:, c, :P],
                                                      in_=src_bf[:P, src_col0 + (c0 + c) * 128:src_col0 + (c0 + c + 1) * 128],
                                                      identity=ident[:P, :P]),
                     (deps if deps is not None else [src_bf]) + [ident], [pbuf[bi]])
            if evac == "act":
                k.op("act", lambda e: e.copy(out=dstT[:, c0:c0 + n, dcol0:dcol0 + P], in_=pv[:, :n, :P]), [pbuf[bi]], [dstT])
            else:
                k.op(evac, lambda e: e.tensor_copy(out=dstT[:, c0:c0 + n, dcol0:dcol0 + P], in_=pv[:, :n, :P]), [pbuf[bi]], [dstT])

    def mm_tok(psum_bank, P, lhsT_t, w_t, col0, ncols, kcs=8, tok0=0):
        for kc in range(kcs):
            k.op("pe", lambda e, kc=kc: e.matmul(bank(psum_bank)[:P, :ncols], lhsT=lhsT_t[:, kc, tok0:tok0 + P],
                                                 rhs=w_t[:, kc, col0:col0 + ncols],
                                                 start=(kc == 0), stop=(kc == kcs - 1)),
                 [lhsT_t, w_t], [pbuf[psum_bank]])

    def mm_feat(psum_bank, slot, P, w_t, col0, rhsT_t, kcs=8, mcols=128):
        pv = bank(psum_bank).rearrange("p (c t) -> p c t", c=4)
        for kc in range(kcs):
            k.op("pe", lambda e, kc=kc: e.matmul(pv[:mcols, slot, :P], lhsT=w_t[:, kc, col0:col0 + mcols],
                                                 rhs=rhsT_t[:, kc, :P],
                                                 start=(kc == 0), stop=(kc == kcs - 1)),
                 [w_t, rhsT_t], [pbuf[psum_bank]])

    MARK = k.mark()

    UVW = 2 * D + 64
    uv_d = nc.dram_tensor("uv_scratch", [NEXP, UVW], BF16, kind="Internal").ap()
    uv_buf = Buf("uv")

    def p0_gen():
        RB = 1024
        for (tab, c0) in ((peer_u, 0), (peer_v, D)):
            for rr in range(0, NEXP, RB):
                k.dma("pool", uv_d[rr:rr + RB, c0:c0 + D], tab[rr:rr + RB, :], writes=[Buf("uvst")])
                yield

    load_gains([(0, 0)])
    seqindT = k.sb("seqindT", [64, 16, 64], BF16)
    k.dma("sp", seqindT[:].rearrange("p a b -> p (a b)"), c_seqindT, writes=[seqindT])
    w_in_bf = k.sb("w_in_bf", [128, 8, INW], BF16)
    w_pa_bf = k.sb("w_pa_bf", [128, 4, D], BF16)
    w_pb_bf = k.sb("w_pb_bf", [128, 4, D], BF16)
    w_o_bf = k.sb("w_o_bf", [128, 8, D], BF16)
    load_weight_bf(w_in_bf, 0, w_in, INW)
    load_weight_bf(w_pa_bf, 0, w_pa, D)
    load_weight_bf(w_pb_bf, 0, w_pb, D)
    load_weight_bf(w_o_bf, 0, w_o, D)
    weights_ready()

    S_f = {m: k.sb("S_f_" + m, [64, 4, 128], F32) for m in ("a", "b")}
    S_bf = {m: k.sb("S_bf_" + m, [64, 4, 128], BF16) for m in ("a", "b")}

    def mixer_core(m, P, ti, q_e, k_e, k_d, v_bf, dec, sg, hg_i, oT_dst, first):
        sample = (ti == n_ptiles)
        mi = 1 if sample else 0
        NS = SB_PER_CORE if sample else 1
        qT = k.ring("qT", [64, 4, 128], BF16, n=1)
        kT = k.ring("kT", [64, 4, 128], BF16, n=1)
        for src, dst in ((q_e, qT), (k_e, kT)):
            bi = nb()
            pv = bank(bi).bitcast(BF16).rearrange("p (c t) -> p c t", c=8)
            for h in range(4):
                k.op("pe", lambda e, h=h: e.transpose(out=pv[:64, h, :P], in_=src[:P, h * 64:(h + 1) * 64],
                                                      identity=ident[:P, :P]), [src, ident], [pbuf[bi]])
            k.op("act", lambda e: e.copy(out=dst[:, :, :P], in_=pv[:64, 0:4, :P]), [pbuf[bi]], [dst])
        bi = nb()
        av = bank(bi).rearrange("p (c t) -> p c t", c=4)
        for h in range(4):
            k.op("pe", lambda e, h=h: e.matmul(av[:P, h, :P], lhsT=kT[:, h, :P], rhs=qT[:, h, :P],
                                               start=True, stop=True), [kT, qT], [pbuf[bi]])
        attT = k.ring("attT", [128, 4, 128], BF16, n=1)
        k.op("dve", lambda e: e.tensor_tensor(out=attT[:P, :, :P], in0=av[:P, :, :P],
                                              in1=bc(tri[:P, mi, :P], [P, 4, P], 1), op=ALU.mult),
             [pbuf[bi], tri], [attT])
        if not sample:
            bo = nb()
            hold(bo)
            ov = bank(bo).rearrange("p (c t) -> p c t", c=4)
            ov_dep = pbuf[bo]
            for h in range(4):
                k.op("pe", lambda e, h=h: e.matmul(ov[:P, h, :], lhsT=attT[:P, h, :P], rhs=v_bf[:P, h * 128:(h + 1) * 128],
                                                   start=True, stop=first), [attT, v_bf], [pbuf[bo]])
                if not first:
                    k.op("pe", lambda e, h=h: e.matmul(ov[:P, h, :], lhsT=qT[:, h, :P], rhs=S_bf[m][:, h, :],
                                                       start=False, stop=True), [qT, S_bf[m]], [pbuf[bo]])
            bs = nb()
            sv = bank(bs).rearrange("p (c t) -> p c t", c=4)
            for h in range(4):
                k.op("pe", lambda e, h=h: e.matmul(sv[:64, h, :], lhsT=k_d[:P, h * 64:(h + 1) * 64],
                                                   rhs=v_bf[:P, h * 128:(h + 1) * 128], start=True, stop=True),
                     [k_d, v_bf], [pbuf[bs]])
            if first:
                k.op("dve", lambda e: e.tensor_copy(out=S_f[m][:], in_=sv[:64, :, :]), [pbuf[bs]], [S_f[m]])
            else:
                for h in range(4):
                    k.op("dve", lambda e, h=h: e.scalar_tensor_tensor(
                        out=S_f[m][:, h, :], in0=S_f[m][:, h, :], scalar=dec[:, h, 0:1], in1=sv[:64, h, :],
                        op0=ALU.mult, op1=ALU.add), [dec, pbuf[bs]], [S_f[m]])
            k.op("pool", lambda e: e.tensor_copy(out=S_bf[m][:], in_=S_f[m][:]), [S_f[m]], [S_bf[m]])
            if ti == n_ptiles - 1:
                dst = o_gla_p if m == "a" else o_ret_p
                ob = out_bufs[2] if m == "a" else out_bufs[3]
                k.dma("sp", dst.rearrange("h k v -> k h v"), S_f[m][:], reads=[S_f[m]], writes=[ob])
        else:
            GS = 2
            bos = []
            for h in range(4):
                bb = nb()
                hold(bb)
                bos.append(bb)
                k.op("pe", lambda e, h=h, bb=bb: e.matmul(bank(bb)[:P, 0:128], lhsT=attT[:P, h, :P],
                                                          rhs=v_bf[:P, h * 128:(h + 1) * 128], start=True, stop=False),
                     [attT, v_bf], [pbuf[bb]])
            src = st_gla if m == "a" else st_ret
            dst = o_gla_s if m == "a" else o_ret_s
            ob = out_bufs[6] if m == "a" else out_bufs[7]
            for g in range(SB_PER_CORE // GS):
                S0 = k.ring("S0_f", [64, GS, 4, 128], F32, n=1)
                for j in range(GS):
                    k.dma("sp", S0[:, j, :, :], src[g * GS + j].rearrange("h k v -> k h v"), writes=[S0])
                qTm = k.ring("qTm", [64, 4, GS, 64], F32, n=1)
                k.op("dve", lambda e: e.tensor_tensor(out=qTm[:], in0=bc(qT[:, :, :P], [64, 4, GS, P], 2),
                                                      in1=bc(seqindT[:, g * GS:(g + 1) * GS, :], [64, 4, GS, P], 1),
                                                      op=ALU.mult), [qT, seqindT], [qTm])
                for h in range(4):
                    for j in range(GS):
                        last = (g == SB_PER_CORE // GS - 1 and j == GS - 1)
                        k.op("pe", lambda e, h=h, j=j: e.matmul(bank(bos[h])[:P, 0:128], lhsT=qTm[:, h, j, :],
                                                                rhs=S0[:, j, h, :], start=False, stop=last),
                             [qTm, S0], [pbuf[bos[h]]])
                kdm = k.ring("kdm", [64, GS, 256], BF16, n=1)
                k.op("dve", lambda e: e.tensor_tensor(out=kdm[:], in0=bc(k_d[:P, :], [P, GS, 256], 1),
                                                      in1=bc(seqind[:P, 1, g * GS:(g + 1) * GS], [P, GS, 256], 2),
                                                      op=ALU.mult), [k_d, seqind], [kdm])
                for j in range(GS):
                    bs = nb()
                    sv = bank(bs).rearrange("p (c t) -> p c t", c=4)
                    for h in range(4):
                        k.op("pe", lambda e, h=h, j=j: e.matmul(sv[:64, h, :], lhsT=kdm[:P, j, h * 64:(h + 1) * 64],
                                                                rhs=v_bf[:P, h * 128:(h + 1) * 128], start=True, stop=True),
                             [kdm, v_bf], [pbuf[bs]])
                    for h in range(4):
                        k.op("dve", lambda e, h=h, j=j: e.scalar_tensor_tensor(
                            out=S0[:, j, h, :], in0=S0[:, j, h, :], scalar=dec[:, h, g * GS + j:g * GS + j + 1],
                            in1=sv[:64, h, :], op0=ALU.mult, op1=ALU.add), [dec, pbuf[bs]], [S0])
                for j in range(GS):
                    k.dma("sp", dst[g * GS + j].rearrange("h k v -> k h v"), S0[:, j, :, :], reads=[S0], writes=[ob])
            o_sb = k.ring("o_sb", [64, 4, 128], F32, n=1)
            for h in range(4):
                k.op("act", lambda e, h=h: e.copy(out=o_sb[:P, h, :], in_=bank(bos[h])[:P, 0:128]), [pbuf[bos[h]]], [o_sb])
                release(bos[h])
            ov = o_sb
            ov_dep = o_sb
            bo = None
        sq = k.ring("hn_sq", [128, 4, 128], F32, n=1)
        k.op("act", lambda e: e.activation(out=sq[:P], in_=ov[:P], func=AF.Square), [ov_dep], [sq])
        ssq = k.ring("hn_ssq", [128, 4], F32)
        k.op("dve", lambda e: e.tensor_reduce(out=ssq[:P], in_=sq[:P], axis=AX.X, op=ALU.add), [sq], [ssq])
        r1 = k.ring("hn_r1", [128, 4], F32)
        k.op("act", lambda e: e.activation(out=r1[:P], in_=ssq[:P], func=AF.Sqrt, scale=1.0 / 128, bias=eps_t[:P, 0:1]),
             [ssq, eps_t], [r1])
        r2 = k.ring("hn_r2", [128, 4], F32)
        k.op("dve", lambda e: e.reciprocal(out=r2[:P], in_=r1[:P]), [r1], [r2])
        on = sq
        k.op("dve", lambda e: e.tensor_tensor(out=on[:P], in0=ov[:P], in1=bc(r2[:P, :], [P, 4, 128], 2), op=ALU.mult),
             [ov_dep, r2], [on])
        if bo is not None:
            release(bo)
        gg = sg
        k.op("pool", lambda e: e.tensor_tensor(out=gg[:P], in0=sg[:P], in1=hgain_t[:P, hg_i, :], op=ALU.mult),
             [hgain_t], [gg])
        ob_bf = k.ring("hn_obf", [128, 512], BF16, n=1)
        k.op("dve", lambda e: e.tensor_tensor(out=ob_bf[:P], in0=on[:P].rearrange("p a b -> p (a b)"), in1=gg[:P],
                                              op=ALU.mult), [on, gg], [ob_bf])
        transpose_to(oT_dst, ob_bf, P, 4)

    p0 = p0_gen()
    for ti in range(NT):
        sample = (ti == n_ptiles)
        P = PS if sample else 128
        mi = 1 if sample else 0
        NS = SB_PER_CORE if sample else 1
        r0 = ti * 128
        first = (ti == 0)
        for _ in range(2):
            if p0 is not None:
                try:
                    next(p0)
                except StopIteration:
                    p0 = None
        xt = k.ring("xt", [128, D], F32, n=1)
        k.dma("sp", xt[:P], x_all[r0:r0 + P, :], writes=[xt])
        rot_t = k.ring("rot_t", [128, 6, 4, 32], F32, n=1)
        k.dma("sp", rot_t[:P].rearrange("p a h i -> p (a h i)"), c_rot[r0:r0 + P, :], writes=[rot_t])
        xn = k.ring("xn", [128, D], BF16, n=1)
        rmsnorm_bf(xt, P, 0, xn)
        xnT = k.ring("xnT", [128, 8, 128], BF16, n=1)
        transpose_to(xnT, xn, P, 8)

        qk_a = k.ring("qk_ab", [128, 512], F32, n=1)
        b1 = nb()
        mm_tok(b1, P, xnT, w_in_bf, C_GQ, 512)
        k.op("act", lambda e: e.copy(out=qk_a[:P], in_=bank(b1)[:P, :]), [pbuf[b1]], [qk_a])
        v_a = k.ring("v_ab", [128, 512], BF16, n=1)
        b2 = nb()
        mm_tok(b2, P, xnT, w_in_bf, C_GV, 512)
        k.op("act", lambda e: e.copy(out=v_a[:P], in_=bank(b2)[:P, :]), [pbuf[b2]], [v_a])
        sg_a = k.ring("sg_ab", [128, 512], F32, n=1)
        b3 = nb()
        mm_tok(b3, P, xnT, w_in_bf, C_GR, 512)
        k.op("act", lambda e: e.activation(out=sg_a[:P], in_=bank(b3)[:P, :], func=AF.Silu), [pbuf[b3]], [sg_a])
        b7 = nb()
        mm_feat(b7, 0, P, w_in_bf, C_GLR, xnT, mcols=16)
        glrT = k.ring("glrT", [16, 128], F32, n=1)
        k.op("act", lambda e: e.copy(out=glrT[:, :P], in_=bank(b7)[:16, :P]), [pbuf[b7]], [glrT])

        b8 = nb()
        k.op("pe", lambda e: e.matmul(bank(b8)[:P, :256], lhsT=glrT[:, :P], rhs=wa2_t[:, :], start=True, stop=False),
             [glrT, wa2_t], [pbuf[b8]])
        k.op("pe", lambda e: e.matmul(bank(b8)[:P, :256], lhsT=ones_row[:, :P], rhs=ba_t[:, :], start=False, stop=True),
             [ones_row, ba_t], [pbuf[b8]])
        ez = k.ring("ez", [128, 256], F32, n=1)
        k.op("act", lambda e: e.activation(out=ez[:P], in_=bank(b8)[:P, :256], func=AF.Exp, scale=-1.0), [pbuf[b8]], [ez])
        nla = ez
        k.op("act", lambda e: e.activation(out=nla[:P], in_=ez[:P], func=AF.Ln, bias=one_t[:P, 0:1]), [one_t], [nla])
        nla2 = ez
        k.op("dve", lambda e: e.tensor_scalar(out=nla2[:P], in0=nla[:P], scalar1=1.0 / 16, scalar2=None, op0=ALU.mult),
             [], [nla2])
        b9 = nb()
        k.op("pe", lambda e: e.matmul(bank(b9)[:P, 0:256], lhsT=tri[:P, mi, :P], rhs=nla2[:P, :], start=True, stop=True),
             [tri, nla2], [pbuf[b9]])
        k.op("pe", lambda e: e.matmul(bank(b9)[:P, 256:512], lhsT=utri[:P, mi, :P], rhs=nla2[:P, :], start=True, stop=True),
             [utri, nla2], [pbuf[b9]])
        b10 = nb()
        dv_ = bank(b10).rearrange("p (c t) -> p c t", c=4)
        for h in range(4):
            k.op("pe", lambda e, h=h: e.matmul(dv_[:64, h, :max(NS, 2)], lhsT=nla2[:P, h * 64:(h + 1) * 64],
                                               rhs=seqind[:P, mi, :max(NS, 2)], start=True, stop=True),
                 [nla2, seqind], [pbuf[b10]])
        dec_a = k.ring("dec_a", [64, 4, 16], F32, n=1)
        k.op("act", lambda e: e.activation(out=dec_a[:, :, :NS], in_=dv_[:64, :, :NS], func=AF.Exp, scale=-1.0),
             [pbuf[b10]], [dec_a])
        qe_a = k.ring("qe_ab", [128, 256], BF16, n=1)
        ke_a = k.ring("ke_ab", [128, 256], BF16, n=1)
        kd_a = k.ring("kd_ab", [128, 256], BF16, n=1)
        eq = k.ring("e3", [128, 256], F32, n=1)
        k.op("act", lambda e: e.activation(out=eq[:P], in_=bank(b9)[:P, 0:256], func=AF.Exp, scale=-1.0), [pbuf[b9]], [eq])
        k.op("dve", lambda e: e.scalar_tensor_tensor(out=qe_a[:P], in0=eq[:P], scalar=0.125, in1=qk_a[:P, 0:256],
                                                     op0=ALU.mult, op1=ALU.mult), [eq, qk_a], [qe_a])
        ek = k.ring("e3", [128, 256], F32, n=1)
        k.op("act", lambda e: e.activation(out=ek[:P], in_=bank(b9)[:P, 0:256], func=AF.Exp, scale=1.0), [pbuf[b9]], [ek])
        k.op("dve", lambda e: e.tensor_tensor(out=ke_a[:P], in0=ek[:P], in1=qk_a[:P, 256:512], op=ALU.mult),
             [ek, qk_a], [ke_a])
        ekd = k.ring("e3", [128, 256], F32, n=1)
        k.op("act", lambda e: e.activation(out=ekd[:P], in_=bank(b9)[:P, 256:512], func=AF.Exp, scale=-1.0), [pbuf[b9]], [ekd])
        k.op("dve", lambda e: e.tensor_tensor(out=kd_a[:P], in0=ekd[:P], in1=qk_a[:P, 256:512], op=ALU.mult),
             [ekd, qk_a], [kd_a])
        oaT = k.ring("oaT", [128, 4, 128], BF16, n=1)
        mixer_core("a", P, ti, qe_a, ke_a, kd_a, v_a, dec_a, sg_a, 0, oaT, first)

        qk_b = k.ring("qk_ab", [128, 512], F32, n=1)
        b4 = nb()
        mm_tok(b4, P, xnT, w_in_bf, C_RQ, 512)
        k.op("act", lambda e: e.copy(out=qk_b[:P], in_=bank(b4)[:P, :]), [pbuf[b4]], [qk_b])
        v_b = k.ring("v_ab", [128, 512], BF16, n=1)
        b5 = nb()
        mm_tok(b5, P, xnT, w_in_bf, C_RV, 512)
        k.op("act", lambda e: e.copy(out=v_b[:P], in_=bank(b5)[:P, :]), [pbuf[b5]], [v_b])
        sg_b = k.ring("sg_ab", [128, 512], F32, n=1)
        b6 = nb()
        mm_tok(b6, P, xnT, w_in_bf, C_RG, 512)
        k.op("act", lambda e: e.activation(out=sg_b[:P], in_=bank(b6)[:P, :], func=AF.Silu), [pbuf[b6]], [sg_b])
        qe_b = k.ring("qe_ab", [128, 256], BF16, n=1)
        ke_b = k.ring("ke_ab", [128, 256], BF16, n=1)
        kd_b = k.ring("kd_ab", [128, 256], BF16, n=1)
        for (dst, col0, ci) in ((qe_b, 0, 0), (ke_b, 256, 2), (kd_b, 256, 4)):
            xv = qk_b[:P, col0:col0 + 256].rearrange("p (h i) -> p h i", h=4)
            x1, x2 = xv[:, :, 0:32], xv[:, :, 32:64]
            dv3 = dst[:P, :].rearrange("p (h i) -> p h i", h=4)
            cc, ss = rot_t[:P, ci], rot_t[:P, ci + 1]
            t1 = k.ring("rt1", [128, 4, 32], F32, n=1)
            t2 = k.ring("rt2", [128, 4, 32], F32, n=1)
            k.op("dve", lambda e: e.tensor_tensor(out=t1[:P], in0=x1, in1=cc, op=ALU.mult), [qk_b, rot_t], [t1])
            k.op("pool", lambda e: e.tensor_tensor(out=t2[:P], in0=x2, in1=ss, op=ALU.mult), [qk_b, rot_t], [t2])
            k.op("dve", lambda e: e.tensor_tensor(out=dv3[:, :, 0:32], in0=t1[:P], in1=t2[:P], op=ALU.subtract),
                 [t1, t2], [dst])
            t3 = k.ring("rt1", [128, 4, 32], F32, n=1)
            t4 = k.ring("rt2", [128, 4, 32], F32, n=1)
            k.op("dve", lambda e: e.tensor_tensor(out=t3[:P], in0=x1, in1=ss, op=ALU.mult), [qk_b, rot_t], [t3])
            k.op("pool", lambda e: e.tensor_tensor(out=t4[:P], in0=x2, in1=cc, op=ALU.mult), [qk_b, rot_t], [t4])
            k.op("dve", lambda e: e.tensor_tensor(out=dv3[:, :, 32:64], in0=t3[:P], in1=t4[:P], op=ALU.add),
                 [t3, t4], [dst])
        dec_b = k.ring("dec_b", [64, 4, 16], F32, n=1)
        k.op("dve", lambda e: e.tensor_copy(out=dec_b[:], in_=bc(decret[:, mi, :], [64, 4, 16], 2)), [decret], [dec_b])
        obT = k.ring("obT", [128, 4, 128], BF16, n=1)
        mixer_core("b", P, ti, qe_b, ke_b, kd_b, v_b, dec_b, sg_b, 1, obT, first)

        mT = k.ring("mT", [128, 8, 128], BF16, n=1)
        for half in range(2):
            bz_a, bz_b, bp_a, bp_b = nb(), nb(), nb(), nb()
            for j in range(4):
                dt_ = half * 4 + j
                mm_feat(bz_a, j, P, w_in_bf, C_ZA + dt_ * 128, xnT)
                mm_feat(bz_b, j, P, w_in_bf, C_ZB + dt_ * 128, xnT)
                mm_feat(bp_a, j, P, w_pa_bf, dt_ * 128, oaT, kcs=4)
                mm_feat(bp_b, j, P, w_pb_bf, dt_ * 128, obT, kcs=4)

            def v4(bi):
                return bank(bi).rearrange("p (c t) -> p c t", c=4)[:, :, :P]
            sza = k.ring("sza", [128, 4, 128], F32, n=1)
            szb = k.ring("szb", [128, 4, 128], F32, n=1)
            k.op("act", lambda e: e.activation(out=sza[:, :, :P], in_=v4(bz_a), func=AF.Sigmoid), [pbuf[bz_a]], [sza])
            k.op("act", lambda e: e.activation(out=szb[:, :, :P], in_=v4(bz_b), func=AF.Sigmoid), [pbuf[bz_b]], [szb])
            ma = sza
            mb = szb
            k.op("dve", lambda e: e.tensor_tensor(out=ma[:, :, :P], in0=sza[:, :, :P], in1=v4(bp_a), op=ALU.mult),
                 [pbuf[bp_a]], [ma])
            k.op("dve", lambda e: e.tensor_tensor(out=mb[:, :, :P], in0=szb[:, :, :P], in1=v4(bp_b), op=ALU.mult),
                 [pbuf[bp_b]], [mb])
            k.op("pool", lambda e: e.tensor_tensor(out=mT[:, half * 4:half * 4 + 4, :P], in0=ma[:, :, :P],
                                                   in1=mb[:, :, :P], op=ALU.add), [ma, mb], [mT])
        h1 = xt
        for half in range(2):
            bh = nb()
            mm_tok(bh, P, mT, w_o_bf, half * 512, 512)
            k.op("dve", lambda e: e.tensor_tensor(out=h1[:P, half * 512:(half + 1) * 512], in0=bank(bh)[:P, :],
                                                  in1=xt[:P, half * 512:(half + 1) * 512], op=ALU.add),
                 [pbuf[bh]], [h1])
        k.dma("sp", h1_d[r0:r0 + P, :], h1[:P], reads=[h1], writes=[h1_bufs[ti]])
        if debug:
            k.dma("sp", dbg_h1[r0:r0 + P, :], h1[:P], reads=[h1], writes=[dbg_buf])


    if p0 is not None:
        for _ in p0:
            pass
    k.phase_reset(MARK)
    load_gains([(1, 0), (2, 1)])
    h2_d = nc.dram_tensor("h2_scratch", [TT, D], F32, kind="Internal").ap()
    h2_bufs = [Buf("h2_%d" % i) for i in range(NT)]
    w_xq_bf = k.sb("w_xq_bf", [128, 8, D], BF16)
    w_xk_bf = k.sb("w_xk_bf", [128, 8, D], BF16)
    w_xv_bf = k.sb("w_xv_bf", [128, 8, D], BF16)
    w_xo_bf = k.sb("w_xo_bf", [128, 8, D], BF16)
    load_weight_bf(w_xk_bf, 0, w_xk, D)
    load_weight_bf(w_xv_bf, 0, w_xv, D)
    load_weight_bf(w_xq_bf, 0, w_xq, D)
    load_weight_bf(w_xo_bf, 0, w_xo, D)
    weights_ready()
    mnT = k.sb("mnT", [128, 8, 256], BF16)
    mkT = k.sb("mkT", [128, 8, 256], BF16)
    mv_bf = k.sb("mv_bf", [128, 2, D], BF16)
    for mc in range(2):
        mt = k.ring("xtB", [128, D], F32, n=3)
        k.dma("sp", mt[:], mem[mc * 128:(mc + 1) * 128, :], writes=[mt])
        mn = k.ring("xn", [128, D], BF16, n=1)
        rmsnorm_bf(mt, 128, 2, mn)
        transpose_to(mnT, mn, 128, 8, dcol0=mc * 128)
    for mc in range(2):
        for (wt, dst, ob, isv) in ((w_xk_bf, o_mk, out_bufs[4], False), (w_xv_bf, o_mv, out_bufs[5], True)):
            kv_f = k.ring("kv_f", [128, D], F32, n=2)
            for half in range(2):
                bb = nb()
                mm_tok(bb, 128, mnT, wt, half * 512, 512, tok0=mc * 128)
                k.op("act", lambda e: e.copy(out=kv_f[:, half * 512:(half + 1) * 512], in_=bank(bb)[:, :]),
                     [pbuf[bb]], [kv_f])
            k.dma("sp", dst[mc * 128:(mc + 1) * 128, :], kv_f[:], reads=[kv_f], writes=[ob])
            if isv:
                k.op("dve", lambda e: e.tensor_copy(out=mv_bf[:, mc, :], in_=kv_f[:]), [kv_f], [mv_bf])
    for j in range(8):
        if j % 2 == 0:
            bb = nb()
        for kc in range(8):
            k.op("pe", lambda e, kc=kc: e.matmul(bank(bb)[:, (j % 2) * 256:(j % 2) * 256 + 256],
                                                 lhsT=w_xk_bf[:, kc, j * 128:(j + 1) * 128], rhs=mnT[:, kc, :],
                                                 start=(kc == 0), stop=(kc == 7)), [w_xk_bf, mnT], [pbuf[bb]])
        if j % 2 == 1:
            k.op("act", lambda e: e.copy(out=mkT[:, j - 1:j + 1, :],
                                         in_=bank(bb).rearrange("p (a m) -> p a m", a=2)), [pbuf[bb]], [mkT])

    def softmax_rows(scv, deps, P, nring=1):
        mx = k.ring("sm_mx", [128, 4], F32)
        k.op("dve", lambda e: e.tensor_reduce(out=mx[:P], in_=scv, axis=AX.X, op=ALU.max), deps, [mx])
        sh = k.ring("sm_sh", [128, 4, 256], F32, n=1)
        k.op("dve", lambda e: e.tensor_tensor(out=sh[:P], in0=scv, in1=bc(mx[:P, :], [P, 4, 256], 2), op=ALU.subtract),
             deps + [mx], [sh])
        k.op("act", lambda e: e.activation(out=sh[:P], in_=sh[:P], func=AF.Exp), [], [sh])
        sm = k.ring("sm_sm", [128, 4], F32)
        k.op("dve", lambda e: e.tensor_reduce(out=sm[:P], in_=sh[:P], axis=AX.X, op=ALU.add), [sh], [sm])
        rs = k.ring("sm_rs", [128, 4], F32)
        k.op("dve", lambda e: e.reciprocal(out=rs[:P], in_=sm[:P]), [sm], [rs])
        pb = k.ring("sm_pb%d" % nring, [128, 4, 256], BF16, n=nring)
        k.op("dve", lambda e: e.tensor_tensor(out=pb[:P], in0=sh[:P], in1=bc(rs[:P, :], [P, 4, 256], 2), op=ALU.mult),
             [sh, rs], [pb])
        return pb

    def n2_gen():
        n2_all = k.sb("n2_all", [128, NEXP // 128], F32)
        for c in range(NEXP // 128):
            ub = k.ring("n2_u", [128, D], BF16, n=3)
            k.dma("pool", ub[:], uv_d[c * 128:(c + 1) * 128, 0:D], writes=[ub])
            nj = k.ring("n2_j", [128, D], BF16, n=1)
            k.op("act", lambda e: e.activation(out=nj[:], in_=ub[:], func=AF.Square, accum_out=n2_all[:, c:c + 1]),
                 [ub], [nj, n2_all])
            yield
        n2_dst = uv_d[:, 2 * D:2 * D + 2].bitcast(F32).rearrange("(c p) o -> p (c o)", p=128)
        k.dma("sp", n2_dst, n2_all[:], reads=[n2_all], writes=[uv_buf],
              fn=lambda e: e.dma_start(out=n2_dst, in_=n2_all[:], allow_slow_non_contiguous=True))
        yield

    p0box = [n2_gen()]
    p0_steps = -(-(NEXP // 128 + 1) // n_ptiles)

    def b_front(ti):
        sample = (ti == n_ptiles)
        P = PS if sample else 128
        r0 = ti * 128
        for _ in range(p0_steps):
            if p0box[0] is not None:
                try:
                    next(p0box[0])
                except StopIteration:
                    p0box[0] = None
        ht = k.ring("xtS", [64, D], F32, n=1) if sample else k.ring("xtB", [128, D], F32, n=3)
        k.dma("sp", ht[:P], h1_d[r0:r0 + P, :], reads=[h1_bufs[ti]], writes=[ht])
        hn = k.ring("xn", [128, D], BF16, n=1)
        rmsnorm_bf(ht, P, 1, hn)
        hnT = k.ring("xnT", [128, 8, 128], BF16, n=1)
        transpose_to(hnT, hn, P, 8)
        qT = k.ring("xqTS", [128, 8, 64], BF16, n=1) if sample else k.ring("xqT", [128, 8, 128], BF16, n=3)
        for half in range(2):
            bb = nb()
            for j in range(4):
                mm_feat(bb, j, P, w_xq_bf, (half * 4 + j) * 128, hnT)
            k.op("act", lambda e: e.mul(out=qT[:, half * 4:half * 4 + 4, :P],
                                        in_=bank(bb).rearrange("p (c t) -> p c t", c=4)[:, :, :P], mul=1.0 / 16),
                 [pbuf[bb]], [qT])
        return dict(ht=ht, qT=qT, P=P, r0=r0, sample=sample)

    def b_mid(ti, st):
        ht, qT, P, r0, sample = st["ht"], st["qT"], st["P"], st["r0"], st["sample"]
        if not sample:
            b0 = nb2()
            hold(b0), hold(b0 + 1)
            scv = pp[b0 // 2][:P, :].rearrange("p (h m) -> p h m", h=4)
            for h in range(4):
                for c in range(2):
                    k.op("pe", lambda e, h=h, c=c: e.matmul(scv[:, h, :], lhsT=qT[:, h * 2 + c, :P], rhs=mkT[:, h * 2 + c, :],
                                                            start=(c == 0), stop=(c == 1)), [qT, mkT], [pbuf[b0 + h // 2]])
            st["pb"] = softmax_rows(scv, [pbuf[b0], pbuf[b0 + 1]], P, nring=2)
            release(b0), release(b0 + 1)

    def sample_gen(st):
        ht, qT, P, r0, sample = st["ht"], st["qT"], st["P"], st["r0"], st["sample"]
        oT = st["oT"]
        for b in range(SB_PER_CORE):
            Kb_bf = k.ring("Kb_bf", [128, 2, D], BF16, n=2)
            k.dma("pool", Kb_bf[:], ck_in[b].rearrange("(mc p) d -> p mc d", p=128), writes=[Kb_bf])
            KbT = k.ring("KbT", [128, 8, 256], BF16, n=1)
            for mc in range(2):
                transpose_to(KbT, Kb_bf[:, mc, :], 128, 8, dcol0=mc * 128, deps=[Kb_bf])
            Vb_bf = k.ring("Vb_bf", [128, 2, D], BF16, n=2)
            k.dma("pool", Vb_bf[:], cv_in[b].rearrange("(mc p) d -> p mc d", p=128), writes=[Vb_bf])
            b0 = nb2()
            hold(b0), hold(b0 + 1)
            scv = pp[b0 // 2][:DEC_S, :].rearrange("p (h m) -> p h m", h=4)
            for h in range(4):
                for c in range(2):
                    k.op("pe", lambda e, h=h, c=c: e.matmul(scv[:, h, :], lhsT=qT[:, h * 2 + c, b * DEC_S:(b + 1) * DEC_S],
                                                            rhs=KbT[:, h * 2 + c, :], start=(c == 0), stop=(c == 1)),
                         [qT, KbT], [pbuf[b0 + h // 2]])
            pb = softmax_rows(scv, [pbuf[b0], pbuf[b0 + 1]], DEC_S)
            release(b0), release(b0 + 1)
            pTb = k.ring("pTb", [128, 8, DEC_S], BF16, n=1)
            bt = nb()
            ptv = bank(bt).bitcast(BF16).rearrange("p (c t) -> p c t", c=8)
            for j in range(8):
                k.op("pe", lambda e, j=j: e.transpose(out=ptv[:, j, :DEC_S], in_=pb[:DEC_S, j // 2, (j % 2) * 128:(j % 2) * 128 + 128],
                                                      identity=ident[:DEC_S, :DEC_S]), [pb, ident], [pbuf[bt]])
            k.op("act", lambda e: e.copy(out=pTb[:], in_=ptv[:, :, :DEC_S]), [pbuf[bt]], [pTb])
            for half in range(2):
                bb = nb()
                pv = bank(bb).rearrange("p (c t) -> p c t", c=4)
                for jj in range(4):
                    j = half * 4 + jj
                    h, c = j // 2, j % 2
                    for mc in range(2):
                        k.op("pe", lambda e, mc=mc: e.matmul(pv[:, jj, :DEC_S],
                                                             lhsT=Vb_bf[:, mc, h * 256 + c * 128:h * 256 + c * 128 + 128],
                                                             rhs=pTb[:, h * 2 + mc, :], start=(mc == 0), stop=(mc == 1)),
                             [Vb_bf, pTb], [pbuf[bb]])
                k.op("act", lambda e: e.copy(out=oT[:, half * 4:half * 4 + 4, b * DEC_S:(b + 1) * DEC_S],
                                             in_=pv[:, :, :DEC_S]), [pbuf[bb]], [oT])
            yield

    def b_back(ti, st):
        ht, qT, P, r0, sample = st["ht"], st["qT"], st["P"], st["r0"], st["sample"]
        oT = st["oT"] if sample else k.ring("xoT", [128, 8, 128], BF16, n=1)
        if not sample:
            pb = st["pb"]
            pT = k.ring("xpT", [128, 8, 128], BF16, n=1)
            transpose_to(pT, pb[:P].rearrange("p h m -> p (h m)"), P, 8, deps=[pb])
            for half in range(2):
                bb = nb()
                pv = bank(bb).rearrange("p (c t) -> p c t", c=4)
                for jj in range(4):
                    j = half * 4 + jj
                    h, c = j // 2, j % 2
                    for mc in range(2):
                        k.op("pe", lambda e, mc=mc: e.matmul(pv[:, jj, :P], lhsT=mv_bf[:, mc, h * 256 + c * 128:h * 256 + c * 128 + 128],
                                                             rhs=pT[:, h * 2 + mc, :P], start=(mc == 0), stop=(mc == 1)),
                             [mv_bf, pT], [pbuf[bb]])
                k.op("act", lambda e: e.copy(out=oT[:, half * 4:half * 4 + 4, :P], in_=pv[:, :, :P]), [pbuf[bb]], [oT])
        for half in range(2):
            bh = nb()
            mm_tok(bh, P, oT, w_xo_bf, half * 512, 512)
            k.op("dve", lambda e: e.tensor_tensor(out=ht[:P, half * 512:(half + 1) * 512], in0=bank(bh)[:P, :],
                                                  in1=ht[:P, half * 512:(half + 1) * 512], op=ALU.add),
                 [pbuf[bh]], [ht])
        k.dma("sp", h2_d[r0:r0 + P, :], ht[:P], reads=[ht], writes=[h2_bufs[ti]])
        if debug:
            k.dma("sp", dbg_h2[r0:r0 + P, :], ht[:P], reads=[ht], writes=[dbg_buf])


    st_s = b_front(n_ptiles)
    st_s["oT"] = k.sb("xoTS", [128, 8, 64], BF16)
    sgen = sample_gen(st_s)
    NTP = n_ptiles
    sts = {0: b_front(0)}
    if NTP > 1:
        sts[1] = b_front(1)
    b_mid(0, sts[0])
    for ti in range(NTP):
        if ti + 2 < NTP:
            sts[ti + 2] = b_front(ti + 2)
        if ti + 1 < NTP:
            b_mid(ti + 1, sts[ti + 1])
        b_back(ti, sts.pop(ti))
        for _ in range(-(-SB_PER_CORE // NTP)):
            if sgen is not None:
                try:
                    next(sgen)
                except StopIteration:
                    sgen = None
    if sgen is not None:
        for _ in sgen:
            pass
    b_back(n_ptiles, st_s)
    p0 = p0box[0]
    if p0 is not None:
        for _ in p0:
            pass
    k.phase_reset(MARK)
    load_gains([(3, 0), (4, 1)])
    wq_bf = k.sb("wq_bf", [128, 8, 2048], BF16)
    load_weight_bf(wq_bf, 0, peer_wq, 2048)
    weights_ready()
    skT = k.sb("skT", [128, 16, 128], BF16)
    iota16 = k.sb("iota16", [128, 16], F32)
    k.dma("sp", iota16[:], c_iota16, writes=[iota16])
    identsel = k.sb("identsel", [128, 128, 128], BF16)
    k.dma("sp", identsel[:].rearrange("p a b -> p (a b)"), c_identsel, writes=[identsel])
    MARK_C = k.mark()
    sk_f = k.sb("sk_f", [128, 16, 128], F32)
    k.dma("sp", sk_f[:], peer_sk.rearrange("a k d -> k a d"), writes=[sk_f])
    sk_bf = k.sb("sk_bf", [128, 16 * 128], BF16)
    k.op("dve", lambda e: e.tensor_copy(out=sk_bf[:], in_=sk_f[:].rearrange("p a d -> p (a d)")), [sk_f], [sk_bf])
    transpose_to(skT, sk_bf, 128, 16)
    k.phase_reset(MARK_C)

    ones_f = k.sb("ones_f", [128, 128], F32)
    k.op("dve", lambda e: e.memset(ones_f[:], 1.0), [], [ones_f])

    def routing(ti, res):
        sample = (ti == n_ptiles)
        P = PS if sample else 128
        r0 = ti * 128
        ht = k.ring("xt", [128, D], F32, n=2)
        k.dma("sp", ht[:P], h2_d[r0:r0 + P, :], reads=[h2_bufs[ti]], writes=[ht])
        fn_bf = k.ring("xn", [128, D], BF16, n=2)
        rmsnorm_bf(ht, P, 3, fn_bf)
        yield
        fnT = k.ring("xnT", [128, 8, 128], BF16, n=1)
        transpose_to(fnT, fn_bf, P, 8)
        yield
        xj = k.ring("n2_j", [128, D], BF16, n=1)
        xsq = k.ring("xsq", [128, 1], F32, n=1)
        k.op("act", lambda e: e.activation(out=xj[:P], in_=fn_bf[:P], func=AF.Square, accum_out=xsq[:P]), [fn_bf], [xj, xsq])
        dg = k.ring("xsq_dg", [128, 128], F32, n=1)
        k.op("dve", lambda e: e.tensor_scalar(out=dg[:P, :P], in0=ident_f[:P, :P], scalar1=xsq[:P, 0:1], scalar2=None,
                                              op0=ALU.mult), [ident_f, xsq], [dg])
        bx = nb()
        k.op("pe", lambda e: e.matmul(bank(bx)[:, 0:P], lhsT=ones_f[:P, :], rhs=dg[:P, :P], start=True, stop=True),
             [ones_f, dg], [pbuf[bx]])
        xsqB = k.ring("xsqB", [128, 128], F32, n=2)
        k.op("act", lambda e: e.copy(out=xsqB[:, :P], in_=bank(bx)[:, 0:P]), [pbuf[bx]], [xsqB])
        yield
        qpT = k.ring("qpT", [128, 16, 128], BF16, n=1)
        for g in range(4):
            bb = nb()
            for j in range(4):
                mm_feat(bb, j, P, wq_bf, (g * 4 + j) * 128, fnT)
            k.op("act", lambda e: e.copy(out=qpT[:, g * 4:g * 4 + 4, :P],
                                         in_=bank(bb).rearrange("p (c t) -> p c t", c=4)[:, :, :P]), [pbuf[bb]], [qpT])
            yield
        s_sb = k.ring("s_sb", [128, 16, 128], F32, n=1)
        for g in range(4):
            bb = nb()
            for j in range(4):
                hj = g * 4 + j
                k.op("pe", lambda e, j=j, hj=hj: e.matmul(bank(bb)[:P, j * 128:(j + 1) * 128], lhsT=qpT[:, hj, :P],
                                                          rhs=skT[:, hj, :], start=True, stop=True), [qpT, skT], [pbuf[bb]])
            k.op("act", lambda e: e.copy(out=s_sb[:P, g * 4:g * 4 + 4, :],
                                         in_=bank(bb).rearrange("p (c t) -> p c t", c=4)[:P]), [pbuf[bb]], [s_sb])
            yield
        v16 = k.ring("v16", [128, 16, 16], F32, n=1)
        i16 = k.ring("i16", [128, 16, 16], U32, n=1)
        for hj in range(16):
            for r in range(2):
                k.op("dve", lambda e: e.max(out=v16[:P, hj, r * 8:r * 8 + 8], in_=s_sb[:P, hj, :]), [s_sb], [v16])
                yield
                k.op("dve", lambda e: e.max_index(out=i16[:P, hj, r * 8:r * 8 + 8], in_max=v16[:P, hj, r * 8:r * 8 + 8],
                                                  in_values=s_sb[:P, hj, :]), [s_sb, v16], [i16])
                yield
                if r == 0:
                    k.op("dve", lambda e: e.match_replace(out=s_sb[:P, hj, :], in_to_replace=v16[:P, hj, 0:8],
                                                          in_values=s_sb[:P, hj, :], imm_value=-1e30), [v16], [s_sb])
                    yield
        i16f = k.ring("i16f", [128, 8, 2, 16], F32, n=1)
        k.op("dve", lambda e: e.tensor_copy(out=i16f[:P].rearrange("p h j k -> p (h j) k"), in_=i16[:P]), [i16], [i16f])
        v4 = v16[:P].rearrange("p (h j) k -> p h j k", j=2)
        cand = k.ring("cand", [128, 8, 16, 16], F32, n=1)
        k.op("dve", lambda e: e.tensor_tensor(out=cand[:P], in0=bc(v4[:, :, 0, :], [P, 8, 16, 16], 3),
                                              in1=bc(v4[:, :, 1, :], [P, 8, 16, 16], 2), op=ALU.add), [v16], [cand])
        yield
        tv = k.ring("tv", [128, 8, 16], F32, n=1)
        tp = k.ring("tp", [128, 8, 16], U32, n=1)
        for h in range(8):
            cf = cand[:P, h].rearrange("p a b -> p (a b)")
            for r in range(2):
                k.op("dve", lambda e: e.max(out=tv[:P, h, r * 8:r * 8 + 8], in_=cf), [cand], [tv])
                yield
                k.op("dve", lambda e: e.max_index(out=tp[:P, h, r * 8:r * 8 + 8], in_max=tv[:P, h, r * 8:r * 8 + 8],
                                                  in_values=cf), [cand, tv], [tp])
                yield
                if r == 0:
                    k.op("dve", lambda e: e.match_replace(out=cf, in_to_replace=tv[:P, h, 0:8], in_values=cf,
                                                          imm_value=-1e30), [tv], [cand])
                    yield
        ta = k.ring("ta", [128, 8, 16], U32, n=1)
        tb = k.ring("tb", [128, 8, 16], U32, n=1)
        k.op("dve", lambda e: e.tensor_single_scalar(out=ta[:P], in_=tp[:P], scalar=4, op=ALU.logical_shift_right), [tp], [ta])
        k.op("dve", lambda e: e.tensor_single_scalar(out=tb[:P], in_=tp[:P], scalar=15, op=ALU.bitwise_and), [tp], [tb])
        taf = k.ring("taf", [128, 8, 16], F32, n=1)
        tbf = k.ring("tbf", [128, 8, 16], F32, n=1)
        k.op("dve", lambda e: e.tensor_copy(out=taf[:P], in_=ta[:P]), [ta], [taf])
        k.op("dve", lambda e: e.tensor_copy(out=tbf[:P], in_=tb[:P]), [tb], [tbf])
        yield
        idxf = k.ring("idxf", [128, 8, 16], F32, n=1)
        sel = {}
        for nm, tf, jj in (("a", taf, 0), ("b", tbf, 1)):
            oh = k.ring("cand", [128, 8, 16, 16], F32, n=1)
            io = iota16[:P, :].unsqueeze(1).unsqueeze(1).broadcast_to([P, 8, 16, 16])
            k.op("dve", lambda e: e.tensor_tensor(out=oh[:P], in0=io, in1=bc(tf[:P], [P, 8, 16, 16], 3), op=ALU.is_equal),
                 [iota16, tf], [oh])
            yield
            k.op("dve", lambda e: e.tensor_tensor(out=oh[:P], in0=oh[:P], in1=bc(i16f[:P, :, jj, :], [P, 8, 16, 16], 2),
                                                  op=ALU.mult), [i16f], [oh])
            yield
            sl = k.ring("sel_" + nm, [128, 8, 16], F32, n=1)
            k.op("dve", lambda e: e.tensor_reduce(out=sl[:P], in_=oh[:P], axis=AX.X, op=ALU.add), [oh], [sl])
            sel[nm] = sl
            yield
        k.op("dve", lambda e: e.scalar_tensor_tensor(out=idxf[:P], in0=sel["a"][:P], scalar=128.0, in1=sel["b"][:P],
                                                     op0=ALU.mult, op1=ALU.add), [sel["a"], sel["b"]], [idxf])
        gsh = k.ring("gsh", [128, 8, 16], F32, n=1)
        k.op("dve", lambda e: e.tensor_tensor(out=gsh[:P], in0=tv[:P], in1=bc(tv[:P, :, 0], [P, 8, 16], 2), op=ALU.subtract),
             [tv], [gsh])
        k.op("act", lambda e: e.activation(out=gsh[:P], in_=gsh[:P], func=AF.Exp), [], [gsh])
        gsm = k.ring("gsm", [128, 8], F32, n=1)
        k.op("dve", lambda e: e.tensor_reduce(out=gsm[:P], in_=gsh[:P], axis=AX.X, op=ALU.add), [gsh], [gsm])
        grs = k.ring("grs", [128, 8], F32, n=1)
        k.op("dve", lambda e: e.reciprocal(out=grs[:P], in_=gsm[:P]), [gsm], [grs])
        gates = k.ring("gates", [128, 8, 16], F32, n=1)
        k.op("dve", lambda e: e.tensor_tensor(out=gates[:P], in0=gsh[:P], in1=bc(grs[:P, :], [P, 8, 16], 2), op=ALU.mult),
             [gsh, grs], [gates])
        yield
        if debug:
            dump("idxf", idxf[:P].rearrange("p h k -> p (h k)"), idxf, ti)
            dump("gates", gates[:P].rearrange("p h k -> p (h k)"), gates, ti)
        bt = nb()
        k.op("pe", lambda e: e.matmul(bank(bt)[:, 0:P], lhsT=idxf[:P].rearrange("p h k -> p (h k)"),
                                      rhs=ident_f[:P, :P], start=True, stop=True), [idxf, ident_f], [pbuf[bt]])
        k.op("pe", lambda e: e.matmul(bank(bt)[:, 128:128 + P], lhsT=gates[:P].rearrange("p h k -> p (h k)"),
                                      rhs=ident_f[:P, :P], start=True, stop=True), [gates, ident_f], [pbuf[bt]])
        idxT = k.ring("idxT", [128, 128], I32, n=2)
        gT = k.ring("gT", [128, 128], F32, n=2)
        idxTf = k.ring("idxTf", [128, 128], F32, n=1)
        k.op("act", lambda e: e.copy(out=idxTf[:, :P], in_=bank(bt)[:, 0:P]), [pbuf[bt]], [idxTf])
        k.op("act", lambda e: e.copy(out=gT[:, :P], in_=bank(bt)[:, 128:128 + P]), [pbuf[bt]], [gT])
        k.op("dve", lambda e: e.tensor_copy(out=idxT[:, :P], in_=idxTf[:, :P]), [idxTf], [idxT])
        res.update(ht=ht, fn_bf=fn_bf, idxT=idxT, gT=gT, P=P, r0=r0, xsqB=xsqB)
        yield

    def drain(gen):
        if gen is not None:
            for _ in gen:
                pass

    GA = 4
    results = [dict() for _ in range(NT + 1)]
    drain(routing(0, results[0]))
    for ti in range(NT):
        R = results[ti]
        ht, fn_bf, idxT, gT, P, r0, xsqB = R["ht"], R["fn_bf"], R["idxT"], R["gT"], R["P"], R["r0"], R["xsqB"]
        nxt = routing(ti + 1, results[ti + 1]) if ti + 1 < NT else None
        if debug:
            dump("idxT", idxT[:, :P], idxT, ti)
        bo0 = nb2()
        hold(bo0), hold(bo0 + 1)
        UVs, pairs, acols, gcols = {}, {}, {}, {}
        for i in range(-GA, P):
            tg = i + GA
            if 0 <= tg < P:
                UV = k.ring("UVg", [128, UVW], BF16, n=GA + 2)
                UVs[tg] = UV
                k.dma("pool", None, None, reads=[idxT, uv_buf], writes=[UV], fn=lambda e: e.indirect_dma_start(
                    out=UV[:], out_offset=None, in_=uv_d,
                    in_offset=bass.IndirectOffsetOnAxis(ap=idxT[:, tg:tg + 1], axis=0)))
            tb_ = i + 2
            if 0 <= tb_ < P:
                b0 = nb2()
                hold(b0), hold(b0 + 1)
                pairs[tb_] = b0
                UVb = UVs[tb_]
                for half in range(2):
                    k.op("pe", lambda e: e.matmul(bank(b0 + half)[:, :], lhsT=ident[:P, tb_:tb_ + 1].broadcast_to([P, 128]),
                                                  rhs=fn_bf[:P, half * 512:(half + 1) * 512], start=True, stop=False),
                         [ident, fn_bf], [pbuf[b0 + half]])
                    k.op("pe", lambda e: e.matmul(bank(b0 + half)[:, :], lhsT=ident[:, :],
                                                  rhs=UVb[:, half * 512:(half + 1) * 512], start=False, stop=True),
                         [ident, UVb], [pbuf[b0 + half]])
            t = i
            Wsel = None
            if 0 <= t < P:
                gcol = gcols.pop(t)
                wcol = k.ring("wcol", [128, 1], F32, n=4)
                k.op("dve", lambda e: e.tensor_tensor(out=wcol[:], in0=gcol[:], in1=gT[:, t:t + 1], op=ALU.mult),
                     [gcol, gT], [wcol])
                Wsel = k.ring("Wsel", [128, 128], BF16, n=4)
                k.op("dve", lambda e: e.tensor_scalar(out=Wsel[:, :P], in0=identsel[:, t, :P], scalar1=wcol[:, 0:1],
                                                      scalar2=None, op0=ALU.mult), [identsel, wcol], [Wsel])
            td = i + 1
            if 0 <= td < P:
                b0 = pairs.pop(td)
                UVd = UVs[td]
                negh = k.ring("negh", [128, 1], F32, n=4)
                k.op("dve", lambda e: e.tensor_scalar(out=negh[:], in0=UVd[:, 2 * D:2 * D + 2].bitcast(F32),
                                                      scalar1=xsqB[:, td:td + 1], scalar2=-0.5, op0=ALU.add, op1=ALU.mult),
                     [UVd, xsqB], [negh])
                junk = k.ring("pjunk", [128, D], BF16, n=2)
                acol = k.ring("acol", [128, 1], F32, n=4)
                k.op("act", lambda e: e.activation(out=junk[:], in_=pp[b0 // 2][:, :], func=AF.Square, accum_out=acol[:, 0:1]),
                     [pbuf[b0], pbuf[b0 + 1]], [junk, acol])
                release(b0), release(b0 + 1)
                gcol = k.ring("gcol", [128, 1], F32, n=4)
                k.op("act", lambda e: e.activation(out=gcol[:], in_=acol[:], func=AF.Gelu, scale=0.5, bias=negh[:, 0:1]),
                     [acol, negh], [gcol])
                gcols[td] = gcol
            if 0 <= t < P:
                UV = UVs.pop(t)
                for half in range(2):
                    k.op("pe", lambda e: e.matmul(bank(bo0 + half)[:P, :], lhsT=Wsel[:, :P],
                                                  rhs=UV[:, D + half * 512:D + (half + 1) * 512],
                                                  start=(t == 0), stop=(t == P - 1)), [Wsel, UV], [pbuf[bo0 + half]])
                for _ in range(2 if (t % 3 == 0 or P < 128) else 1):
                    if nxt is not None:
                        try:
                            next(nxt)
                        except StopIteration:
                            nxt = None
        drain(nxt)
        k.op("dve", lambda e: e.tensor_tensor(out=ht[:P], in0=pp[bo0 // 2][:P, :], in1=ht[:P], op=ALU.add),
             [pbuf[bo0], pbuf[bo0 + 1]], [ht])
        release(bo0), release(bo0 + 1)
        if debug:
            k.dma("sp", dbg_h3[r0:r0 + P, :], ht[:P], reads=[ht], writes=[dbg_buf])
        yt = k.ring("yt", [128, D], F32, n=1)
        rmsnorm_bf(ht, P, 4, None, out_f=yt)
        k.dma("sp", y_all[r0:r0 + P, :], yt[:P], reads=[yt], writes=[out_bufs[0]])

    k.finish(out_bufs + ([dbg_buf] if debug else []) + list(dumps.values()))
    return nc


def host_consts(n_ptiles=16):
    TP = n_ptiles * 128
    TT = TP + PS
    c = {}
    c["c_ident"] = np.eye(128, dtype=np.float32)
    tri = np.zeros((2, 128, 128), np.float32)
    utri = np.zeros((2, 128, 128), np.float32)
    s = np.arange(128)[:, None]
    t = np.arange(128)[None, :]
    tri[0] = (s <= t)
    utri[0] = (s > t)
    same = (s // DEC_S == t // DEC_S) & (s < PS) & (t < PS)
    tri[1] = (s <= t) & same
    utri[1] = (s > t) & same
    c["c_tri"], c["c_utri"] = tri, utri
    si = np.zeros((2, 128, 16), np.float32)
    si[0, :, 0] = 1.0
    for b in range(SB_PER_CORE):
        si[1, b * DEC_S:(b + 1) * DEC_S, b] = 1.0
    c["c_seqind"] = si
    siT = np.zeros((16, 64), np.float32)
    for b in range(SB_PER_CORE):
        siT[b, b * DEC_S:(b + 1) * DEC_S] = 1.0
    c["c_seqindT"] = np.ascontiguousarray(np.broadcast_to(siT.reshape(1, -1), (64, 16 * 64))).astype(ml_dtypes.bfloat16)
    half = 32
    inv = (10000.0 ** (-np.arange(half, dtype=np.float32) / half)).astype(np.float32)
    pos = np.concatenate([np.arange(TP), PAST + (np.arange(PS) % DEC_S)]).astype(np.float32)
    tl = np.concatenate([np.arange(TP) % 128, np.arange(PS) % DEC_S]).astype(np.float64)
    L = np.concatenate([np.full(TP, 128.0), np.full(PS, float(DEC_S))])
    ang = (pos[:, None] * inv[None, :]).astype(np.float32)
    cos = np.cos(ang).astype(np.float64)
    sin = np.sin(ang).astype(np.float64)
    lg = np.log1p(-(2.0 ** (-5.0 - np.arange(4, dtype=np.float64))))
    fq = np.exp(lg[None, :] * (tl[:, None] + 1.0))
    fk = np.exp(-lg[None, :] * (tl[:, None] + 1.0)) * 0.125
    fkd = np.exp(lg[None, :] * (L[:, None] - 1.0 - tl[:, None])) * 0.125
    rot = np.zeros((TT, 6, 4, 32), np.float64)
    for i, f in enumerate((fq, fk, fkd)):
        rot[:, 2 * i] = cos[:, None, :] * f[:, :, None]
        rot[:, 2 * i + 1] = sin[:, None, :] * f[:, :, None]
    c["c_rot"] = rot.reshape(TT, 768).astype(np.float32)
    dr = np.zeros((2, 64, 4), np.float64)
    dr[0] = np.exp(lg * 128.0)[None, :]
    dr[1] = np.exp(lg * float(DEC_S))[None, :]
    c["c_decret"] = dr.astype(np.float32)
    c["c_iota16"] = np.ascontiguousarray(np.broadcast_to(np.arange(16, dtype=np.float32)[None, :], (128, 16)))
    c["c_identsel"] = np.ascontiguousarray(np.broadcast_to(np.eye(128, dtype=np.float32).reshape(1, -1), (128, 128 * 128))).astype(ml_dtypes.bfloat16)
    return c


def make_in_maps(inp, n_ptiles=16, cores=8):
    f = lambda a: np.ascontiguousarray(np.asarray(a, dtype=np.float32))
    consts = host_consts(n_ptiles)
    TP = n_ptiles * 128
    shared = {
        "w_in": f(inp["w_in"][0]), "w_a2": f(inp["w_a2"][0]), "b_a": f(inp["b_a"][0]).reshape(1, 256),
        "gains": f(np.stack([inp["norm_mix"][0], inp["norm_xattn"][0], inp["norm_mem"][0], inp["norm_ffn"][0],
                             inp["norm_final"]])),
        "hgains": f(np.stack([inp["gla_head_norm"][0].reshape(512), inp["ret_head_norm"][0].reshape(512)])),
        "w_pa": f(inp["w_pa"][0]), "w_pb": f(inp["w_pb"][0]), "w_o": f(inp["w_o"][0]),
        "w_xq": f(inp["w_xq"][0]), "w_xk": f(inp["w_xk"][0]), "w_xv": f(inp["w_xv"][0]), "w_xo": f(inp["w_xo"][0]),
        "peer_wq": f(inp["peer_wq"][0]), "peer_sk": f(inp["peer_subkeys"][0]).reshape(16, 128, 128),
        "peer_u": f(inp["peer_u"][0]), "peer_v": f(inp["peer_v"][0]),
    }
    shared.update(consts)
    maps = []
    for c in range(cores):
        sb = slice(c * SB_PER_CORE, (c + 1) * SB_PER_CORE)
        m = dict(shared)
        m["x_all"] = f(np.concatenate([np.asarray(inp["x_prompt"][c])[:TP], np.asarray(inp["x_sample"][sb]).reshape(PS, D)], 0))
        m["mem"] = f(inp["mem_prompt"][c])
        m["st_gla"] = f(inp["state_gla"][0][sb])
        m["st_ret"] = f(inp["state_ret"][0][sb])
        m["ck"] = f(np.asarray(inp["cache_mem_k"][0][sb]).reshape(SB_PER_CORE, NMEM, D))
        m["cv"] = f(np.asarray(inp["cache_mem_v"][0][sb]).reshape(SB_PER_CORE, NMEM, D))
        maps.append(m)
    return maps


_NC_CACHE = {}


def kernel(**inp):
    if "nc" not in _NC_CACHE:
        _NC_CACHE["nc"] = build()
    nc = _NC_CACHE["nc"]
    maps = make_in_maps(inp)
    res = run_bass_kernel_spmd(nc, maps, core_ids=list(range(8)))
    R = res.results
    y_p = np.stack([r["y_all"][:SEQ] for r in R], 0)
    y_s = np.concatenate([r["y_all"][SEQ:].reshape(SB_PER_CORE, DEC_S, D) for r in R], 0)
    gla_p = np.stack([r["o_gla_p"] for r in R], 0)[None]
    ret_p = np.stack([r["o_ret_p"] for r in R], 0)[None]
    mk = np.stack([r["o_mk"].reshape(NMEM, 4, 256) for r in R], 0)[None]
    mv = np.stack([r["o_mv"].reshape(NMEM, 4, 256) for r in R], 0)[None]
    gla_s = np.concatenate([r["o_gla_s"] for r in R], 0)[None]
    ret_s = np.concatenate([r["o_ret_s"] for r in R], 0)[None]
    return tuple(np.ascontiguousarray(a, dtype=np.float32) for a in (y_p, y_s, gla_p, ret_p, mk, mv, gla_s, ret_s))
```

```python
import numpy as np
import ml_dtypes
import concourse.bass as bass
import concourse.mybir as mybir
from concourse.bass_utils import run_bass_kernel_spmd

F32 = mybir.dt.float32
BF16 = mybir.dt.bfloat16
I32 = mybir.dt.int32
U32 = mybir.dt.uint32
AF = mybir.ActivationFunctionType
ALU = mybir.AluOpType
AX = mybir.AxisListType

D = 1024
SEQ = 2048
NB = 8
DEC_B = 128
DEC_S = 4
PAST = 16384
INW = 5136
NMEM = 256
EPS = 1e-6
NEXP = 16384
SB_PER_CORE = DEC_B // 8
PS = SB_PER_CORE * DEC_S
C_GQ, C_GK, C_GV, C_GLR, C_GR, C_RQ, C_RK, C_RV, C_RG, C_ZA, C_ZB = (
    0, 256, 512, 1024, 1040, 1552, 1808, 2064, 2576, 3088, 4112)


class Buf:
    __slots__ = ("w", "r", "name")

    def __init__(self, name=""):
        self.w = None
        self.r = {}
        self.name = name


class T:
    def __init__(self, t, buf):
        self.t = t
        self.b = buf

    def __getitem__(self, k):
        return self.t[k]


class K:
    def __init__(self, nc):
        self.nc = nc
        self.E = {"pe": nc.tensor, "act": nc.scalar, "dve": nc.vector, "pool": nc.gpsimd, "sp": nc.sync}
        self.sems = {e: nc.alloc_semaphore("s_" + e) for e in self.E}
        self.cnt = {e: 0 for e in self.E}
        self.seen = {e: {} for e in self.E}
        self.slots = {}
        for q, n in (("sp", 12), ("pool", 8), ("act", 4)):
            self.slots[q] = []
            for i in range(n):
                nm = "d_%s%d" % (q, i)
                self.sems[nm] = nc.alloc_semaphore(nm)
                self.slots[q].append([nm, 0])
        self.slot_i = {q: 0 for q in self.slots}
        self.ncnt = 0
        self.rot = {}

    ARENA_WORDS = 47104

    def sb(self, name, shape, dt):
        if not hasattr(self, "arena"):
            self.arena = self.nc.alloc_sbuf_tensor("arena", [128, self.ARENA_WORDS], F32)
            self.top = 0
        n = 1
        for x in shape[1:]:
            n *= x
        esz = 2 if dt == BF16 else 4
        words = (n * esz + 3) // 4
        words = (words + 7) // 8 * 8
        assert self.top + words <= self.ARENA_WORDS, "arena overflow at %s (%d + %d)" % (name, self.top, words)
        ap = self.arena[0:shape[0], self.top:self.top + words]
        self.top += words
        if dt != F32:
            ap = ap.bitcast(dt)
        ap = ap[:, 0:n]
        if len(shape) > 2:
            names = " ".join("d%d" % i for i in range(len(shape) - 1))
            kw = {"d%d" % i: shape[i + 1] for i in range(len(shape) - 1)}
            ap = ap.rearrange("p (%s) -> p %s" % (names, names), **kw)
        return T(ap, Buf(name))

    def mark(self):
        return self.top

    def phase_reset(self, mark):
        tags = {}
        for e in self.E:
            if self.cnt[e] > 0:
                tags[e] = self.cnt[e]
        for q in self.slots:
            for nm, v in self.slots[q]:
                if v > 0:
                    tags[nm] = v
        for e in self.E:
            self._wait(e, {s_: v for s_, v in tags.items() if s_ != e})
        self.top = mark
        self.rot = {}

    def ring(self, name, shape, dt, n=2):
        if name not in self.rot:
            self.rot[name] = [[self.sb("%s_%d" % (name, i), shape, dt) for i in range(n)], 0]
        lst = self.rot[name]
        t = lst[0][lst[1] % len(lst[0])]
        lst[1] += 1
        return t

    def _wait(self, eng, tags):
        e = self.E[eng]
        for s, v in tags.items():
            if eng == "pe" and s == "pe":
                continue
            if s == eng and eng == "sp":
                continue
            if self.seen[eng].get(s, 0) < v:
                e.wait_ge(self.sems[s], v)
                self.seen[eng][s] = v

    @staticmethod
    def _need(reads, writes):
        tags = {}

        def upd(tag):
            if tag is None:
                return
            s, v = tag
            if tags.get(s, 0) < v:
                tags[s] = v

        for b in reads:
            upd(b.w)
        for b in writes:
            upd(b.w)
            for s, v in b.r.items():
                upd((s, v))
        return tags

    @staticmethod
    def _bufs(xs):
        out = []
        for x in xs:
            if isinstance(x, T):
                out.append(x.b)
            elif isinstance(x, Buf):
                out.append(x)
            elif isinstance(x, (list, tuple)):
                out.extend(K._bufs(x))
            elif x is None:
                pass
            else:
                raise TypeError(type(x))
        return out

    def _done(self, tag, reads, writes):
        s, v = tag
        for b in writes:
            b.w = tag
            b.r = {}
        for b in reads:
            if b in writes:
                continue
            if b.r.get(s, 0) < v:
                b.r[s] = v

    def op(self, eng, fn, reads=(), writes=()):
        reads = self._bufs(reads)
        writes = self._bufs(writes)
        self._wait(eng, self._need(reads, writes))
        ins = fn(self.E[eng])
        self.cnt[eng] += 1
        ins.then_inc(self.sems[eng], 1)
        self._done((eng, self.cnt[eng]), reads, writes)
        self.ncnt += 1
        return ins

    def dma(self, q, out, in_, reads=(), writes=(), fn=None):
        reads = self._bufs(reads)
        writes = self._bufs(writes)
        need = self._need(reads, writes)
        i = self.slot_i[q]
        self.slot_i[q] = (i + 1) % len(self.slots[q])
        slot = self.slots[q][i]
        if slot[1] > 0:
            if need.get(slot[0], 0) < slot[1]:
                need[slot[0]] = slot[1]
        self._wait(q, need)
        if fn is None:
            ins = self.E[q].dma_start(out=out, in_=in_)
        else:
            ins = fn(self.E[q])
        slot[1] += 16
        ins.then_inc(self.sems[slot[0]], 16)
        self._done((slot[0], slot[1]), reads, writes)
        self.ncnt += 1
        return ins

    def finish(self, bufs):
        self._wait("sp", self._need(self._bufs(bufs), []))


def bc(ap, shape, axis):
    return ap.unsqueeze(axis).broadcast_to(list(shape))


def build(n_ptiles=16, debug=False, peer_mode=1, cut=99):
    nc = bass.Bass("TRN2", target_bir_lowering=False)
    k = K(nc)
    NT = n_ptiles + 1
    TP = n_ptiles * 128
    TT = TP + PS

    def din(name, shape, dt=F32):
        return nc.dram_tensor(name, list(shape), dt, kind="ExternalInput").ap()

    def dout(name, shape, dt=F32):
        return nc.dram_tensor(name, list(shape), dt, kind="ExternalOutput").ap()

    x_all = din("x_all", [TT, D])
    mem = din("mem", [NMEM, D])
    st_gla = din("st_gla", [SB_PER_CORE, 4, 64, 128])
    st_ret = din("st_ret", [SB_PER_CORE, 4, 64, 128])
    ck_in = din("ck", [SB_PER_CORE, NMEM, D])
    cv_in = din("cv", [SB_PER_CORE, NMEM, D])
    w_in = din("w_in", [D, INW])
    w_a2 = din("w_a2", [16, 256])
    b_a = din("b_a", [1, 256])
    gains = din("gains", [5, D])
    hgains = din("hgains", [2, 512])
    w_pa = din("w_pa", [512, D])
    w_pb = din("w_pb", [512, D])
    w_o = din("w_o", [D, D])
    w_xq = din("w_xq", [D, D])
    w_xk = din("w_xk", [D, D])
    w_xv = din("w_xv", [D, D])
    w_xo = din("w_xo", [D, D])
    peer_wq = din("peer_wq", [D, 2048])
    peer_sk = din("peer_sk", [16, 128, 128])
    peer_u = din("peer_u", [NEXP, D])
    peer_v = din("peer_v", [NEXP, D])
    c_ident = din("c_ident", [128, 128])
    c_tri = din("c_tri", [2, 128, 128])
    c_utri = din("c_utri", [2, 128, 128])
    c_seqind = din("c_seqind", [2, 128, 16])
    c_seqindT = din("c_seqindT", [64, 16 * 64], BF16)
    c_rot = din("c_rot", [TT, 6 * 128])
    c_decret = din("c_decret", [2, 64, 4])
    c_iota16 = din("c_iota16", [128, 16])
    c_identsel = din("c_identsel", [128, 128 * 128], BF16)

    y_all = dout("y_all", [TT, D])
    o_gla_p = dout("o_gla_p", [4, 64, 128])
    o_ret_p = dout("o_ret_p", [4, 64, 128])
    o_mk = dout("o_mk", [NMEM, D])
    o_mv = dout("o_mv", [NMEM, D])
    o_gla_s = dout("o_gla_s", [SB_PER_CORE, 4, 64, 128])
    o_ret_s = dout("o_ret_s", [SB_PER_CORE, 4, 64, 128])
    out_bufs = [Buf("o%d" % i) for i in range(8)]
    if debug:
        dbg_h1 = dout("dbg_h1", [TT, D])
        dbg_h2 = dout("dbg_h2", [TT, D])
        dbg_h3 = dout("dbg_h3", [TT, D])
        dbg_buf = Buf("dbg")

    dumps = {}

    def dump(name, ap, dep, ti_=0, only=0):
        if not debug or ti_ != only or name in dumps:
            return
        shp = list(ap.shape)
        dtn = dout("dd_" + name, shp, ap.dtype)
        dumps[name] = Buf("dd_" + name)
        k.dma("sp", dtn, ap, reads=[dep], writes=[dumps[name]])

    h1_d = nc.dram_tensor("h1_scratch", [TT, D], F32, kind="Internal").ap()
    h1_bufs = [Buf("h1_%d" % i) for i in range(NT)]

    pp = [nc.alloc_psum_tensor("pp%d" % i, [128, 1024], F32) for i in range(4)]
    pbuf = [Buf("bank%d" % i) for i in range(8)]
    bank_i = [0]

    def bank(i):
        return pp[i // 2][:, (i % 2) * 512:(i % 2) * 512 + 512]

    held = set()

    def nb():
        while True:
            i = bank_i[0] % 8
            bank_i[0] += 1
            if i not in held:
                return i

    def hold(i):
        held.add(i)

    def release(i):
        held.discard(i)

    def nb2():
        while True:
            if bank_i[0] % 2:
                bank_i[0] += 1
            i = bank_i[0] % 8
            bank_i[0] += 2
            if i not in held and (i + 1) not in held:
                return i

    ident_f = k.sb("ident_f", [128, 128], F32)
    ident = k.sb("ident", [128, 128], BF16)
    tri = k.sb("tri", [128, 2, 128], F32)
    utri = k.sb("utri", [128, 2, 128], F32)
    seqind = k.sb("seqind", [128, 2, 16], F32)
    decret = k.sb("decret", [64, 2, 4], F32)
    ones_row = k.sb("ones_row", [1, 128], F32)
    hgain_t = k.sb("hgain_t", [128, 2, 512], F32)
    wa2_t = k.sb("wa2_t", [16, 256], F32)
    ba_t = k.sb("ba_t", [1, 256], F32)

    k.dma("sp", ident_f[:], c_ident, writes=[ident_f])
    k.dma("sp", tri[:], c_tri.rearrange("a p t -> p a t"), writes=[tri])
    k.dma("sp", utri[:], c_utri.rearrange("a p t -> p a t"), writes=[utri])
    k.dma("sp", seqind[:], c_seqind.rearrange("a p t -> p a t"), writes=[seqind])
    k.dma("sp", decret[:], c_decret.rearrange("a p t -> p a t"), writes=[decret])
    k.dma("sp", wa2_t[:], w_a2, writes=[wa2_t])
    k.dma("sp", ba_t[:], b_a, writes=[ba_t])
    gslot = {}
    gain_box = [None]

    def load_gains(pairs):
        gain_box[0] = k.sb("gain_t", [128, len(pairs), D], F32)
        gain_t = gain_box[0]
        for gi, slot in pairs:
            gslot[gi] = slot
            k.dma("sp", gain_t[:, slot, :], gains[gi].partition_broadcast(128), writes=[gain_t])
    for i in range(2):
        k.dma("sp", hgain_t[:, i, :], hgains[i].partition_broadcast(128), writes=[hgain_t])
    k.op("dve", lambda e: e.tensor_copy(out=ident[:], in_=ident_f[:]), [ident_f], [ident])
    k.op("pool", lambda e: e.memset(ones_row[:], 1.0), [], [ones_row])
    eps_t = k.sb("eps_t", [128, 1], F32)
    k.op("pool", lambda e: e.memset(eps_t[:], EPS), [], [eps_t])
    one_t = k.sb("one_t", [128, 1], F32)
    k.op("pool", lambda e: e.memset(one_t[:], 1.0), [], [one_t])

    wl_pending = []

    def weights_ready():
        tags = {nm: v for nm, v in k.slots["pool"] if v > 0}
        for e in ("pe", "act", "dve", "pool", "sp"):
            k._wait(e, dict(tags))
        del wl_pending[:]

    def load_weight_bf(dst, row0, src, ncols, col0=0, chunk=512):
        kcs = src.shape[0] // 128
        for kc in range(kcs):
            for c0 in range(0, ncols, 2048):
                cw = min(2048, ncols - c0)
                k.dma("pool", dst[:, row0 + kc, col0 + c0:col0 + c0 + cw], src[kc * 128:(kc + 1) * 128, c0:c0 + cw],
                      writes=[Buf("wld")])
        wl_pending.append(dst)

    def rmsnorm_bf(xt, P, gi, out_bf, out_f=None):
        junk = out_bf if out_bf is not None else k.ring("rms_junk", [128, D], BF16, n=1)
        ssq = k.ring("rms_ssq", [128, 1], F32, n=2)
        k.op("act", lambda e: e.activation(out=junk[:P], in_=xt[:P], func=AF.Square, accum_out=ssq[:P]),
             [xt], [junk, ssq])
        rstd = k.ring("rms_rstd", [128, 1], F32, n=2)
        k.op("act", lambda e: e.activation(out=rstd[:P], in_=ssq[:P], func=AF.Sqrt, scale=1.0 / D, bias=eps_t[:P, 0:1]),
             [ssq, eps_t], [rstd])
        rstd2 = k.ring("rms_rstd2", [128, 1], F32, n=2)
        k.op("dve", lambda e: e.reciprocal(out=rstd2[:P], in_=rstd[:P]), [rstd], [rstd2])
        if out_f is not None:
            k.op("dve", lambda e: e.scalar_tensor_tensor(out=out_f[:P], in0=xt[:P], scalar=rstd2[:P, 0:1],
                                                         in1=gain_box[0][:P, gslot[gi], :], op0=ALU.mult, op1=ALU.mult),
                 [xt, rstd2, gain_box[0]], [out_f])
            if out_bf is not None:
                k.op("pool", lambda e: e.tensor_copy(out=out_bf[:P], in_=out_f[:P]), [out_f], [out_bf])
        else:
            k.op("dve", lambda e: e.scalar_tensor_tensor(out=out_bf[:P], in0=xt[:P], scalar=rstd2[:P, 0:1],
                                                         in1=gain_box[0][:P, gslot[gi], :], op0=ALU.mult, op1=ALU.mult),
                 [xt, rstd2, gain_box[0]], [out_bf])

    def transpose_to(dstT, src_bf, P, nchunks, evac="act", src_col0=0, dcol0=0, deps=None):
        for c0 in range(0, nchunks, 8):
            n = min(8, nchunks - c0)
            bi = nb()
            pv = bank(bi).bitcast(BF16).rearrange("p (c t) -> p c t", c=8)
            for c in range(n):
                k.op("pe", lambda e, c=c: e.transpose(out=pv[:, c, :P],
                                                      in_=src_bf[:P, src_col0 + (c0 + c) * 128:src_col0 + (c0 + c + 1) * 128],
                                                      identity=ident[:P, :P]),
                     (deps if deps is not None else [src_bf]) + [ident], [pbuf[bi]])
            if evac == "act":
                k.op("act", lambda e: e.copy(out=dstT[:, c0:c0 + n, dcol0:dcol0 + P], in_=pv[:, :n, :P]), [pbuf[bi]], [dstT])
            else:
                k.op(evac, lambda e: e.tensor_copy(out=dstT[:, c0:c0 + n, dcol0:dcol0 + P], in_=pv[:, :n, :P]), [pbuf[bi]], [dstT])

    def mm_tok(psum_bank, P, lhsT_t, w_t, col0, ncols, kcs=8, tok0=0):
        for kc in range(kcs):
            k.op("pe", lambda e, kc=kc: e.matmul(bank(psum_bank)[:P, :ncols], lhsT=lhsT_t[:, kc, tok0:tok0 + P],
                                                 rhs=w_t[:, kc, col0:col0 + ncols],
                                                 start=(kc == 0), stop=(kc == kcs - 1)),
                 [lhsT_t, w_t], [pbuf[psum_bank]])

    def mm_feat(psum_bank, slot, P, w_t, col0, rhsT_t, kcs=8, mcols=128):
        pv = bank(psum_bank).rearrange("p (c t) -> p c t", c=4)
        for kc in range(kcs):
            k.op("pe", lambda e, kc=kc: e.matmul(pv[:mcols, slot, :P], lhsT=w_t[:, kc, col0:col0 + mcols],
                                                 rhs=rhsT_t[:, kc, :P],
                                                 start=(kc == 0), stop=(kc == kcs - 1)),
                 [w_t, rhsT_t], [pbuf[psum_bank]])

    MARK = k.mark()

    UVW = 2 * D + 64
    uv_d = nc.dram_tensor("uv_scratch", [NEXP, UVW], BF16, kind="Internal").ap()
    uv_buf = Buf("uv")

    def p0_gen():
        RB = 1024
        for (tab, c0) in ((peer_u, 0), (peer_v, D)):
            for rr in range(0, NEXP, RB):
                k.dma("pool", uv_d[rr:rr + RB, c0:c0 + D], tab[rr:rr + RB, :], writes=[Buf("uvst")])
                yield

    load_gains([(0, 0)])
    seqindT = k.sb("seqindT", [64, 16, 64], BF16)
    k.dma("sp", seqindT[:].rearrange("p a b -> p (a b)"), c_seqindT, writes=[seqindT])
    w_in_bf = k.sb("w_in_bf", [128, 8, INW], BF16)
    w_pa_bf = k.sb("w_pa_bf", [128, 4, D], BF16)
    w_pb_bf = k.sb("w_pb_bf", [128, 4, D], BF16)
    w_o_bf = k.sb("w_o_bf", [128, 8, D], BF16)
    load_weight_bf(w_in_bf, 0, w_in, INW)
    load_weight_bf(w_pa_bf, 0, w_pa, D)
    load_weight_bf(w_pb_bf, 0, w_pb, D)
    load_weight_bf(w_o_bf, 0, w_o, D)
    weights_ready()

    S_f = {m: k.sb("S_f_" + m, [64, 4, 128], F32) for m in ("a", "b")}
    S_bf = {m: k.sb("S_bf_" + m, [64, 4, 128], BF16) for m in ("a", "b")}

    def mixer_core(m, P, ti, q_e, k_e, k_d, v_bf, dec, sg, hg_i, oT_dst, first):
        sample = (ti == n_ptiles)
        mi = 1 if sample else 0
        NS = SB_PER_CORE if sample else 1
        qT = k.ring("qT", [64, 4, 128], BF16, n=1)
        kT = k.ring("kT", [64, 4, 128], BF16, n=1)
        for src, dst in ((q_e, qT), (k_e, kT)):
            bi = nb()
            pv = bank(bi).bitcast(BF16).rearrange("p (c t) -> p c t", c=8)
            for h in range(4):
                k.op("pe", lambda e, h=h: e.transpose(out=pv[:64, h, :P], in_=src[:P, h * 64:(h + 1) * 64],
                                                      identity=ident[:P, :P]), [src, ident], [pbuf[bi]])
            k.op("act", lambda e: e.copy(out=dst[:, :, :P], in_=pv[:64, 0:4, :P]), [pbuf[bi]], [dst])
        bi = nb()
        av = bank(bi).rearrange("p (c t) -> p c t", c=4)
        for h in range(4):
            k.op("pe", lambda e, h=h: e.matmul(av[:P, h, :P], lhsT=kT[:, h, :P], rhs=qT[:, h, :P],
                                               start=True, stop=True), [kT, qT], [pbuf[bi]])
        attT = k.ring("attT", [128, 4, 128], BF16, n=1)
        k.op("dve", lambda e: e.tensor_tensor(out=attT[:P, :, :P], in0=av[:P, :, :P],
                                              in1=bc(tri[:P, mi, :P], [P, 4, P], 1), op=ALU.mult),
             [pbuf[bi], tri], [attT])
        if not sample:
            bo = nb()
            hold(bo)
            ov = bank(bo).rearrange("p (c t) -> p c t", c=4)
            ov_dep = pbuf[bo]
            for h in range(4):
                k.op("pe", lambda e, h=h: e.matmul(ov[:P, h, :], lhsT=attT[:P, h, :P], rhs=v_bf[:P, h * 128:(h + 1) * 128],
                                                   start=True, stop=first), [attT, v_bf], [pbuf[bo]])
                if not first:
                    k.op("pe", lambda e, h=h: e.matmul(ov[:P, h, :], lhsT=qT[:, h, :P], rhs=S_bf[m][:, h, :],
                                                       start=False, stop=True), [qT, S_bf[m]], [pbuf[bo]])
            bs = nb()
            sv = bank(bs).rearrange("p (c t) -> p c t", c=4)
            for h in range(4):
                k.op("pe", lambda e, h=h: e.matmul(sv[:64, h, :], lhsT=k_d[:P, h * 64:(h + 1) * 64],
                                                   rhs=v_bf[:P, h * 128:(h + 1) * 128], start=True, stop=True),
                     [k_d, v_bf], [pbuf[bs]])
            if first:
                k.op("dve", lambda e: e.tensor_copy(out=S_f[m][:], in_=sv[:64, :, :]), [pbuf[bs]], [S_f[m]])
            else:
                for h in range(4):
                    k.op("dve", lambda e, h=h: e.scalar_tensor_tensor(
                        out=S_f[m][:, h, :], in0=S_f[m][:, h, :], scalar=dec[:, h, 0:1], in1=sv[:64, h, :],
                        op0=ALU.mult, op1=ALU.add), [dec, pbuf[bs]], [S_f[m]])
            k.op("pool", lambda e: e.tensor_copy(out=S_bf[m][:], in_=S_f[m][:]), [S_f[m]], [S_bf[m]])
            if ti == n_ptiles - 1:
                dst = o_gla_p if m == "a" else o_ret_p
                ob = out_bufs[2] if m == "a" else out_bufs[3]
                k.dma("sp", dst.rearrange("h k v -> k h v"), S_f[m][:], reads=[S_f[m]], writes=[ob])
        else:
            GS = 2
            bos = []
            for h in range(4):
                bb = nb()
                hold(bb)
                bos.append(bb)
                k.op("pe", lambda e, h=h, bb=bb: e.matmul(bank(bb)[:P, 0:128], lhsT=attT[:P, h, :P],
                                                          rhs=v_bf[:P, h * 128:(h + 1) * 128], start=True, stop=False),
                     [attT, v_bf], [pbuf[bb]])
            src = st_gla if m == "a" else st_ret
            dst = o_gla_s if m == "a" else o_ret_s
            ob = out_bufs[6] if m == "a" else out_bufs[7]
            for g in range(SB_PER_CORE // GS):
                S0 = k.ring("S0_f", [64, GS, 4, 128], F32, n=1)
                for j in range(GS):
                    k.dma("sp", S0[:, j, :, :], src[g * GS + j].rearrange("h k v -> k h v"), writes=[S0])
                qTm = k.ring("qTm", [64, 4, GS, 64], F32, n=1)
                k.op("dve", lambda e: e.tensor_tensor(out=qTm[:], in0=bc(qT[:, :, :P], [64, 4, GS, P], 2),
                                                      in1=bc(seqindT[:, g * GS:(g + 1) * GS, :], [64, 4, GS, P], 1),
                                                      op=ALU.mult), [qT, seqindT], [qTm])
                for h in range(4):
                    for j in range(GS):
                        last = (g == SB_PER_CORE // GS - 1 and j == GS - 1)
                        k.op("pe", lambda e, h=h, j=j: e.matmul(bank(bos[h])[:P, 0:128], lhsT=qTm[:, h, j, :],
                                                                rhs=S0[:, j, h, :], start=False, stop=last),
                             [qTm, S0], [pbuf[bos[h]]])
                kdm = k.ring("kdm", [64, GS, 256], BF16, n=1)
                k.op("dve", lambda e: e.tensor_tensor(out=kdm[:], in0=bc(k_d[:P, :], [P, GS, 256], 1),
                                                      in1=bc(seqind[:P, 1, g * GS:(g + 1) * GS], [P, GS, 256], 2),
                                                      op=ALU.mult), [k_d, seqind], [kdm])
                for j in range(GS):
                    bs = nb()
                    sv = bank(bs).rearrange("p (c t) -> p c t", c=4)
                    for h in range(4):
                        k.op("pe", lambda e, h=h, j=j: e.matmul(sv[:64, h, :], lhsT=kdm[:P, j, h * 64:(h + 1) * 64],
                                                                rhs=v_bf[:P, h * 128:(h + 1) * 128], start=True, stop=True),
                             [kdm, v_bf], [pbuf[bs]])
                    for h in range(4):
                        k.op("dve", lambda e, h=h, j=j: e.scalar_tensor_tensor(
                            out=S0[:, j, h, :], in0=S0[:, j, h, :], scalar=dec[:, h, g * GS + j:g * GS + j + 1],
                            in1=sv[:64, h, :], op0=ALU.mult, op1=ALU.add), [dec, pbuf[bs]], [S0])
                for j in range(GS):
                    k.dma("sp", dst[g * GS + j].rearrange("h k v -> k h v"), S0[:, j, :, :], reads=[S0], writes=[ob])
            o_sb = k.ring("o_sb", [64, 4, 128], F32, n=1)
            for h in range(4):
                k.op("act", lambda e, h=h: e.copy(out=o_sb[:P, h, :], in_=bank(bos[h])[:P, 0:128]), [pbuf[bos[h]]], [o_sb])
                release(bos[h])
            ov = o_sb
            ov_dep = o_sb
            bo = None
        sq = k.ring("hn_sq", [128, 4, 128], F32, n=1)
        k.op("act", lambda e: e.activation(out=sq[:P], in_=ov[:P], func=AF.Square), [ov_dep], [sq])
        ssq = k.ring("hn_ssq", [128, 4], F32)
        k.op("dve", lambda e: e.tensor_reduce(out=ssq[:P], in_=sq[:P], axis=AX.X, op=ALU.add), [sq], [ssq])
        r1 = k.ring("hn_r1", [128, 4], F32)
        k.op("act", lambda e: e.activation(out=r1[:P], in_=ssq[:P], func=AF.Sqrt, scale=1.0 / 128, bias=eps_t[:P, 0:1]),
             [ssq, eps_t], [r1])
        r2 = k.ring("hn_r2", [128, 4], F32)
        k.op("dve", lambda e: e.reciprocal(out=r2[:P], in_=r1[:P]), [r1], [r2])
        on = sq
        k.op("dve", lambda e: e.tensor_tensor(out=on[:P], in0=ov[:P], in1=bc(r2[:P, :], [P, 4, 128], 2), op=ALU.mult),
             [ov_dep, r2], [on])
        if bo is not None:
            release(bo)
        gg = sg
        k.op("pool", lambda e: e.tensor_tensor(out=gg[:P], in0=sg[:P], in1=hgain_t[:P, hg_i, :], op=ALU.mult),
             [hgain_t], [gg])
        ob_bf = k.ring("hn_obf", [128, 512], BF16, n=1)
        k.op("dve", lambda e: e.tensor_tensor(out=ob_bf[:P], in0=on[:P].rearrange("p a b -> p (a b)"), in1=gg[:P],
                                              op=ALU.mult), [on, gg], [ob_bf])
        transpose_to(oT_dst, ob_bf, P, 4)

    p0 = p0_gen()
    for ti in range(NT):
        sample = (ti == n_ptiles)
        P = PS if sample else 128
        mi = 1 if sample else 0
        NS = SB_PER_CORE if sample else 1
        r0 = ti * 128
        first = (ti == 0)
        for _ in range(2):
            if p0 is not None:
                try:
                    next(p0)
                except StopIteration:
                    p0 = None
        xt = k.ring("xt", [128, D], F32, n=1)
        k.dma("sp", xt[:P], x_all[r0:r0 + P, :], writes=[xt])
        rot_t = k.ring("rot_t", [128, 6, 4, 32], F32, n=1)
        k.dma("sp", rot_t[:P].rearrange("p a h i -> p (a h i)"), c_rot[r0:r0 + P, :], writes=[rot_t])
        xn = k.ring("xn", [128, D], BF16, n=1)
        rmsnorm_bf(xt, P, 0, xn)
        xnT = k.ring("xnT", [128, 8, 128], BF16, n=1)
        transpose_to(xnT, xn, P, 8)

        qk_a = k.ring("qk_ab", [128, 512], F32, n=1)
        b1 = nb()
        mm_tok(b1, P, xnT, w_in_bf, C_GQ, 512)
        k.op("act", lambda e: e.copy(out=qk_a[:P], in_=bank(b1)[:P, :]), [pbuf[b1]], [qk_a])
        v_a = k.ring("v_ab", [128, 512], BF16, n=1)
        b2 = nb()
        mm_tok(b2, P, xnT, w_in_bf, C_GV, 512)
        k.op("act", lambda e: e.copy(out=v_a[:P], in_=bank(b2)[:P, :]), [pbuf[b2]], [v_a])
        sg_a = k.ring("sg_ab", [128, 512], F32, n=1)
        b3 = nb()
        mm_tok(b3, P, xnT, w_in_bf, C_GR, 512)
        k.op("act", lambda e: e.activation(out=sg_a[:P], in_=bank(b3)[:P, :], func=AF.Silu), [pbuf[b3]], [sg_a])
        b7 = nb()
        mm_feat(b7, 0, P, w_in_bf, C_GLR, xnT, mcols=16)
        glrT = k.ring("glrT", [16, 128], F32, n=1)
        k.op("act", lambda e: e.copy(out=glrT[:, :P], in_=bank(b7)[:16, :P]), [pbuf[b7]], [glrT])

        b8 = nb()
        k.op("pe", lambda e: e.matmul(bank(b8)[:P, :256], lhsT=glrT[:, :P], rhs=wa2_t[:, :], start=True, stop=False),
             [glrT, wa2_t], [pbuf[b8]])
        k.op("pe", lambda e: e.matmul(bank(b8)[:P, :256], lhsT=ones_row[:, :P], rhs=ba_t[:, :], start=False, stop=True),
             [ones_row, ba_t], [pbuf[b8]])
        ez = k.ring("ez", [128, 256], F32, n=1)
        k.op("act", lambda e: e.activation(out=ez[:P], in_=bank(b8)[:P, :256], func=AF.Exp, scale=-1.0), [pbuf[b8]], [ez])
        nla = ez
        k.op("act", lambda e: e.activation(out=nla[:P], in_=ez[:P], func=AF.Ln, bias=one_t[:P, 0:1]), [one_t], [nla])
        nla2 = ez
        k.op("dve", lambda e: e.tensor_scalar(out=nla2[:P], in0=nla[:P], scalar1=1.0 / 16, scalar2=None, op0=ALU.mult),
             [], [nla2])
        b9 = nb()
        k.op("pe", lambda e: e.matmul(bank(b9)[:P, 0:256], lhsT=tri[:P, mi, :P], rhs=nla2[:P, :], start=True, stop=True),
             [tri, nla2], [pbuf[b9]])
        k.op("pe", lambda e: e.matmul(bank(b9)[:P, 256:512], lhsT=utri[:P, mi, :P], rhs=nla2[:P, :], start=True, stop=True),
             [utri, nla2], [pbuf[b9]])
        b10 = nb()
        dv_ = bank(b10).rearrange("p (c t) -> p c t", c=4)
        for h in range(4):
            k.op("pe", lambda e, h=h: e.matmul(dv_[:64, h, :max(NS, 2)], lhsT=nla2[:P, h * 64:(h + 1) * 64],
                                               rhs=seqind[:P, mi, :max(NS, 2)], start=True, stop=True),
                 [nla2, seqind], [pbuf[b10]])
        dec_a = k.ring("dec_a", [64, 4, 16], F32, n=1)
        k.op("act", lambda e: e.activation(out=dec_a[:, :, :NS], in_=dv_[:64, :, :NS], func=AF.Exp, scale=-1.0),
             [pbuf[b10]], [dec_a])
        qe_a = k.ring("qe_ab", [128, 256], BF16, n=1)
        ke_a = k.ring("ke_ab", [128, 256], BF16, n=1)
        kd_a = k.ring("kd_ab", [128, 256], BF16, n=1)
        eq = k.ring("e3", [128, 256], F32, n=1)
        k.op("act", lambda e: e.activation(out=eq[:P], in_=bank(b9)[:P, 0:256], func=AF.Exp, scale=-1.0), [pbuf[b9]], [eq])
        k.op("dve", lambda e: e.scalar_tensor_tensor(out=qe_a[:P], in0=eq[:P], scalar=0.125, in1=qk_a[:P, 0:256],
                                                     op0=ALU.mult, op1=ALU.mult), [eq, qk_a], [qe_a])
        ek = k.ring("e3", [128, 256], F32, n=1)
        k.op("act", lambda e: e.activation(out=ek[:P], in_=bank(b9)[:P, 0:256], func=AF.Exp, scale=1.0), [pbuf[b9]], [ek])
        k.op("dve", lambda e: e.tensor_tensor(out=ke_a[:P], in0=ek[:P], in1=qk_a[:P, 256:512], op=ALU.mult),
             [ek, qk_a], [ke_a])
        ekd = k.ring("e3", [128, 256], F32, n=1)
        k.op("act", lambda e: e.activation(out=ekd[:P], in_=bank(b9)[:P, 256:512], func=AF.Exp, scale=-1.0), [pbuf[b9]], [ekd])
        k.op("dve", lambda e: e.tensor_tensor(out=kd_a[:P], in0=ekd[:P], in1=qk_a[:P, 256:512], op=ALU.mult),
             [ekd, qk_a], [kd_a])
        oaT = k.ring("oaT", [128, 4, 128], BF16, n=1)
        mixer_core("a", P, ti, qe_a, ke_a, kd_a, v_a, dec_a, sg_a, 0, oaT, first)

        qk_b = k.ring("qk_ab", [128, 512], F32, n=1)
        b4 = nb()
        mm_tok(b4, P, xnT, w_in_bf, C_RQ, 512)
        k.op("act", lambda e: e.copy(out=qk_b[:P], in_=bank(b4)[:P, :]), [pbuf[b4]], [qk_b])
        v_b = k.ring("v_ab", [128, 512], BF16, n=1)
        b5 = nb()
        mm_tok(b5, P, xnT, w_in_bf, C_RV, 512)
        k.op("act", lambda e: e.copy(out=v_b[:P], in_=bank(b5)[:P, :]), [pbuf[b5]], [v_b])
        sg_b = k.ring("sg_ab", [128, 512], F32, n=1)
        b6 = nb()
        mm_tok(b6, P, xnT, w_in_bf, C_RG, 512)
        k.op("act", lambda e: e.activation(out=sg_b[:P], in_=bank(b6)[:P, :], func=AF.Silu), [pbuf[b6]], [sg_b])
        qe_b = k.ring("qe_ab", [128, 256], BF16, n=1)
        ke_b = k.ring("ke_ab", [128, 256], BF16, n=1)
        kd_b = k.ring("kd_ab", [128, 256], BF16, n=1)
        for (dst, col0, ci) in ((qe_b, 0, 0), (ke_b, 256, 2), (kd_b, 256, 4)):
            xv = qk_b[:P, col0:col0 + 256].rearrange("p (h i) -> p h i", h=4)
            x1, x2 = xv[:, :, 0:32], xv[:, :, 32:64]
            dv3 = dst[:P, :].rearrange("p (h i) -> p h i", h=4)
            cc, ss = rot_t[:P, ci], rot_t[:P, ci + 1]
            t1 = k.ring("rt1", [128, 4, 32], F32, n=1)
            t2 = k.ring("rt2", [128, 4, 32], F32, n=1)
            k.op("dve", lambda e: e.tensor_tensor(out=t1[:P], in0=x1, in1=cc, op=ALU.mult), [qk_b, rot_t], [t1])
            k.op("pool", lambda e: e.tensor_tensor(out=t2[:P], in0=x2, in1=ss, op=ALU.mult), [qk_b, rot_t], [t2])
            k.op("dve", lambda e: e.tensor_tensor(out=dv3[:, :, 0:32], in0=t1[:P], in1=t2[:P], op=ALU.subtract),
                 [t1, t2], [dst])
            t3 = k.ring("rt1", [128, 4, 32], F32, n=1)
            t4 = k.ring("rt2", [128, 4, 32], F32, n=1)
            k.op("dve", lambda e: e.tensor_tensor(out=t3[:P], in0=x1, in1=ss, op=ALU.mult), [qk_b, rot_t], [t3])
            k.op("pool", lambda e: e.tensor_tensor(out=t4[:P], in0=x2, in1=cc, op=ALU.mult), [qk_b, rot_t], [t4])
            k.op("dve", lambda e: e.tensor_tensor(out=dv3[:, :, 32:64], in0=t3[:P], in1=t4[:P], op=ALU.add),
                 [t3, t4], [dst])
        dec_b = k.ring("dec_b", [64, 4, 16], F32, n=1)
        k.op("dve", lambda e: e.tensor_copy(out=dec_b[:], in_=bc(decret[:, mi, :], [64, 4, 16], 2)), [decret], [dec_b])
        obT = k.ring("obT", [128, 4, 128], BF16, n=1)
        mixer_core("b", P, ti, qe_b, ke_b, kd_b, v_b, dec_b, sg_b, 1, obT, first)

        mT = k.ring("mT", [128, 8, 128], BF16, n=1)
        for half in range(2):
            bz_a, bz_b, bp_a, bp_b = nb(), nb(), nb(), nb()
            for j in range(4):
                dt_ = half * 4 + j
                mm_feat(bz_a, j, P, w_in_bf, C_ZA + dt_ * 128, xnT)
                mm_feat(bz_b, j, P, w_in_bf, C_ZB + dt_ * 128, xnT)
                mm_feat(bp_a, j, P, w_pa_bf, dt_ * 128, oaT, kcs=4)
                mm_feat(bp_b, j, P, w_pb_bf, dt_ * 128, obT, kcs=4)

            def v4(bi):
                return bank(bi).rearrange("p (c t) -> p c t", c=4)[:, :, :P]
            sza = k.ring("sza", [128, 4, 128], F32, n=1)
            szb = k.ring("szb", [128, 4, 128], F32, n=1)
            k.op("act", lambda e: e.activation(out=sza[:, :, :P], in_=v4(bz_a), func=AF.Sigmoid), [pbuf[bz_a]], [sza])
            k.op("act", lambda e: e.activation(out=szb[:, :, :P], in_=v4(bz_b), func=AF.Sigmoid), [pbuf[bz_b]], [szb])
            ma = sza
            mb = szb
            k.op("dve", lambda e: e.tensor_tensor(out=ma[:, :, :P], in0=sza[:, :, :P], in1=v4(bp_a), op=ALU.mult),
                 [pbuf[bp_a]], [ma])
            k.op("dve", lambda e: e.tensor_tensor(out=mb[:, :, :P], in0=szb[:, :, :P], in1=v4(bp_b), op=ALU.mult),
                 [pbuf[bp_b]], [mb])
            k.op("pool", lambda e: e.tensor_tensor(out=mT[:, half * 4:half * 4 + 4, :P], in0=ma[:, :, :P],
                                                   in1=mb[:, :, :P], op=ALU.add), [ma, mb], [mT])
        h1 = xt
        for half in range(2):
            bh = nb()
            mm_tok(bh, P, mT, w_o_bf, half * 512, 512)
            k.op("dve", lambda e: e.tensor_tensor(out=h1[:P, half * 512:(half + 1) * 512], in0=bank(bh)[:P, :],
                                                  in1=xt[:P, half * 512:(half + 1) * 512], op=ALU.add),
                 [pbuf[bh]], [h1])
        k.dma("sp", h1_d[r0:r0 + P, :], h1[:P], reads=[h1], writes=[h1_bufs[ti]])
        if debug:
            k.dma("sp", dbg_h1[r0:r0 + P, :], h1[:P], reads=[h1], writes=[dbg_buf])


    if p0 is not None:
        for _ in p0:
            pass
    k.phase_reset(MARK)
    load_gains([(1, 0), (2, 1)])
    h2_d = nc.dram_tensor("h2_scratch", [TT, D], F32, kind="Internal").ap()
    h2_bufs = [Buf("h2_%d" % i) for i in range(NT)]
    w_xq_bf = k.sb("w_xq_bf", [128, 8, D], BF16)
    w_xk_bf = k.sb("w_xk_bf", [128, 8, D], BF16)
    w_xv_bf = k.sb("w_xv_bf", [128, 8, D], BF16)
    w_xo_bf = k.sb("w_xo_bf", [128, 8, D], BF16)
    load_weight_bf(w_xk_bf, 0, w_xk, D)
    load_weight_bf(w_xv_bf, 0, w_xv, D)
    load_weight_bf(w_xq_bf, 0, w_xq, D)
    load_weight_bf(w_xo_bf, 0, w_xo, D)
    weights_ready()
    mnT = k.sb("mnT", [128, 8, 256], BF16)
    mkT = k.sb("mkT", [128, 8, 256], BF16)
    mv_bf = k.sb("mv_bf", [128, 2, D], BF16)
    for mc in range(2):
        mt = k.ring("xtB", [128, D], F32, n=3)
        k.dma("sp", mt[:], mem[mc * 128:(mc + 1) * 128, :], writes=[mt])
        mn = k.ring("xn", [128, D], BF16, n=1)
        rmsnorm_bf(mt, 128, 2, mn)
        transpose_to(mnT, mn, 128, 8, dcol0=mc * 128)
    for mc in range(2):
        for (wt, dst, ob, isv) in ((w_xk_bf, o_mk, out_bufs[4], False), (w_xv_bf, o_mv, out_bufs[5], True)):
            kv_f = k.ring("kv_f", [128, D], F32, n=2)
            for half in range(2):
                bb = nb()
                mm_tok(bb, 128, mnT, wt, half * 512, 512, tok0=mc * 128)
                k.op("act", lambda e: e.copy(out=kv_f[:, half * 512:(half + 1) * 512], in_=bank(bb)[:, :]),
                     [pbuf[bb]], [kv_f])
            k.dma("sp", dst[mc * 128:(mc + 1) * 128, :], kv_f[:], reads=[kv_f], writes=[ob])
            if isv:
                k.op("dve", lambda e: e.tensor_copy(out=mv_bf[:, mc, :], in_=kv_f[:]), [kv_f], [mv_bf])
    for j in range(8):
        if j % 2 == 0:
            bb = nb()
        for kc in range(8):
            k.op("pe", lambda e, kc=kc: e.matmul(bank(bb)[:, (j % 2) * 256:(j % 2) * 256 + 256],
                                                 lhsT=w_xk_bf[:, kc, j * 128:(j + 1) * 128], rhs=mnT[:, kc, :],
                                                 start=(kc == 0), stop=(kc == 7)), [w_xk_bf, mnT], [pbuf[bb]])
        if j % 2 == 1:
            k.op("act", lambda e: e.copy(out=mkT[:, j - 1:j + 1, :],
                                         in_=bank(bb).rearrange("p (a m) -> p a m", a=2)), [pbuf[bb]], [mkT])

    def softmax_rows(scv, deps, P, nring=1):
        mx = k.ring("sm_mx", [128, 4], F32)
        k.op("dve", lambda e: e.tensor_reduce(out=mx[:P], in_=scv, axis=AX.X, op=ALU.max), deps, [mx])
        sh = k.ring("sm_sh", [128, 4, 256], F32, n=1)
        k.op("dve", lambda e: e.tensor_tensor(out=sh[:P], in0=scv, in1=bc(mx[:P, :], [P, 4, 256], 2), op=ALU.subtract),
             deps + [mx], [sh])
        k.op("act", lambda e: e.activation(out=sh[:P], in_=sh[:P], func=AF.Exp), [], [sh])
        sm = k.ring("sm_sm", [128, 4], F32)
        k.op("dve", lambda e: e.tensor_reduce(out=sm[:P], in_=sh[:P], axis=AX.X, op=ALU.add), [sh], [sm])
        rs = k.ring("sm_rs", [128, 4], F32)
        k.op("dve", lambda e: e.reciprocal(out=rs[:P], in_=sm[:P]), [sm], [rs])
        pb = k.ring("sm_pb%d" % nring, [128, 4, 256], BF16, n=nring)
        k.op("dve", lambda e: e.tensor_tensor(out=pb[:P], in0=sh[:P], in1=bc(rs[:P, :], [P, 4, 256], 2), op=ALU.mult),
             [sh, rs], [pb])
        return pb

    def n2_gen():
        n2_all = k.sb("n2_all", [128, NEXP // 128], F32)
        for c in range(NEXP // 128):
            ub = k.ring("n2_u", [128, D], BF16, n=3)
            k.dma("pool", ub[:], uv_d[c * 128:(c + 1) * 128, 0:D], writes=[ub])
            if c % 2 == 0:
                nj = k.ring("n2_j", [128, D], BF16, n=1)
                k.op("act", lambda e: e.activation(out=nj[:], in_=ub[:], func=AF.Square, accum_out=n2_all[:, c:c + 1]),
                     [ub], [nj, n2_all])
            else:
                nj = k.ring("n2_j2", [128, D], BF16, n=1)
                k.op("dve", lambda e: e.scalar_tensor_tensor(out=nj[:], in0=ub[:], scalar=1.0, in1=ub[:], op0=ALU.mult,
                                                             op1=ALU.mult, accum_out=n2_all[:, c:c + 1]), [ub], [nj, n2_all])
            yield
        n2_dst = uv_d[:, 2 * D:2 * D + 2].bitcast(F32).rearrange("(c p) o -> p (c o)", p=128)
        k.dma("sp", n2_dst, n2_all[:], reads=[n2_all], writes=[uv_buf],
              fn=lambda e: e.dma_start(out=n2_dst, in_=n2_all[:], allow_slow_non_contiguous=True))
        yield

    p0box = [n2_gen()]
    p0_steps = -(-(NEXP // 128 + 1) // n_ptiles)

    def b_front(ti):
        sample = (ti == n_ptiles)
        P = PS if sample else 128
        r0 = ti * 128
        for _ in range(p0_steps):
            if p0box[0] is not None:
                try:
                    next(p0box[0])
                except StopIteration:
                    p0box[0] = None
        ht = k.ring("xtS", [64, D], F32, n=1) if sample else k.ring("xtB", [128, D], F32, n=3)
        k.dma("sp", ht[:P], h1_d[r0:r0 + P, :], reads=[h1_bufs[ti]], writes=[ht])
        hn = k.ring("xn", [128, D], BF16, n=1)
        rmsnorm_bf(ht, P, 1, hn)
        hnT = k.ring("xnT", [128, 8, 128], BF16, n=1)
        transpose_to(hnT, hn, P, 8)
        qT = k.ring("xqTS", [128, 8, 64], BF16, n=1) if sample else k.ring("xqT", [128, 8, 128], BF16, n=3)
        for half in range(2):
            bb = nb()
            for j in range(4):
                mm_feat(bb, j, P, w_xq_bf, (half * 4 + j) * 128, hnT)
            k.op("act", lambda e: e.mul(out=qT[:, half * 4:half * 4 + 4, :P],
                                        in_=bank(bb).rearrange("p (c t) -> p c t", c=4)[:, :, :P], mul=1.0 / 16),
                 [pbuf[bb]], [qT])
        return dict(ht=ht, qT=qT, P=P, r0=r0, sample=sample)

    def b_mid(ti, st):
        ht, qT, P, r0, sample = st["ht"], st["qT"], st["P"], st["r0"], st["sample"]
        if not sample:
            b0 = nb2()
            hold(b0), hold(b0 + 1)
            scv = pp[b0 // 2][:P, :].rearrange("p (h m) -> p h m", h=4)
            for h in range(4):
                for c in range(2):
                    k.op("pe", lambda e, h=h, c=c: e.matmul(scv[:, h, :], lhsT=qT[:, h * 2 + c, :P], rhs=mkT[:, h * 2 + c, :],
                                                            start=(c == 0), stop=(c == 1)), [qT, mkT], [pbuf[b0 + h // 2]])
            st["pb"] = softmax_rows(scv, [pbuf[b0], pbuf[b0 + 1]], P, nring=2)
            release(b0), release(b0 + 1)

    def sample_gen(st):
        ht, qT, P, r0, sample = st["ht"], st["qT"], st["P"], st["r0"], st["sample"]
        oT = st["oT"]
        for b in range(SB_PER_CORE):
            Kb_bf = k.ring("Kb_bf", [128, 2, D], BF16, n=2)
            k.dma("pool", Kb_bf[:], ck_in[b].rearrange("(mc p) d -> p mc d", p=128), writes=[Kb_bf])
            KbT = k.ring("KbT", [128, 8, 256], BF16, n=1)
            for mc in range(2):
                transpose_to(KbT, Kb_bf[:, mc, :], 128, 8, dcol0=mc * 128, deps=[Kb_bf])
            Vb_bf = k.ring("Vb_bf", [128, 2, D], BF16, n=2)
            k.dma("pool", Vb_bf[:], cv_in[b].rearrange("(mc p) d -> p mc d", p=128), writes=[Vb_bf])
            b0 = nb2()
            hold(b0), hold(b0 + 1)
            scv = pp[b0 // 2][:DEC_S, :].rearrange("p (h m) -> p h m", h=4)
            for h in range(4):
                for c in range(2):
                    k.op("pe", lambda e, h=h, c=c: e.matmul(scv[:, h, :], lhsT=qT[:, h * 2 + c, b * DEC_S:(b + 1) * DEC_S],
                                                            rhs=KbT[:, h * 2 + c, :], start=(c == 0), stop=(c == 1)),
                         [qT, KbT], [pbuf[b0 + h // 2]])
            pb = softmax_rows(scv, [pbuf[b0], pbuf[b0 + 1]], DEC_S)
            release(b0), release(b0 + 1)
            pTb = k.ring("pTb", [128, 8, DEC_S], BF16, n=1)
            bt = nb()
            ptv = bank(bt).bitcast(BF16).rearrange("p (c t) -> p c t", c=8)
            for j in range(8):
                k.op("pe", lambda e, j=j: e.transpose(out=ptv[:, j, :DEC_S], in_=pb[:DEC_S, j // 2, (j % 2) * 128:(j % 2) * 128 + 128],
                                                      identity=ident[:DEC_S, :DEC_S]), [pb, ident], [pbuf[bt]])
            k.op("act", lambda e: e.copy(out=pTb[:], in_=ptv[:, :, :DEC_S]), [pbuf[bt]], [pTb])
            for half in range(2):
                bb = nb()
                pv = bank(bb).rearrange("p (c t) -> p c t", c=4)
                for jj in range(4):
                    j = half * 4 + jj
                    h, c = j // 2, j % 2
                    for mc in range(2):
                        k.op("pe", lambda e, mc=mc: e.matmul(pv[:, jj, :DEC_S],
                                                             lhsT=Vb_bf[:, mc, h * 256 + c * 128:h * 256 + c * 128 + 128],
                                                             rhs=pTb[:, h * 2 + mc, :], start=(mc == 0), stop=(mc == 1)),
                             [Vb_bf, pTb], [pbuf[bb]])
                k.op("act", lambda e: e.copy(out=oT[:, half * 4:half * 4 + 4, b * DEC_S:(b + 1) * DEC_S],
                                             in_=pv[:, :, :DEC_S]), [pbuf[bb]], [oT])
            yield

    def b_back(ti, st):
        ht, qT, P, r0, sample = st["ht"], st["qT"], st["P"], st["r0"], st["sample"]
        oT = st["oT"] if sample else k.ring("xoT", [128, 8, 128], BF16, n=1)
        if not sample:
            pb = st["pb"]
            pT = k.ring("xpT", [128, 8, 128], BF16, n=1)
            transpose_to(pT, pb[:P].rearrange("p h m -> p (h m)"), P, 8, deps=[pb])
            for half in range(2):
                bb = nb()
                pv = bank(bb).rearrange("p (c t) -> p c t", c=4)
                for jj in range(4):
                    j = half * 4 + jj
                    h, c = j // 2, j % 2
                    for mc in range(2):
                        k.op("pe", lambda e, mc=mc: e.matmul(pv[:, jj, :P], lhsT=mv_bf[:, mc, h * 256 + c * 128:h * 256 + c * 128 + 128],
                                                             rhs=pT[:, h * 2 + mc, :P], start=(mc == 0), stop=(mc == 1)),
                             [mv_bf, pT], [pbuf[bb]])
                k.op("act", lambda e: e.copy(out=oT[:, half * 4:half * 4 + 4, :P], in_=pv[:, :, :P]), [pbuf[bb]], [oT])
        for half in range(2):
            bh = nb()
            mm_tok(bh, P, oT, w_xo_bf, half * 512, 512)
            k.op("dve", lambda e: e.tensor_tensor(out=ht[:P, half * 512:(half + 1) * 512], in0=bank(bh)[:P, :],
                                                  in1=ht[:P, half * 512:(half + 1) * 512], op=ALU.add),
                 [pbuf[bh]], [ht])
        k.dma("sp", h2_d[r0:r0 + P, :], ht[:P], reads=[ht], writes=[h2_bufs[ti]])
        if debug:
            k.dma("sp", dbg_h2[r0:r0 + P, :], ht[:P], reads=[ht], writes=[dbg_buf])


    st_s = b_front(n_ptiles)
    st_s["oT"] = k.sb("xoTS", [128, 8, 64], BF16)
    sgen = sample_gen(st_s)
    NTP = n_ptiles
    sts = {0: b_front(0)}
    if NTP > 1:
        sts[1] = b_front(1)
    b_mid(0, sts[0])
    for ti in range(NTP):
        if ti + 2 < NTP:
            sts[ti + 2] = b_front(ti + 2)
        if ti + 1 < NTP:
            b_mid(ti + 1, sts[ti + 1])
        b_back(ti, sts.pop(ti))
        for _ in range(-(-SB_PER_CORE // NTP)):
            if sgen is not None:
                try:
                    next(sgen)
                except StopIteration:
                    sgen = None
    if sgen is not None:
        for _ in sgen:
            pass
    b_back(n_ptiles, st_s)
    p0 = p0box[0]
    if p0 is not None:
        for _ in p0:
            pass
    k.phase_reset(MARK)
    load_gains([(3, 0), (4, 1)])
    wq_bf = k.sb("wq_bf", [128, 8, 2048], BF16)
    load_weight_bf(wq_bf, 0, peer_wq, 2048)
    weights_ready()
    skT = k.sb("skT", [128, 16, 128], BF16)
    iota16 = k.sb("iota16", [128, 16], F32)
    k.dma("sp", iota16[:], c_iota16, writes=[iota16])
    identsel = k.sb("identsel", [128, 128, 128], BF16)
    k.dma("sp", identsel[:].rearrange("p a b -> p (a b)"), c_identsel, writes=[identsel])
    MARK_C = k.mark()
    sk_f = k.sb("sk_f", [128, 16, 128], F32)
    k.dma("sp", sk_f[:], peer_sk.rearrange("a k d -> k a d"), writes=[sk_f])
    sk_bf = k.sb("sk_bf", [128, 16 * 128], BF16)
    k.op("dve", lambda e: e.tensor_copy(out=sk_bf[:], in_=sk_f[:].rearrange("p a d -> p (a d)")), [sk_f], [sk_bf])
    transpose_to(skT, sk_bf, 128, 16)
    k.phase_reset(MARK_C)

    ones_f = k.sb("ones_f", [128, 128], F32)
    k.op("dve", lambda e: e.memset(ones_f[:], 1.0), [], [ones_f])

    def routing(ti, res):
        sample = (ti == n_ptiles)
        P = PS if sample else 128
        r0 = ti * 128
        ht = k.ring("xt", [128, D], F32, n=2)
        k.dma("sp", ht[:P], h2_d[r0:r0 + P, :], reads=[h2_bufs[ti]], writes=[ht])
        fn_bf = k.ring("xn", [128, D], BF16, n=2)
        rmsnorm_bf(ht, P, 3, fn_bf)
        yield
        fnT = k.ring("xnT", [128, 8, 128], BF16, n=1)
        transpose_to(fnT, fn_bf, P, 8)
        yield
        xj = k.ring("n2_j", [128, D], BF16, n=1)
        xsq = k.ring("xsq", [128, 1], F32, n=1)
        k.op("act", lambda e: e.activation(out=xj[:P], in_=fn_bf[:P], func=AF.Square, accum_out=xsq[:P]), [fn_bf], [xj, xsq])
        dg = k.ring("xsq_dg", [128, 128], F32, n=1)
        k.op("dve", lambda e: e.tensor_scalar(out=dg[:P, :P], in0=ident_f[:P, :P], scalar1=xsq[:P, 0:1], scalar2=None,
                                              op0=ALU.mult), [ident_f, xsq], [dg])
        bx = nb()
        k.op("pe", lambda e: e.matmul(bank(bx)[:, 0:P], lhsT=ones_f[:P, :], rhs=dg[:P, :P], start=True, stop=True),
             [ones_f, dg], [pbuf[bx]])
        xsqB = k.ring("xsqB", [128, 128], F32, n=2)
        k.op("act", lambda e: e.copy(out=xsqB[:, :P], in_=bank(bx)[:, 0:P]), [pbuf[bx]], [xsqB])
        yield
        qpT = k.ring("qpT", [128, 16, 128], BF16, n=1)
        for g in range(4):
            bb = nb()
            for j in range(4):
                mm_feat(bb, j, P, wq_bf, (g * 4 + j) * 128, fnT)
            k.op("act", lambda e: e.copy(out=qpT[:, g * 4:g * 4 + 4, :P],
                                         in_=bank(bb).rearrange("p (c t) -> p c t", c=4)[:, :, :P]), [pbuf[bb]], [qpT])
            yield
        s_sb = k.ring("s_sb", [128, 16, 128], F32, n=1)
        for g in range(4):
            bb = nb()
            for j in range(4):
                hj = g * 4 + j
                k.op("pe", lambda e, j=j, hj=hj: e.matmul(bank(bb)[:P, j * 128:(j + 1) * 128], lhsT=qpT[:, hj, :P],
                                                          rhs=skT[:, hj, :], start=True, stop=True), [qpT, skT], [pbuf[bb]])
            k.op("act", lambda e: e.copy(out=s_sb[:P, g * 4:g * 4 + 4, :],
                                         in_=bank(bb).rearrange("p (c t) -> p c t", c=4)[:P]), [pbuf[bb]], [s_sb])
            yield
        v16 = k.ring("v16", [128, 16, 16], F32, n=1)
        i16 = k.ring("i16", [128, 16, 16], U32, n=1)
        for hj in range(16):
            for r in range(2):
                k.op("dve", lambda e: e.max(out=v16[:P, hj, r * 8:r * 8 + 8], in_=s_sb[:P, hj, :]), [s_sb], [v16])
                yield
                k.op("dve", lambda e: e.max_index(out=i16[:P, hj, r * 8:r * 8 + 8], in_max=v16[:P, hj, r * 8:r * 8 + 8],
                                                  in_values=s_sb[:P, hj, :]), [s_sb, v16], [i16])
                yield
                if r == 0:
                    k.op("dve", lambda e: e.match_replace(out=s_sb[:P, hj, :], in_to_replace=v16[:P, hj, 0:8],
                                                          in_values=s_sb[:P, hj, :], imm_value=-1e30), [v16], [s_sb])
                    yield
        i16f = k.ring("i16f", [128, 8, 2, 16], F32, n=1)
        k.op("dve", lambda e: e.tensor_copy(out=i16f[:P].rearrange("p h j k -> p (h j) k"), in_=i16[:P]), [i16], [i16f])
        v4 = v16[:P].rearrange("p (h j) k -> p h j k", j=2)
        cand = k.ring("cand", [128, 8, 16, 16], F32, n=1)
        k.op("dve", lambda e: e.tensor_tensor(out=cand[:P], in0=bc(v4[:, :, 0, :], [P, 8, 16, 16], 3),
                                              in1=bc(v4[:, :, 1, :], [P, 8, 16, 16], 2), op=ALU.add), [v16], [cand])
        yield
        tv = k.ring("tv", [128, 8, 16], F32, n=1)
        tp = k.ring("tp", [128, 8, 16], U32, n=1)
        for h in range(8):
            cf = cand[:P, h].rearrange("p a b -> p (a b)")
            for r in range(2):
                k.op("dve", lambda e: e.max(out=tv[:P, h, r * 8:r * 8 + 8], in_=cf), [cand], [tv])
                yield
                k.op("dve", lambda e: e.max_index(out=tp[:P, h, r * 8:r * 8 + 8], in_max=tv[:P, h, r * 8:r * 8 + 8],
                                                  in_values=cf), [cand, tv], [tp])
                yield
                if r == 0:
                    k.op("dve", lambda e: e.match_replace(out=cf, in_to_replace=tv[:P, h, 0:8], in_values=cf,
                                                          imm_value=-1e30), [tv], [cand])
                    yield
        ta = k.ring("ta", [128, 8, 16], U32, n=1)
        tb = k.ring("tb", [128, 8, 16], U32, n=1)
        k.op("dve", lambda e: e.tensor_single_scalar(out=ta[:P], in_=tp[:P], scalar=4, op=ALU.logical_shift_right), [tp], [ta])
        k.op("dve", lambda e: e.tensor_single_scalar(out=tb[:P], in_=tp[:P], scalar=15, op=ALU.bitwise_and), [tp], [tb])
        taf = k.ring("taf", [128, 8, 16], F32, n=1)
        tbf = k.ring("tbf", [128, 8, 16], F32, n=1)
        k.op("dve", lambda e: e.tensor_copy(out=taf[:P], in_=ta[:P]), [ta], [taf])
        k.op("dve", lambda e: e.tensor_copy(out=tbf[:P], in_=tb[:P]), [tb], [tbf])
        yield
        idxf = k.ring("idxf", [128, 8, 16], F32, n=1)
        sel = {}
        for nm, tf, jj in (("a", taf, 0), ("b", tbf, 1)):
            oh = k.ring("cand", [128, 8, 16, 16], F32, n=1)
            io = iota16[:P, :].unsqueeze(1).unsqueeze(1).broadcast_to([P, 8, 16, 16])
            k.op("dve", lambda e: e.tensor_tensor(out=oh[:P], in0=io, in1=bc(tf[:P], [P, 8, 16, 16], 3), op=ALU.is_equal),
                 [iota16, tf], [oh])
            yield
            k.op("dve", lambda e: e.tensor_tensor(out=oh[:P], in0=oh[:P], in1=bc(i16f[:P, :, jj, :], [P, 8, 16, 16], 2),
                                                  op=ALU.mult), [i16f], [oh])
            yield
            sl = k.ring("sel_" + nm, [128, 8, 16], F32, n=1)
            k.op("dve", lambda e: e.tensor_reduce(out=sl[:P], in_=oh[:P], axis=AX.X, op=ALU.add), [oh], [sl])
            sel[nm] = sl
            yield
        k.op("dve", lambda e: e.scalar_tensor_tensor(out=idxf[:P], in0=sel["a"][:P], scalar=128.0, in1=sel["b"][:P],
                                                     op0=ALU.mult, op1=ALU.add), [sel["a"], sel["b"]], [idxf])
        gsh = k.ring("gsh", [128, 8, 16], F32, n=1)
        k.op("dve", lambda e: e.tensor_tensor(out=gsh[:P], in0=tv[:P], in1=bc(tv[:P, :, 0], [P, 8, 16], 2), op=ALU.subtract),
             [tv], [gsh])
        k.op("act", lambda e: e.activation(out=gsh[:P], in_=gsh[:P], func=AF.Exp), [], [gsh])
        gsm = k.ring("gsm", [128, 8], F32, n=1)
        k.op("dve", lambda e: e.tensor_reduce(out=gsm[:P], in_=gsh[:P], axis=AX.X, op=ALU.add), [gsh], [gsm])
        grs = k.ring("grs", [128, 8], F32, n=1)
        k.op("dve", lambda e: e.reciprocal(out=grs[:P], in_=gsm[:P]), [gsm], [grs])
        gates = k.ring("gates", [128, 8, 16], F32, n=1)
        k.op("dve", lambda e: e.tensor_tensor(out=gates[:P], in0=gsh[:P], in1=bc(grs[:P, :], [P, 8, 16], 2), op=ALU.mult),
             [gsh, grs], [gates])
        yield
        if debug:
            dump("idxf", idxf[:P].rearrange("p h k -> p (h k)"), idxf, ti)
            dump("gates", gates[:P].rearrange("p h k -> p (h k)"), gates, ti)
        bt = nb()
        k.op("pe", lambda e: e.matmul(bank(bt)[:, 0:P], lhsT=idxf[:P].rearrange("p h k -> p (h k)"),
                                      rhs=ident_f[:P, :P], start=True, stop=True), [idxf, ident_f], [pbuf[bt]])
        k.op("pe", lambda e: e.matmul(bank(bt)[:, 128:128 + P], lhsT=gates[:P].rearrange("p h k -> p (h k)"),
                                      rhs=ident_f[:P, :P], start=True, stop=True), [gates, ident_f], [pbuf[bt]])
        idxT = k.ring("idxT", [128, 128], I32, n=2)
        gT = k.ring("gT", [128, 128], F32, n=2)
        idxTf = k.ring("idxTf", [128, 128], F32, n=1)
        k.op("act", lambda e: e.copy(out=idxTf[:, :P], in_=bank(bt)[:, 0:P]), [pbuf[bt]], [idxTf])
        k.op("act", lambda e: e.copy(out=gT[:, :P], in_=bank(bt)[:, 128:128 + P]), [pbuf[bt]], [gT])
        k.op("dve", lambda e: e.tensor_copy(out=idxT[:, :P], in_=idxTf[:, :P]), [idxTf], [idxT])
        res.update(ht=ht, fn_bf=fn_bf, idxT=idxT, gT=gT, P=P, r0=r0, xsqB=xsqB)
        yield

    def drain(gen):
        if gen is not None:
            for _ in gen:
                pass

    GA = 4
    results = [dict() for _ in range(NT + 1)]
    drain(routing(0, results[0]))
    UVs_next = {}
    for ti in range(NT):
        R = results[ti]
        ht, fn_bf, idxT, gT, P, r0, xsqB = R["ht"], R["fn_bf"], R["idxT"], R["gT"], R["P"], R["r0"], R["xsqB"]
        nxt = routing(ti + 1, results[ti + 1]) if ti + 1 < NT else None
        if debug:
            dump("idxT", idxT[:, :P], idxT, ti)
        bo0 = nb2()
        hold(bo0), hold(bo0 + 1)
        UVs, pairs, acols, gcols = dict(UVs_next), {}, {}, {}
        UVs_next = {}
        for i in range(-GA, P):
            tg = i + GA
            if 0 <= tg < P and tg not in UVs:
                UV = k.ring("UVg", [128, UVW], BF16, n=GA + 2)
                UVs[tg] = UV
                k.dma("pool", None, None, reads=[idxT, uv_buf], writes=[UV], fn=lambda e: e.indirect_dma_start(
                    out=UV[:], out_offset=None, in_=uv_d,
                    in_offset=bass.IndirectOffsetOnAxis(ap=idxT[:, tg:tg + 1], axis=0)))
            if tg >= P and ti + 1 < NT:
                if nxt is not None:
                    drain(nxt)
                    nxt = None
                tg2 = tg - P
                idxT2 = results[ti + 1]["idxT"]
                UV2 = k.ring("UVg", [128, UVW], BF16, n=GA + 2)
                UVs_next[tg2] = UV2
                k.dma("pool", None, None, reads=[idxT2, uv_buf], writes=[UV2], fn=lambda e: e.indirect_dma_start(
                    out=UV2[:], out_offset=None, in_=uv_d,
                    in_offset=bass.IndirectOffsetOnAxis(ap=idxT2[:, tg2:tg2 + 1], axis=0)))
            tb_ = i + 2
            if 0 <= tb_ < P:
                b0 = nb2()
                hold(b0), hold(b0 + 1)
                pairs[tb_] = b0
                UVb = UVs[tb_]
                for half in range(2):
                    k.op("pe", lambda e: e.matmul(bank(b0 + half)[:, :], lhsT=ident[:P, tb_:tb_ + 1].broadcast_to([P, 128]),
                                                  rhs=fn_bf[:P, half * 512:(half + 1) * 512], start=True, stop=False),
                         [ident, fn_bf], [pbuf[b0 + half]])
                    k.op("pe", lambda e: e.matmul(bank(b0 + half)[:, :], lhsT=ident[:, :],
                                                  rhs=UVb[:, half * 512:(half + 1) * 512], start=False, stop=True),
                         [ident, UVb], [pbuf[b0 + half]])
            t = i
            Wsel = None
            if 0 <= t < P:
                gcol = gcols.pop(t)
                wcol = k.ring("wcol", [128, 1], F32, n=4)
                k.op("dve", lambda e: e.tensor_tensor(out=wcol[:], in0=gcol[:], in1=gT[:, t:t + 1], op=ALU.mult),
                     [gcol, gT], [wcol])
                Wsel = k.ring("Wsel", [128, 128], BF16, n=4)
                k.op("dve", lambda e: e.tensor_scalar(out=Wsel[:, :P], in0=identsel[:, t, :P], scalar1=wcol[:, 0:1],
                                                      scalar2=None, op0=ALU.mult), [identsel, wcol], [Wsel])
            td = i + 1
            if 0 <= td < P:
                b0 = pairs.pop(td)
                UVd = UVs[td]
                negh = k.ring("negh", [128, 1], F32, n=4)
                k.op("dve", lambda e: e.tensor_scalar(out=negh[:], in0=UVd[:, 2 * D:2 * D + 2].bitcast(F32),
                                                      scalar1=xsqB[:, td:td + 1], scalar2=-0.5, op0=ALU.add, op1=ALU.mult),
                     [UVd, xsqB], [negh])
                junk = k.ring("pjunk", [128, D], BF16, n=2)
                acol = k.ring("acol", [128, 1], F32, n=4)
                k.op("act", lambda e: e.activation(out=junk[:], in_=pp[b0 // 2][:, :], func=AF.Square, accum_out=acol[:, 0:1]),
                     [pbuf[b0], pbuf[b0 + 1]], [junk, acol])
                release(b0), release(b0 + 1)
                gcol = k.ring("gcol", [128, 1], F32, n=4)
                k.op("act", lambda e: e.activation(out=gcol[:], in_=acol[:], func=AF.Gelu, scale=0.5, bias=negh[:, 0:1]),
                     [acol, negh], [gcol])
                gcols[td] = gcol
            if 0 <= t < P:
                UV = UVs.pop(t)
                for half in range(2):
                    k.op("pe", lambda e: e.matmul(bank(bo0 + half)[:P, :], lhsT=Wsel[:, :P],
                                                  rhs=UV[:, D + half * 512:D + (half + 1) * 512],
                                                  start=(t == 0), stop=(t == P - 1)), [Wsel, UV], [pbuf[bo0 + half]])
                for _ in range(2 if (t % 3 == 0 or P < 128) else 1):
                    if nxt is not None:
                        try:
                            next(nxt)
                        except StopIteration:
                            nxt = None
        drain(nxt)
        k.op("dve", lambda e: e.tensor_tensor(out=ht[:P], in0=pp[bo0 // 2][:P, :], in1=ht[:P], op=ALU.add),
             [pbuf[bo0], pbuf[bo0 + 1]], [ht])
        release(bo0), release(bo0 + 1)
        if debug:
            k.dma("sp", dbg_h3[r0:r0 + P, :], ht[:P], reads=[ht], writes=[dbg_buf])
        yt = k.ring("yt", [128, D], F32, n=1)
        rmsnorm_bf(ht, P, 4, None, out_f=yt)
        k.dma("sp", y_all[r0:r0 + P, :], yt[:P], reads=[yt], writes=[out_bufs[0]])

    k.finish(out_bufs + ([dbg_buf] if debug else []) + list(dumps.values()))
    return nc


def host_consts(n_ptiles=16):
    TP = n_ptiles * 128
    TT = TP + PS
    c = {}
    c["c_ident"] = np.eye(128, dtype=np.float32)
    tri = np.zeros((2, 128, 128), np.float32)
    utri = np.zeros((2, 128, 128), np.float32)
    s = np.arange(128)[:, None]
    t = np.arange(128)[None, :]
    tri[0] = (s <= t)
    utri[0] = (s > t)
    same = (s // DEC_S == t // DEC_S) & (s < PS) & (t < PS)
    tri[1] = (s <= t) & same
    utri[1] = (s > t) & same
    c["c_tri"], c["c_utri"] = tri, utri
    si = np.zeros((2, 128, 16), np.float32)
    si[0, :, 0] = 1.0
    for b in range(SB_PER_CORE):
        si[1, b * DEC_S:(b + 1) * DEC_S, b] = 1.0
    c["c_seqind"] = si
    siT = np.zeros((16, 64), np.float32)
    for b in range(SB_PER_CORE):
        siT[b, b * DEC_S:(b + 1) * DEC_S] = 1.0
    c["c_seqindT"] = np.ascontiguousarray(np.broadcast_to(siT.reshape(1, -1), (64, 16 * 64))).astype(ml_dtypes.bfloat16)
    half = 32
    inv = (10000.0 ** (-np.arange(half, dtype=np.float32) / half)).astype(np.float32)
    pos = np.concatenate([np.arange(TP), PAST + (np.arange(PS) % DEC_S)]).astype(np.float32)
    tl = np.concatenate([np.arange(TP) % 128, np.arange(PS) % DEC_S]).astype(np.float64)
    L = np.concatenate([np.full(TP, 128.0), np.full(PS, float(DEC_S))])
    ang = (pos[:, None] * inv[None, :]).astype(np.float32)
    cos = np.cos(ang).astype(np.float64)
    sin = np.sin(ang).astype(np.float64)
    lg = np.log1p(-(2.0 ** (-5.0 - np.arange(4, dtype=np.float64))))
    fq = np.exp(lg[None, :] * (tl[:, None] + 1.0))
    fk = np.exp(-lg[None, :] * (tl[:, None] + 1.0)) * 0.125
    fkd = np.exp(lg[None, :] * (L[:, None] - 1.0 - tl[:, None])) * 0.125
    rot = np.zeros((TT, 6, 4, 32), np.float64)
    for i, f in enumerate((fq, fk, fkd)):
        rot[:, 2 * i] = cos[:, None, :] * f[:, :, None]
        rot[:, 2 * i + 1] = sin[:, None, :] * f[:, :, None]
    c["c_rot"] = rot.reshape(TT, 768).astype(np.float32)
    dr = np.zeros((2, 64, 4), np.float64)
    dr[0] = np.exp(lg * 128.0)[None, :]
    dr[1] = np.exp(lg * float(DEC_S))[None, :]
    c["c_decret"] = dr.astype(np.float32)
    c["c_iota16"] = np.ascontiguousarray(np.broadcast_to(np.arange(16, dtype=np.float32)[None, :], (128, 16)))
    c["c_identsel"] = np.ascontiguousarray(np.broadcast_to(np.eye(128, dtype=np.float32).reshape(1, -1), (128, 128 * 128))).astype(ml_dtypes.bfloat16)
    return c


def make_in_maps(inp, n_ptiles=16, cores=8):
    f = lambda a: np.ascontiguousarray(np.asarray(a, dtype=np.float32))
    consts = host_consts(n_ptiles)
    TP = n_ptiles * 128
    shared = {
        "w_in": f(inp["w_in"][0]), "w_a2": f(inp["w_a2"][0]), "b_a": f(inp["b_a"][0]).reshape(1, 256),
        "gains": f(np.stack([inp["norm_mix"][0], inp["norm_xattn"][0], inp["norm_mem"][0], inp["norm_ffn"][0],
                             inp["norm_final"]])),
        "hgains": f(np.stack([inp["gla_head_norm"][0].reshape(512), inp["ret_head_norm"][0].reshape(512)])),
        "w_pa": f(inp["w_pa"][0]), "w_pb": f(inp["w_pb"][0]), "w_o": f(inp["w_o"][0]),
        "w_xq": f(inp["w_xq"][0]), "w_xk": f(inp["w_xk"][0]), "w_xv": f(inp["w_xv"][0]), "w_xo": f(inp["w_xo"][0]),
        "peer_wq": f(inp["peer_wq"][0]), "peer_sk": f(inp["peer_subkeys"][0]).reshape(16, 128, 128),
        "peer_u": f(inp["peer_u"][0]), "peer_v": f(inp["peer_v"][0]),
    }
    shared.update(consts)
    maps = []
    for c in range(cores):
        sb = slice(c * SB_PER_CORE, (c + 1) * SB_PER_CORE)
        m = dict(shared)
        m["x_all"] = f(np.concatenate([np.asarray(inp["x_prompt"][c])[:TP], np.asarray(inp["x_sample"][sb]).reshape(PS, D)], 0))
        m["mem"] = f(inp["mem_prompt"][c])
        m["st_gla"] = f(inp["state_gla"][0][sb])
        m["st_ret"] = f(inp["state_ret"][0][sb])
        m["ck"] = f(np.asarray(inp["cache_mem_k"][0][sb]).reshape(SB_PER_CORE, NMEM, D))
        m["cv"] = f(np.asarray(inp["cache_mem_v"][0][sb]).reshape(SB_PER_CORE, NMEM, D))
        maps.append(m)
    return maps


_NC_CACHE = {}


def kernel(**inp):
    if "nc" not in _NC_CACHE:
        _NC_CACHE["nc"] = build()
    nc = _NC_CACHE["nc"]
    maps = make_in_maps(inp)
    res = run_bass_kernel_spmd(nc, maps, core_ids=list(range(8)))
    R = res.results
    y_p = np.stack([r["y_all"][:SEQ] for r in R], 0)
    y_s = np.concatenate([r["y_all"][SEQ:].reshape(SB_PER_CORE, DEC_S, D) for r in R], 0)
    gla_p = np.stack([r["o_gla_p"] for r in R], 0)[None]
    ret_p = np.stack([r["o_ret_p"] for r in R], 0)[None]
    mk = np.stack([r["o_mk"].reshape(NMEM, 4, 256) for r in R], 0)[None]
    mv = np.stack([r["o_mv"].reshape(NMEM, 4, 256) for r in R], 0)[None]
    gla_s = np.concatenate([r["o_gla_s"] for r in R], 0)[None]
    ret_s = np.concatenate([r["o_ret_s"] for r in R], 0)[None]
    return tuple(np.ascontiguousarray(a, dtype=np.float32) for a in (y_p, y_s, gla_p, ret_p, mk, mv, gla_s, ret_s))
```

```python
import numpy as np
import ml_dtypes
import concourse.bass as bass
import concourse.mybir as mybir
from concourse.bass_utils import run_bass_kernel_spmd

F32 = mybir.dt.float32
BF16 = mybir.dt.bfloat16
I32 = mybir.dt.int32
U32 = mybir.dt.uint32
AF = mybir.ActivationFunctionType
ALU = mybir.AluOpType
AX = mybir.AxisListType

D = 1024
SEQ = 2048
NB = 8
DEC_B = 128
DEC_S = 4
PAST = 16384
INW = 5136
NMEM = 256
EPS = 1e-6
NEXP = 16384
SB_PER_CORE = DEC_B // 8
PS = SB_PER_CORE * DEC_S
C_GQ, C_GK, C_GV, C_GLR, C_GR, C_RQ, C_RK, C_RV, C_RG, C_ZA, C_ZB = (
    0, 256, 512, 1024, 1040, 1552, 1808, 2064, 2576, 3088, 4112)


class Buf:
    __slots__ = ("w", "r", "name")

    def __init__(self, name=""):
        self.w = None
        self.r = {}
        self.name = name


class T:
    def __init__(self, t, buf):
        self.t = t
        self.b = buf

    def __getitem__(self, k):
        return self.t[k]


class K:
    def __init__(self, nc):
        self.nc = nc
        self.E = {"pe": nc.tensor, "act": nc.scalar, "dve": nc.vector, "pool": nc.gpsimd, "sp": nc.sync}
        self.sems = {e: nc.alloc_semaphore("s_" + e) for e in self.E}
        self.cnt = {e: 0 for e in self.E}
        self.seen = {e: {} for e in self.E}
        self.slots = {}
        for q, n in (("sp", 12), ("pool", 8), ("act", 4)):
            self.slots[q] = []
            for i in range(n):
                nm = "d_%s%d" % (q, i)
                self.sems[nm] = nc.alloc_semaphore(nm)
                self.slots[q].append([nm, 0])
        self.slot_i = {q: 0 for q in self.slots}
        self.ncnt = 0
        self.rot = {}

    ARENA_WORDS = 47104

    def sb(self, name, shape, dt):
        if not hasattr(self, "arena"):
            self.arena = self.nc.alloc_sbuf_tensor("arena", [128, self.ARENA_WORDS], F32)
            self.top = 0
        n = 1
        for x in shape[1:]:
            n *= x
        esz = 2 if dt == BF16 else 4
        words = (n * esz + 3) // 4
        words = (words + 7) // 8 * 8
        assert self.top + words <= self.ARENA_WORDS, "arena overflow at %s (%d + %d)" % (name, self.top, words)
        ap = self.arena[0:shape[0], self.top:self.top + words]
        self.top += words
        if dt != F32:
            ap = ap.bitcast(dt)
        ap = ap[:, 0:n]
        if len(shape) > 2:
            names = " ".join("d%d" % i for i in range(len(shape) - 1))
            kw = {"d%d" % i: shape[i + 1] for i in range(len(shape) - 1)}
            ap = ap.rearrange("p (%s) -> p %s" % (names, names), **kw)
        return T(ap, Buf(name))

    def mark(self):
        return self.top

    def phase_reset(self, mark):
        tags = {}
        for e in self.E:
            if self.cnt[e] > 0:
                tags[e] = self.cnt[e]
        for q in self.slots:
            for nm, v in self.slots[q]:
                if v > 0:
                    tags[nm] = v
        for e in self.E:
            self._wait(e, {s_: v for s_, v in tags.items() if s_ != e})
        self.top = mark
        self.rot = {}

    def ring(self, name, shape, dt, n=2):
        if name not in self.rot:
            self.rot[name] = [[self.sb("%s_%d" % (name, i), shape, dt) for i in range(n)], 0]
        lst = self.rot[name]
        t = lst[0][lst[1] % len(lst[0])]
        lst[1] += 1
        return t

    def _wait(self, eng, tags):
        e = self.E[eng]
        for s, v in tags.items():
            if eng == "pe" and s == "pe":
                continue
            if s == eng and eng == "sp":
                continue
            if self.seen[eng].get(s, 0) < v:
                e.wait_ge(self.sems[s], v)
                self.seen[eng][s] = v

    @staticmethod
    def _need(reads, writes):
        tags = {}

        def upd(tag):
            if tag is None:
                return
            s, v = tag
            if tags.get(s, 0) < v:
                tags[s] = v

        for b in reads:
            upd(b.w)
        for b in writes:
            upd(b.w)
            for s, v in b.r.items():
                upd((s, v))
        return tags

    @staticmethod
    def _bufs(xs):
        out = []
        for x in xs:
            if isinstance(x, T):
                out.append(x.b)
            elif isinstance(x, Buf):
                out.append(x)
            elif isinstance(x, (list, tuple)):
                out.extend(K._bufs(x))
            elif x is None:
                pass
            else:
                raise TypeError(type(x))
        return out

    def _done(self, tag, reads, writes):
        s, v = tag
        for b in writes:
            b.w = tag
            b.r = {}
        for b in reads:
            if b in writes:
                continue
            if b.r.get(s, 0) < v:
                b.r[s] = v

    def op(self, eng, fn, reads=(), writes=()):
        reads = self._bufs(reads)
        writes = self._bufs(writes)
        self._wait(eng, self._need(reads, writes))
        ins = fn(self.E[eng])
        self.cnt[eng] += 1
        ins.then_inc(self.sems[eng], 1)
        self._done((eng, self.cnt[eng]), reads, writes)
        self.ncnt += 1
        return ins

    def dma(self, q, out, in_, reads=(), writes=(), fn=None):
        reads = self._bufs(reads)
        writes = self._bufs(writes)
        need = self._need(reads, writes)
        i = self.slot_i[q]
        self.slot_i[q] = (i + 1) % len(self.slots[q])
        slot = self.slots[q][i]
        if slot[1] > 0:
            if need.get(slot[0], 0) < slot[1]:
                need[slot[0]] = slot[1]
        self._wait(q, need)
        if fn is None:
            ins = self.E[q].dma_start(out=out, in_=in_)
        else:
            ins = fn(self.E[q])
        slot[1] += 16
        ins.then_inc(self.sems[slot[0]], 16)
        self._done((slot[0], slot[1]), reads, writes)
        self.ncnt += 1
        return ins

    def finish(self, bufs):
        self._wait("sp", self._need(self._bufs(bufs), []))


def bc(ap, shape, axis):
    return ap.unsqueeze(axis).broadcast_to(list(shape))


def build(n_ptiles=16, debug=False, peer_mode=1, cut=99):
    nc = bass.Bass("TRN2", target_bir_lowering=False)
    k = K(nc)
    NT = n_ptiles + 1
    TP = n_ptiles * 128
    TT = TP + PS

    def din(name, shape, dt=F32):
        return nc.dram_tensor(name, list(shape), dt, kind="ExternalInput").ap()

    def dout(name, shape, dt=F32):
        return nc.dram_tensor(name, list(shape), dt, kind="ExternalOutput").ap()

    x_all = din("x_all", [TT, D])
    mem = din("mem", [NMEM, D])
    st_gla = din("st_gla", [SB_PER_CORE, 4, 64, 128])
    st_ret = din("st_ret", [SB_PER_CORE, 4, 64, 128])
    ck_in = din("ck", [SB_PER_CORE, NMEM, D])
    cv_in = din("cv", [SB_PER_CORE, NMEM, D])
    w_in = din("w_in", [D, INW])
    w_a2 = din("w_a2", [16, 256])
    b_a = din("b_a", [1, 256])
    gains = din("gains", [5, D])
    hgains = din("hgains", [2, 512])
    w_pa = din("w_pa", [512, D])
    w_pb = din("w_pb", [512, D])
    w_o = din("w_o", [D, D])
    w_xq = din("w_xq", [D, D])
    w_xk = din("w_xk", [D, D])
    w_xv = din("w_xv", [D, D])
    w_xo = din("w_xo", [D, D])
    peer_wq = din("peer_wq", [D, 2048])
    peer_sk = din("peer_sk", [16, 128, 128])
    peer_u = din("peer_u", [NEXP, D])
    peer_v = din("peer_v", [NEXP, D])
    c_ident = din("c_ident", [128, 128])
    c_tri = din("c_tri", [2, 128, 128])
    c_utri = din("c_utri", [2, 128, 128])
    c_seqind = din("c_seqind", [2, 128, 16])
    c_seqindT = din("c_seqindT", [64, 16 * 64], BF16)
    c_rot = din("c_rot", [TT, 6 * 128])
    c_decret = din("c_decret", [2, 64, 4])
    c_iota16 = din("c_iota16", [128, 16])
    c_identsel = din("c_identsel", [128, 128 * 128], BF16)

    y_all = dout("y_all", [TT, D])
    o_gla_p = dout("o_gla_p", [4, 64, 128])
    o_ret_p = dout("o_ret_p", [4, 64, 128])
    o_mk = dout("o_mk", [NMEM, D])
    o_mv = dout("o_mv", [NMEM, D])
    o_gla_s = dout("o_gla_s", [SB_PER_CORE, 4, 64, 128])
    o_ret_s = dout("o_ret_s", [SB_PER_CORE, 4, 64, 128])
    out_bufs = [Buf("o%d" % i) for i in range(8)]
    if debug:
        dbg_h1 = dout("dbg_h1", [TT, D])
        dbg_h2 = dout("dbg_h2", [TT, D])
        dbg_h3 = dout("dbg_h3", [TT, D])
        dbg_buf = Buf("dbg")

    dumps = {}

    def dump(name, ap, dep, ti_=0, only=0):
        if not debug or ti_ != only or name in dumps:
            return
        shp = list(ap.shape)
        dtn = dout("dd_" + name, shp, ap.dtype)
        dumps[name] = Buf("dd_" + name)
        k.dma("sp", dtn, ap, reads=[dep], writes=[dumps[name]])

    h1_d = nc.dram_tensor("h1_scratch", [TT, D], F32, kind="Internal").ap()
    h1_bufs = [Buf("h1_%d" % i) for i in range(NT)]

    pp = [nc.alloc_psum_tensor("pp%d" % i, [128, 1024], F32) for i in range(4)]
    pbuf = [Buf("bank%d" % i) for i in range(8)]
    bank_i = [0]

    def bank(i):
        return pp[i // 2][:, (i % 2) * 512:(i % 2) * 512 + 512]

    held = set()

    def nb():
        while True:
            i = bank_i[0] % 8
            bank_i[0] += 1
            if i not in held:
                return i

    def hold(i):
        held.add(i)

    def release(i):
        held.discard(i)

    def nb2():
        while True:
            if bank_i[0] % 2:
                bank_i[0] += 1
            i = bank_i[0] % 8
            bank_i[0] += 2
            if i not in held and (i + 1) not in held:
                return i

    ident_f = k.sb("ident_f", [128, 128], F32)
    ident = k.sb("ident", [128, 128], BF16)
    tri = k.sb("tri", [128, 2, 128], F32)
    utri = k.sb("utri", [128, 2, 128], F32)
    seqind = k.sb("seqind", [128, 2, 16], F32)
    decret = k.sb("decret", [64, 2, 4], F32)
    ones_row = k.sb("ones_row", [1, 128], F32)
    hgain_t = k.sb("hgain_t", [128, 2, 512], F32)
    wa2_t = k.sb("wa2_t", [16, 256], F32)
    ba_t = k.sb("ba_t", [1, 256], F32)

    k.dma("sp", ident_f[:], c_ident, writes=[ident_f])
    k.dma("sp", tri[:], c_tri.rearrange("a p t -> p a t"), writes=[tri])
    k.dma("sp", utri[:], c_utri.rearrange("a p t -> p a t"), writes=[utri])
    k.dma("sp", seqind[:], c_seqind.rearrange("a p t -> p a t"), writes=[seqind])
    k.dma("sp", decret[:], c_decret.rearrange("a p t -> p a t"), writes=[decret])
    k.dma("sp", wa2_t[:], w_a2, writes=[wa2_t])
    k.dma("sp", ba_t[:], b_a, writes=[ba_t])
    gslot = {}
    gain_box = [None]

    def load_gains(pairs):
        gain_box[0] = k.sb("gain_t", [128, len(pairs), D], F32)
        gain_t = gain_box[0]
        for gi, slot in pairs:
            gslot[gi] = slot
            k.dma("sp", gain_t[:, slot, :], gains[gi].partition_broadcast(128), writes=[gain_t])
    for i in range(2):
        k.dma("sp", hgain_t[:, i, :], hgains[i].partition_broadcast(128), writes=[hgain_t])
    k.op("dve", lambda e: e.tensor_copy(out=ident[:], in_=ident_f[:]), [ident_f], [ident])
    k.op("pool", lambda e: e.memset(ones_row[:], 1.0), [], [ones_row])
    eps_t = k.sb("eps_t", [128, 1], F32)
    k.op("pool", lambda e: e.memset(eps_t[:], EPS), [], [eps_t])
    one_t = k.sb("one_t", [128, 1], F32)
    k.op("pool", lambda e: e.memset(one_t[:], 1.0), [], [one_t])

    wl_pending = []

    def weights_ready():
        tags = {nm: v for nm, v in k.slots["pool"] if v > 0}
        for e in ("pe", "act", "dve", "pool", "sp"):
            k._wait(e, dict(tags))
        del wl_pending[:]

    def load_weight_bf(dst, row0, src, ncols, col0=0, chunk=512):
        kcs = src.shape[0] // 128
        for kc in range(kcs):
            for c0 in range(0, ncols, 2048):
                cw = min(2048, ncols - c0)
                k.dma("pool", dst[:, row0 + kc, col0 + c0:col0 + c0 + cw], src[kc * 128:(kc + 1) * 128, c0:c0 + cw],
                      writes=[Buf("wld")])
        wl_pending.append(dst)

    def rmsnorm_bf(xt, P, gi, out_bf, out_f=None):
        junk = out_bf if out_bf is not None else k.ring("rms_junk", [128, D], BF16, n=1)
        ssq = k.ring("rms_ssq", [128, 1], F32, n=2)
        k.op("act", lambda e: e.activation(out=junk[:P], in_=xt[:P], func=AF.Square, accum_out=ssq[:P]),
             [xt], [junk, ssq])
        rstd = k.ring("rms_rstd", [128, 1], F32, n=2)
        k.op("act", lambda e: e.activation(out=rstd[:P], in_=ssq[:P], func=AF.Sqrt, scale=1.0 / D, bias=eps_t[:P, 0:1]),
             [ssq, eps_t], [rstd])
        rstd2 = k.ring("rms_rstd2", [128, 1], F32, n=2)
        k.op("dve", lambda e: e.reciprocal(out=rstd2[:P], in_=rstd[:P]), [rstd], [rstd2])
        if out_f is not None:
            k.op("dve", lambda e: e.scalar_tensor_tensor(out=out_f[:P], in0=xt[:P], scalar=rstd2[:P, 0:1],
                                                         in1=gain_box[0][:P, gslot[gi], :], op0=ALU.mult, op1=ALU.mult),
                 [xt, rstd2, gain_box[0]], [out_f])
            if out_bf is not None:
                k.op("pool", lambda e: e.tensor_copy(out=out_bf[:P], in_=out_f[:P]), [out_f], [out_bf])
        else:
            k.op("dve", lambda e: e.scalar_tensor_tensor(out=out_bf[:P], in0=xt[:P], scalar=rstd2[:P, 0:1],
                                                         in1=gain_box[0][:P, gslot[gi], :], op0=ALU.mult, op1=ALU.mult),
                 [xt, rstd2, gain_box[0]], [out_bf])

    def transpose_to(dstT, src_bf, P, nchunks, evac="act", src_col0=0, dcol0=0, deps=None):
        for c0 in range(0, nchunks, 8):
            n = min(8, nchunks - c0)
            bi = nb()
            pv = bank(bi).bitcast(BF16).rearrange("p (c t) -> p c t", c=8)
            for c in range(n):
                k.op("pe", lambda e, c=c: e.transpose(out=pv[:, c, :P],
                                                      in_=src_bf[:P, src_col0 + (c0 + c) * 128:src_col0 + (c0 + c + 1) * 128],
                                                      identity=ident[:P, :P]),
                     (deps if deps is not None else [src_bf]) + [ident], [pbuf[bi]])
            if evac == "act":
                k.op("act", lambda e: e.copy(out=dstT[:, c0:c0 + n, dcol0:dcol0 + P], in_=pv[:, :n, :P]), [pbuf[bi]], [dstT])
            else:
                k.op(evac, lambda e: e.tensor_copy(out=dstT[:, c0:c0 + n, dcol0:dcol0 + P], in_=pv[:, :n, :P]), [pbuf[bi]], [dstT])

    def mm_tok(psum_bank, P, lhsT_t, w_t, col0, ncols, kcs=8, tok0=0):
        for kc in range(kcs):
            k.op("pe", lambda e, kc=kc: e.matmul(bank(psum_bank)[:P, :ncols], lhsT=lhsT_t[:, kc, tok0:tok0 + P],
                                                 rhs=w_t[:, kc, col0:col0 + ncols],
                                                 start=(kc == 0), stop=(kc == kcs - 1)),
                 [lhsT_t, w_t], [pbuf[psum_bank]])

    def mm_feat(psum_bank, slot, P, w_t, col0, rhsT_t, kcs=8, mcols=128):
        pv = bank(psum_bank).rearrange("p (c t) -> p c t", c=4)
        for kc in range(kcs):
            k.op("pe", lambda e, kc=kc: e.matmul(pv[:mcols, slot, :P], lhsT=w_t[:, kc, col0:col0 + mcols],
                                                 rhs=rhsT_t[:, kc, :P],
                                                 start=(kc == 0), stop=(kc == kcs - 1)),
                 [w_t, rhsT_t], [pbuf[psum_bank]])

    MARK = k.mark()

    UVW = 2 * D + 64
    uv_d = nc.dram_tensor("uv_scratch", [NEXP, UVW], BF16, kind="Internal").ap()
    uv_buf = Buf("uv")

    def p0_gen():
        RB = 1024
        for (tab, c0) in ((peer_u, 0), (peer_v, D)):
            for rr in range(0, NEXP, RB):
                k.dma("pool", uv_d[rr:rr + RB, c0:c0 + D], tab[rr:rr + RB, :], writes=[Buf("uvst")])
                yield

    load_gains([(0, 0)])
    seqindT = k.sb("seqindT", [64, 16, 64], BF16)
    k.dma("sp", seqindT[:].rearrange("p a b -> p (a b)"), c_seqindT, writes=[seqindT])
    w_in_bf = k.sb("w_in_bf", [128, 8, INW], BF16)
    w_pa_bf = k.sb("w_pa_bf", [128, 4, D], BF16)
    w_pb_bf = k.sb("w_pb_bf", [128, 4, D], BF16)
    w_o_bf = k.sb("w_o_bf", [128, 8, D], BF16)
    load_weight_bf(w_in_bf, 0, w_in, INW)
    load_weight_bf(w_pa_bf, 0, w_pa, D)
    load_weight_bf(w_pb_bf, 0, w_pb, D)
    load_weight_bf(w_o_bf, 0, w_o, D)
    weights_ready()

    S_f = {m: k.sb("S_f_" + m, [64, 4, 128], F32) for m in ("a", "b")}
    S_bf = {m: k.sb("S_bf_" + m, [64, 4, 128], BF16) for m in ("a", "b")}

    def mixer_core(m, P, ti, q_e, k_e, k_d, v_bf, dec, sg, hg_i, oT_dst, first):
        sample = (ti == n_ptiles)
        mi = 1 if sample else 0
        NS = SB_PER_CORE if sample else 1
        qT = k.ring("qT", [64, 4, 128], BF16, n=1)
        kT = k.ring("kT", [64, 4, 128], BF16, n=1)
        for src, dst in ((q_e, qT), (k_e, kT)):
            bi = nb()
            pv = bank(bi).bitcast(BF16).rearrange("p (c t) -> p c t", c=8)
            for h in range(4):
                k.op("pe", lambda e, h=h: e.transpose(out=pv[:64, h, :P], in_=src[:P, h * 64:(h + 1) * 64],
                                                      identity=ident[:P, :P]), [src, ident], [pbuf[bi]])
            k.op("act", lambda e: e.copy(out=dst[:, :, :P], in_=pv[:64, 0:4, :P]), [pbuf[bi]], [dst])
        bi = nb()
        av = bank(bi).rearrange("p (c t) -> p c t", c=4)
        for h in range(4):
            k.op("pe", lambda e, h=h: e.matmul(av[:P, h, :P], lhsT=kT[:, h, :P], rhs=qT[:, h, :P],
                                               start=True, stop=True), [kT, qT], [pbuf[bi]])
        attT = k.ring("attT", [128, 4, 128], BF16, n=1)
        k.op("dve", lambda e: e.tensor_tensor(out=attT[:P, :, :P], in0=av[:P, :, :P],
                                              in1=bc(tri[:P, mi, :P], [P, 4, P], 1), op=ALU.mult),
             [pbuf[bi], tri], [attT])
        if not sample:
            bo = nb()
            hold(bo)
            ov = bank(bo).rearrange("p (c t) -> p c t", c=4)
            ov_dep = pbuf[bo]
            for h in range(4):
                k.op("pe", lambda e, h=h: e.matmul(ov[:P, h, :], lhsT=attT[:P, h, :P], rhs=v_bf[:P, h * 128:(h + 1) * 128],
                                                   start=True, stop=first), [attT, v_bf], [pbuf[bo]])
                if not first:
                    k.op("pe", lambda e, h=h: e.matmul(ov[:P, h, :], lhsT=qT[:, h, :P], rhs=S_bf[m][:, h, :],
                                                       start=False, stop=True), [qT, S_bf[m]], [pbuf[bo]])
            bs = nb()
            sv = bank(bs).rearrange("p (c t) -> p c t", c=4)
            for h in range(4):
                k.op("pe", lambda e, h=h: e.matmul(sv[:64, h, :], lhsT=k_d[:P, h * 64:(h + 1) * 64],
                                                   rhs=v_bf[:P, h * 128:(h + 1) * 128], start=True, stop=True),
                     [k_d, v_bf], [pbuf[bs]])
            if first:
                k.op("dve", lambda e: e.tensor_copy(out=S_f[m][:], in_=sv[:64, :, :]), [pbuf[bs]], [S_f[m]])
            else:
                for h in range(4):
                    k.op("dve", lambda e, h=h: e.scalar_tensor_tensor(
                        out=S_f[m][:, h, :], in0=S_f[m][:, h, :], scalar=dec[:, h, 0:1], in1=sv[:64, h, :],
                        op0=ALU.mult, op1=ALU.add), [dec, pbuf[bs]], [S_f[m]])
            k.op("pool", lambda e: e.tensor_copy(out=S_bf[m][:], in_=S_f[m][:]), [S_f[m]], [S_bf[m]])
            if ti == n_ptiles - 1:
                dst = o_gla_p if m == "a" else o_ret_p
                ob = out_bufs[2] if m == "a" else out_bufs[3]
                k.dma("sp", dst.rearrange("h k v -> k h v"), S_f[m][:], reads=[S_f[m]], writes=[ob])
        else:
            GS = 1
            bos = []
            for h in range(4):
                bb = nb()
                hold(bb)
                bos.append(bb)
                k.op("pe", lambda e, h=h, bb=bb: e.matmul(bank(bb)[:P, 0:128], lhsT=attT[:P, h, :P],
                                                          rhs=v_bf[:P, h * 128:(h + 1) * 128], start=True, stop=False),
                     [attT, v_bf], [pbuf[bb]])
            src = st_gla if m == "a" else st_ret
            dst = o_gla_s if m == "a" else o_ret_s
            ob = out_bufs[6] if m == "a" else out_bufs[7]
            for g in range(SB_PER_CORE // GS):
                S0 = k.ring("S0_f", [64, GS, 4, 128], F32, n=1)
                for j in range(GS):
                    k.dma("sp", S0[:, j, :, :], src[g * GS + j].rearrange("h k v -> k h v"), writes=[S0])
                qTm = k.ring("qTm", [64, 4, GS, 64], F32, n=1)
                k.op("dve", lambda e: e.tensor_tensor(out=qTm[:], in0=bc(qT[:, :, :P], [64, 4, GS, P], 2),
                                                      in1=bc(seqindT[:, g * GS:(g + 1) * GS, :], [64, 4, GS, P], 1),
                                                      op=ALU.mult), [qT, seqindT], [qTm])
                for h in range(4):
                    for j in range(GS):
                        last = (g == SB_PER_CORE // GS - 1 and j == GS - 1)
                        k.op("pe", lambda e, h=h, j=j: e.matmul(bank(bos[h])[:P, 0:128], lhsT=qTm[:, h, j, :],
                                                                rhs=S0[:, j, h, :], start=False, stop=last),
                             [qTm, S0], [pbuf[bos[h]]])
                kdm = k.ring("kdm", [64, GS, 256], BF16, n=1)
                k.op("dve", lambda e: e.tensor_tensor(out=kdm[:], in0=bc(k_d[:P, :], [P, GS, 256], 1),
                                                      in1=bc(seqind[:P, 1, g * GS:(g + 1) * GS], [P, GS, 256], 2),
                                                      op=ALU.mult), [k_d, seqind], [kdm])
                for j in range(GS):
                    bs = nb()
                    sv = bank(bs).rearrange("p (c t) -> p c t", c=4)
                    for h in range(4):
                        k.op("pe", lambda e, h=h, j=j: e.matmul(sv[:64, h, :], lhsT=kdm[:P, j, h * 64:(h + 1) * 64],
                                                                rhs=v_bf[:P, h * 128:(h + 1) * 128], start=True, stop=True),
                             [kdm, v_bf], [pbuf[bs]])
                    for h in range(4):
                        k.op("dve", lambda e, h=h, j=j: e.scalar_tensor_tensor(
                            out=S0[:, j, h, :], in0=S0[:, j, h, :], scalar=dec[:, h, g * GS + j:g * GS + j + 1],
                            in1=sv[:64, h, :], op0=ALU.mult, op1=ALU.add), [dec, pbuf[bs]], [S0])
                for j in range(GS):
                    k.dma("sp", dst[g * GS + j].rearrange("h k v -> k h v"), S0[:, j, :, :], reads=[S0], writes=[ob])
            o_sb = k.ring("o_sb", [64, 4, 128], F32, n=1)
            for h in range(4):
                k.op("act", lambda e, h=h: e.copy(out=o_sb[:P, h, :], in_=bank(bos[h])[:P, 0:128]), [pbuf[bos[h]]], [o_sb])
                release(bos[h])
            ov = o_sb
            ov_dep = o_sb
            bo = None
        sq = k.ring("hn_sq", [128, 4, 128], F32, n=1)
        k.op("act", lambda e: e.activation(out=sq[:P], in_=ov[:P], func=AF.Square), [ov_dep], [sq])
        ssq = k.ring("hn_ssq", [128, 4], F32)
        k.op("dve", lambda e: e.tensor_reduce(out=ssq[:P], in_=sq[:P], axis=AX.X, op=ALU.add), [sq], [ssq])
        r1 = k.ring("hn_r1", [128, 4], F32)
        k.op("act", lambda e: e.activation(out=r1[:P], in_=ssq[:P], func=AF.Sqrt, scale=1.0 / 128, bias=eps_t[:P, 0:1]),
             [ssq, eps_t], [r1])
        r2 = k.ring("hn_r2", [128, 4], F32)
        k.op("dve", lambda e: e.reciprocal(out=r2[:P], in_=r1[:P]), [r1], [r2])
        on = sq
        k.op("dve", lambda e: e.tensor_tensor(out=on[:P], in0=ov[:P], in1=bc(r2[:P, :], [P, 4, 128], 2), op=ALU.mult),
             [ov_dep, r2], [on])
        if bo is not None:
            release(bo)
        gg = sg
        k.op("pool", lambda e: e.tensor_tensor(out=gg[:P], in0=sg[:P], in1=hgain_t[:P, hg_i, :], op=ALU.mult),
             [hgain_t], [gg])
        ob_bf = k.ring("hn_obf", [128, 512], BF16, n=1)
        k.op("dve", lambda e: e.tensor_tensor(out=ob_bf[:P], in0=on[:P].rearrange("p a b -> p (a b)"), in1=gg[:P],
                                              op=ALU.mult), [on, gg], [ob_bf])
        transpose_to(oT_dst, ob_bf, P, 4)

    p0 = p0_gen()
    for ti in range(NT):
        sample = (ti == n_ptiles)
        P = PS if sample else 128
        mi = 1 if sample else 0
        NS = SB_PER_CORE if sample else 1
        r0 = ti * 128
        first = (ti == 0)
        for _ in range(2):
            if p0 is not None:
                try:
                    next(p0)
                except StopIteration:
                    p0 = None
        xt = k.ring("xt", [128, D], F32, n=1)
        k.dma("sp", xt[:P], x_all[r0:r0 + P, :], writes=[xt])
        rot_t = k.ring("rot_t", [128, 6, 4, 32], F32, n=1)
        k.dma("sp", rot_t[:P].rearrange("p a h i -> p (a h i)"), c_rot[r0:r0 + P, :], writes=[rot_t])
        xn = k.ring("xn", [128, D], BF16, n=1)
        rmsnorm_bf(xt, P, 0, xn)
        xnT = k.ring("xnT", [128, 8, 128], BF16, n=1)
        transpose_to(xnT, xn, P, 8)

        qk_a = k.ring("qk_ab", [128, 512], F32, n=1)
        b1 = nb()
        mm_tok(b1, P, xnT, w_in_bf, C_GQ, 512)
        k.op("act", lambda e: e.copy(out=qk_a[:P], in_=bank(b1)[:P, :]), [pbuf[b1]], [qk_a])
        v_a = k.ring("v_ab", [128, 512], BF16, n=1)
        b2 = nb()
        mm_tok(b2, P, xnT, w_in_bf, C_GV, 512)
        k.op("act", lambda e: e.copy(out=v_a[:P], in_=bank(b2)[:P, :]), [pbuf[b2]], [v_a])
        sg_a = k.ring("sg_ab", [128, 512], F32, n=1)
        b3 = nb()
        mm_tok(b3, P, xnT, w_in_bf, C_GR, 512)
        k.op("act", lambda e: e.activation(out=sg_a[:P], in_=bank(b3)[:P, :], func=AF.Silu), [pbuf[b3]], [sg_a])
        qk_b = k.ring("qk_bb", [128, 512], F32, n=1)
        b4 = nb()
        mm_tok(b4, P, xnT, w_in_bf, C_RQ, 512)
        k.op("act", lambda e: e.copy(out=qk_b[:P], in_=bank(b4)[:P, :]), [pbuf[b4]], [qk_b])
        v_b = k.ring("v_bb", [128, 512], BF16, n=1)
        b5 = nb()
        mm_tok(b5, P, xnT, w_in_bf, C_RV, 512)
        k.op("act", lambda e: e.copy(out=v_b[:P], in_=bank(b5)[:P, :]), [pbuf[b5]], [v_b])
        sg_b = k.ring("sg_bb", [128, 512], F32, n=1)
        b6 = nb()
        mm_tok(b6, P, xnT, w_in_bf, C_RG, 512)
        k.op("act", lambda e: e.activation(out=sg_b[:P], in_=bank(b6)[:P, :], func=AF.Silu), [pbuf[b6]], [sg_b])
        qe_b = k.ring("qe_bb", [128, 256], BF16, n=1)
        ke_b = k.ring("ke_bb", [128, 256], BF16, n=1)
        kd_b = k.ring("kd_bb", [128, 256], BF16, n=1)
        for (dst, col0, ci) in ((qe_b, 0, 0), (ke_b, 256, 2), (kd_b, 256, 4)):
            xv = qk_b[:P, col0:col0 + 256].rearrange("p (h i) -> p h i", h=4)
            x1, x2 = xv[:, :, 0:32], xv[:, :, 32:64]
            dv3 = dst[:P, :].rearrange("p (h i) -> p h i", h=4)
            cc, ss = rot_t[:P, ci], rot_t[:P, ci + 1]
            t1 = k.ring("rt1", [128, 4, 32], F32, n=1)
            t2 = k.ring("rt2", [128, 4, 32], F32, n=1)
            k.op("dve", lambda e: e.tensor_tensor(out=t1[:P], in0=x1, in1=cc, op=ALU.mult), [qk_b, rot_t], [t1])
            k.op("pool", lambda e: e.tensor_tensor(out=t2[:P], in0=x2, in1=ss, op=ALU.mult), [qk_b, rot_t], [t2])
            k.op("dve", lambda e: e.tensor_tensor(out=dv3[:, :, 0:32], in0=t1[:P], in1=t2[:P], op=ALU.subtract),
                 [t1, t2], [dst])
            t3 = k.ring("rt1", [128, 4, 32], F32, n=1)
            t4 = k.ring("rt2", [128, 4, 32], F32, n=1)
            k.op("dve", lambda e: e.tensor_tensor(out=t3[:P], in0=x1, in1=ss, op=ALU.mult), [qk_b, rot_t], [t3])
            k.op("pool", lambda e: e.tensor_tensor(out=t4[:P], in0=x2, in1=cc, op=ALU.mult), [qk_b, rot_t], [t4])
            k.op("dve", lambda e: e.tensor_tensor(out=dv3[:, :, 32:64], in0=t3[:P], in1=t4[:P], op=ALU.add),
                 [t3, t4], [dst])
        dec_b = k.ring("dec_b", [64, 4, 16], F32, n=1)
        k.op("dve", lambda e: e.tensor_copy(out=dec_b[:], in_=bc(decret[:, mi, :], [64, 4, 16], 2)), [decret], [dec_b])
        b7 = nb()
        mm_feat(b7, 0, P, w_in_bf, C_GLR, xnT, mcols=16)
        glrT = k.ring("glrT", [16, 128], F32, n=1)
        k.op("act", lambda e: e.copy(out=glrT[:, :P], in_=bank(b7)[:16, :P]), [pbuf[b7]], [glrT])

        b8 = nb()
        k.op("pe", lambda e: e.matmul(bank(b8)[:P, :256], lhsT=glrT[:, :P], rhs=wa2_t[:, :], start=True, stop=False),
             [glrT, wa2_t], [pbuf[b8]])
        k.op("pe", lambda e: e.matmul(bank(b8)[:P, :256], lhsT=ones_row[:, :P], rhs=ba_t[:, :], start=False, stop=True),
             [ones_row, ba_t], [pbuf[b8]])
        ez = k.ring("ez", [128, 256], F32, n=1)
        k.op("act", lambda e: e.activation(out=ez[:P], in_=bank(b8)[:P, :256], func=AF.Exp, scale=-1.0), [pbuf[b8]], [ez])
        nla = ez
        k.op("act", lambda e: e.activation(out=nla[:P], in_=ez[:P], func=AF.Ln, bias=one_t[:P, 0:1]), [one_t], [nla])
        nla2 = ez
        k.op("dve", lambda e: e.tensor_scalar(out=nla2[:P], in0=nla[:P], scalar1=1.0 / 16, scalar2=None, op0=ALU.mult),
             [], [nla2])
        b9 = nb()
        k.op("pe", lambda e: e.matmul(bank(b9)[:P, 0:256], lhsT=tri[:P, mi, :P], rhs=nla2[:P, :], start=True, stop=True),
             [tri, nla2], [pbuf[b9]])
        k.op("pe", lambda e: e.matmul(bank(b9)[:P, 256:512], lhsT=utri[:P, mi, :P], rhs=nla2[:P, :], start=True, stop=True),
             [utri, nla2], [pbuf[b9]])
        b10 = nb()
        dv_ = bank(b10).rearrange("p (c t) -> p c t", c=4)
        for h in range(4):
            k.op("pe", lambda e, h=h: e.matmul(dv_[:64, h, :max(NS, 2)], lhsT=nla2[:P, h * 64:(h + 1) * 64],
                                               rhs=seqind[:P, mi, :max(NS, 2)], start=True, stop=True),
                 [nla2, seqind], [pbuf[b10]])
        dec_a = k.ring("dec_a", [64, 4, 16], F32, n=1)
        k.op("act", lambda e: e.activation(out=dec_a[:, :, :NS], in_=dv_[:64, :, :NS], func=AF.Exp, scale=-1.0),
             [pbuf[b10]], [dec_a])
        qe_a = k.ring("qe_ab", [128, 256], BF16, n=1)
        ke_a = k.ring("ke_ab", [128, 256], BF16, n=1)
        kd_a = k.ring("kd_ab", [128, 256], BF16, n=1)
        eq = k.ring("e3", [128, 256], F32, n=1)
        k.op("act", lambda e: e.activation(out=eq[:P], in_=bank(b9)[:P, 0:256], func=AF.Exp, scale=-1.0), [pbuf[b9]], [eq])
        k.op("dve", lambda e: e.scalar_tensor_tensor(out=qe_a[:P], in0=eq[:P], scalar=0.125, in1=qk_a[:P, 0:256],
                                                     op0=ALU.mult, op1=ALU.mult), [eq, qk_a], [qe_a])
        ek = k.ring("e3", [128, 256], F32, n=1)
        k.op("act", lambda e: e.activation(out=ek[:P], in_=bank(b9)[:P, 0:256], func=AF.Exp, scale=1.0), [pbuf[b9]], [ek])
        k.op("dve", lambda e: e.tensor_tensor(out=ke_a[:P], in0=ek[:P], in1=qk_a[:P, 256:512], op=ALU.mult),
             [ek, qk_a], [ke_a])
        ekd = k.ring("e3", [128, 256], F32, n=1)
        k.op("act", lambda e: e.activation(out=ekd[:P], in_=bank(b9)[:P, 256:512], func=AF.Exp, scale=-1.0), [pbuf[b9]], [ekd])
        k.op("dve", lambda e: e.tensor_tensor(out=kd_a[:P], in0=ekd[:P], in1=qk_a[:P, 256:512], op=ALU.mult),
             [ekd, qk_a], [kd_a])
        oaT = k.ring("oaT", [128, 4, 128], BF16, n=1)
        mixer_core("a", P, ti, qe_a, ke_a, kd_a, v_a, dec_a, sg_a, 0, oaT, first)

        obT = k.ring("obT", [128, 4, 128], BF16, n=1)
        mixer_core("b", P, ti, qe_b, ke_b, kd_b, v_b, dec_b, sg_b, 1, obT, first)

        mT = k.ring("mT", [128, 8, 128], BF16, n=1)
        for half in range(2):
            bz_a, bz_b, bp_a, bp_b = nb(), nb(), nb(), nb()
            for j in range(4):
                dt_ = half * 4 + j
                mm_feat(bz_a, j, P, w_in_bf, C_ZA + dt_ * 128, xnT)
                mm_feat(bz_b, j, P, w_in_bf, C_ZB + dt_ * 128, xnT)
                mm_feat(bp_a, j, P, w_pa_bf, dt_ * 128, oaT, kcs=4)
                mm_feat(bp_b, j, P, w_pb_bf, dt_ * 128, obT, kcs=4)

            def v4(bi):
                return bank(bi).rearrange("p (c t) -> p c t", c=4)[:, :, :P]
            sza = k.ring("sza", [128, 4, 128], F32, n=1)
            szb = k.ring("szb", [128, 4, 128], F32, n=1)
            k.op("act", lambda e: e.activation(out=sza[:, :, :P], in_=v4(bz_a), func=AF.Sigmoid), [pbuf[bz_a]], [sza])
            k.op("act", lambda e: e.activation(out=szb[:, :, :P], in_=v4(bz_b), func=AF.Sigmoid), [pbuf[bz_b]], [szb])
            ma = sza
            mb = szb
            k.op("dve", lambda e: e.tensor_tensor(out=ma[:, :, :P], in0=sza[:, :, :P], in1=v4(bp_a), op=ALU.mult),
                 [pbuf[bp_a]], [ma])
            k.op("dve", lambda e: e.tensor_tensor(out=mb[:, :, :P], in0=szb[:, :, :P], in1=v4(bp_b), op=ALU.mult),
                 [pbuf[bp_b]], [mb])
            k.op("pool", lambda e: e.tensor_tensor(out=mT[:, half * 4:half * 4 + 4, :P], in0=ma[:, :, :P],
                                                   in1=mb[:, :, :P], op=ALU.add), [ma, mb], [mT])
        h1 = xt
        for half in range(2):
            bh = nb()
            mm_tok(bh, P, mT, w_o_bf, half * 512, 512)
            k.op("dve", lambda e: e.tensor_tensor(out=h1[:P, half * 512:(half + 1) * 512], in0=bank(bh)[:P, :],
                                                  in1=xt[:P, half * 512:(half + 1) * 512], op=ALU.add),
                 [pbuf[bh]], [h1])
        k.dma("sp", h1_d[r0:r0 + P, :], h1[:P], reads=[h1], writes=[h1_bufs[ti]])
        if debug:
            k.dma("sp", dbg_h1[r0:r0 + P, :], h1[:P], reads=[h1], writes=[dbg_buf])


    if p0 is not None:
        for _ in p0:
            pass
    k.phase_reset(MARK)
    load_gains([(1, 0), (2, 1)])
    h2_d = nc.dram_tensor("h2_scratch", [TT, D], F32, kind="Internal").ap()
    h2_bufs = [Buf("h2_%d" % i) for i in range(NT)]
    w_xq_bf = k.sb("w_xq_bf", [128, 8, D], BF16)
    w_xk_bf = k.sb("w_xk_bf", [128, 8, D], BF16)
    w_xv_bf = k.sb("w_xv_bf", [128, 8, D], BF16)
    w_xo_bf = k.sb("w_xo_bf", [128, 8, D], BF16)
    load_weight_bf(w_xk_bf, 0, w_xk, D)
    load_weight_bf(w_xv_bf, 0, w_xv, D)
    load_weight_bf(w_xq_bf, 0, w_xq, D)
    load_weight_bf(w_xo_bf, 0, w_xo, D)
    weights_ready()
    mnT = k.sb("mnT", [128, 8, 256], BF16)
    mkT = k.sb("mkT", [128, 8, 256], BF16)
    mv_bf = k.sb("mv_bf", [128, 2, D], BF16)
    for mc in range(2):
        mt = k.ring("xtB", [128, D], F32, n=3)
        k.dma("sp", mt[:], mem[mc * 128:(mc + 1) * 128, :], writes=[mt])
        mn = k.ring("xn", [128, D], BF16, n=1)
        rmsnorm_bf(mt, 128, 2, mn)
        transpose_to(mnT, mn, 128, 8, dcol0=mc * 128)
    for mc in range(2):
        for (wt, dst, ob, isv) in ((w_xk_bf, o_mk, out_bufs[4], False), (w_xv_bf, o_mv, out_bufs[5], True)):
            kv_f = k.ring("kv_f", [128, D], F32, n=2)
            for half in range(2):
                bb = nb()
                mm_tok(bb, 128, mnT, wt, half * 512, 512, tok0=mc * 128)
                k.op("act", lambda e: e.copy(out=kv_f[:, half * 512:(half + 1) * 512], in_=bank(bb)[:, :]),
                     [pbuf[bb]], [kv_f])
            k.dma("sp", dst[mc * 128:(mc + 1) * 128, :], kv_f[:], reads=[kv_f], writes=[ob])
            if isv:
                k.op("dve", lambda e: e.tensor_copy(out=mv_bf[:, mc, :], in_=kv_f[:]), [kv_f], [mv_bf])
    for j in range(8):
        if j % 2 == 0:
            bb = nb()
        for kc in range(8):
            k.op("pe", lambda e, kc=kc: e.matmul(bank(bb)[:, (j % 2) * 256:(j % 2) * 256 + 256],
                                                 lhsT=w_xk_bf[:, kc, j * 128:(j + 1) * 128], rhs=mnT[:, kc, :],
                                                 start=(kc == 0), stop=(kc == 7)), [w_xk_bf, mnT], [pbuf[bb]])
        if j % 2 == 1:
            k.op("act", lambda e: e.copy(out=mkT[:, j - 1:j + 1, :],
                                         in_=bank(bb).rearrange("p (a m) -> p a m", a=2)), [pbuf[bb]], [mkT])

    def softmax_rows(scv, deps, P, nring=1):
        mx = k.ring("sm_mx", [128, 4], F32)
        k.op("dve", lambda e: e.tensor_reduce(out=mx[:P], in_=scv, axis=AX.X, op=ALU.max), deps, [mx])
        sh = k.ring("sm_sh", [128, 4, 256], F32, n=1)
        k.op("dve", lambda e: e.tensor_tensor(out=sh[:P], in0=scv, in1=bc(mx[:P, :], [P, 4, 256], 2), op=ALU.subtract),
             deps + [mx], [sh])
        k.op("act", lambda e: e.activation(out=sh[:P], in_=sh[:P], func=AF.Exp), [], [sh])
        sm = k.ring("sm_sm", [128, 4], F32)
        k.op("dve", lambda e: e.tensor_reduce(out=sm[:P], in_=sh[:P], axis=AX.X, op=ALU.add), [sh], [sm])
        rs = k.ring("sm_rs", [128, 4], F32)
        k.op("dve", lambda e: e.reciprocal(out=rs[:P], in_=sm[:P]), [sm], [rs])
        pb = k.ring("sm_pb%d" % nring, [128, 4, 256], BF16, n=nring)
        k.op("dve", lambda e: e.tensor_tensor(out=pb[:P], in0=sh[:P], in1=bc(rs[:P, :], [P, 4, 256], 2), op=ALU.mult),
             [sh, rs], [pb])
        return pb

    def n2_gen():
        n2_all = k.sb("n2_all", [128, NEXP // 128], F32)
        for c in range(NEXP // 128):
            ub = k.ring("n2_u", [128, D], BF16, n=3)
            k.dma("pool", ub[:], uv_d[c * 128:(c + 1) * 128, 0:D], writes=[ub])
            nj = k.ring("n2_j", [128, D], BF16, n=1)
            k.op("act", lambda e: e.activation(out=nj[:], in_=ub[:], func=AF.Square, accum_out=n2_all[:, c:c + 1]),
                 [ub], [nj, n2_all])
            yield
        n2_dst = uv_d[:, 2 * D:2 * D + 2].bitcast(F32).rearrange("(c p) o -> p (c o)", p=128)
        k.dma("sp", n2_dst, n2_all[:], reads=[n2_all], writes=[uv_buf],
              fn=lambda e: e.dma_start(out=n2_dst, in_=n2_all[:], allow_slow_non_contiguous=True))
        yield

    p0box = [n2_gen()]
    p0_steps = -(-(NEXP // 128 + 1) // n_ptiles)

    def b_front(ti):
        sample = (ti == n_ptiles)
        P = PS if sample else 128
        r0 = ti * 128
        for _ in range(p0_steps):
            if p0box[0] is not None:
                try:
                    next(p0box[0])
                except StopIteration:
                    p0box[0] = None
        ht = k.ring("xtB", [128, D], F32, n=3)
        k.dma("sp", ht[:P], h1_d[r0:r0 + P, :], reads=[h1_bufs[ti]], writes=[ht])
        hn = k.ring("xn", [128, D], BF16, n=1)
        rmsnorm_bf(ht, P, 1, hn)
        hnT = k.ring("xnT", [128, 8, 128], BF16, n=1)
        transpose_to(hnT, hn, P, 8)
        qT = k.ring("xqT", [128, 8, 128], BF16, n=3)
        for half in range(2):
            bb = nb()
            for j in range(4):
                mm_feat(bb, j, P, w_xq_bf, (half * 4 + j) * 128, hnT)
            k.op("act", lambda e: e.mul(out=qT[:, half * 4:half * 4 + 4, :P],
                                        in_=bank(bb).rearrange("p (c t) -> p c t", c=4)[:, :, :P], mul=1.0 / 16),
                 [pbuf[bb]], [qT])
        return dict(ht=ht, qT=qT, P=P, r0=r0, sample=sample)

    def b_mid(ti, st):
        ht, qT, P, r0, sample = st["ht"], st["qT"], st["P"], st["r0"], st["sample"]
        if not sample:
            b0 = nb2()
            hold(b0), hold(b0 + 1)
            scv = pp[b0 // 2][:P, :].rearrange("p (h m) -> p h m", h=4)
            for h in range(4):
                for c in range(2):
                    k.op("pe", lambda e, h=h, c=c: e.matmul(scv[:, h, :], lhsT=qT[:, h * 2 + c, :P], rhs=mkT[:, h * 2 + c, :],
                                                            start=(c == 0), stop=(c == 1)), [qT, mkT], [pbuf[b0 + h // 2]])
            st["pb"] = softmax_rows(scv, [pbuf[b0], pbuf[b0 + 1]], P, nring=2)
            release(b0), release(b0 + 1)

    def b_back(ti, st):
        ht, qT, P, r0, sample = st["ht"], st["qT"], st["P"], st["r0"], st["sample"]
        oT = k.ring("xoT", [128, 8, 128], BF16, n=1)
        if not sample:
            pb = st["pb"]
            pT = k.ring("xpT", [128, 8, 128], BF16, n=1)
            transpose_to(pT, pb[:P].rearrange("p h m -> p (h m)"), P, 8, deps=[pb])
            for half in range(2):
                bb = nb()
                pv = bank(bb).rearrange("p (c t) -> p c t", c=4)
                for jj in range(4):
                    j = half * 4 + jj
                    h, c = j // 2, j % 2
                    for mc in range(2):
                        k.op("pe", lambda e, mc=mc: e.matmul(pv[:, jj, :P], lhsT=mv_bf[:, mc, h * 256 + c * 128:h * 256 + c * 128 + 128],
                                                             rhs=pT[:, h * 2 + mc, :P], start=(mc == 0), stop=(mc == 1)),
                             [mv_bf, pT], [pbuf[bb]])
                k.op("act", lambda e: e.copy(out=oT[:, half * 4:half * 4 + 4, :P], in_=pv[:, :, :P]), [pbuf[bb]], [oT])
        else:
            for b in range(SB_PER_CORE):
                Kb_bf = k.ring("Kb_bf", [128, 2, D], BF16, n=2)
                k.dma("pool", Kb_bf[:], ck_in[b].rearrange("(mc p) d -> p mc d", p=128), writes=[Kb_bf])
                KbT = k.ring("KbT", [128, 8, 256], BF16, n=1)
                for mc in range(2):
                    transpose_to(KbT, Kb_bf[:, mc, :], 128, 8, dcol0=mc * 128, deps=[Kb_bf])
                Vb_bf = k.ring("Vb_bf", [128, 2, D], BF16, n=2)
                k.dma("pool", Vb_bf[:], cv_in[b].rearrange("(mc p) d -> p mc d", p=128), writes=[Vb_bf])
                b0 = nb2()
                hold(b0), hold(b0 + 1)
                scv = pp[b0 // 2][:DEC_S, :].rearrange("p (h m) -> p h m", h=4)
                for h in range(4):
                    for c in range(2):
                        k.op("pe", lambda e, h=h, c=c: e.matmul(scv[:, h, :], lhsT=qT[:, h * 2 + c, b * DEC_S:(b + 1) * DEC_S],
                                                                rhs=KbT[:, h * 2 + c, :], start=(c == 0), stop=(c == 1)),
                             [qT, KbT], [pbuf[b0 + h // 2]])
                pb = softmax_rows(scv, [pbuf[b0], pbuf[b0 + 1]], DEC_S)
                release(b0), release(b0 + 1)
                pTb = k.ring("pTb", [128, 8, DEC_S], BF16, n=1)
                bt = nb()
                ptv = bank(bt).bitcast(BF16).rearrange("p (c t) -> p c t", c=8)
                for j in range(8):
                    k.op("pe", lambda e, j=j: e.transpose(out=ptv[:, j, :DEC_S], in_=pb[:DEC_S, j // 2, (j % 2) * 128:(j % 2) * 128 + 128],
                                                          identity=ident[:DEC_S, :DEC_S]), [pb, ident], [pbuf[bt]])
                k.op("act", lambda e: e.copy(out=pTb[:], in_=ptv[:, :, :DEC_S]), [pbuf[bt]], [pTb])
                for half in range(2):
                    bb = nb()
                    pv = bank(bb).rearrange("p (c t) -> p c t", c=4)
                    for jj in range(4):
                        j = half * 4 + jj
                        h, c = j // 2, j % 2
                        for mc in range(2):
                            k.op("pe", lambda e, mc=mc: e.matmul(pv[:, jj, :DEC_S],
                                                                 lhsT=Vb_bf[:, mc, h * 256 + c * 128:h * 256 + c * 128 + 128],
                                                                 rhs=pTb[:, h * 2 + mc, :], start=(mc == 0), stop=(mc == 1)),
                                 [Vb_bf, pTb], [pbuf[bb]])
                    k.op("act", lambda e: e.copy(out=oT[:, half * 4:half * 4 + 4, b * DEC_S:(b + 1) * DEC_S],
                                                 in_=pv[:, :, :DEC_S]), [pbuf[bb]], [oT])
        for half in range(2):
            bh = nb()
            mm_tok(bh, P, oT, w_xo_bf, half * 512, 512)
            k.op("dve", lambda e: e.tensor_tensor(out=ht[:P, half * 512:(half + 1) * 512], in0=bank(bh)[:P, :],
                                                  in1=ht[:P, half * 512:(half + 1) * 512], op=ALU.add),
                 [pbuf[bh]], [ht])
        k.dma("sp", h2_d[r0:r0 + P, :], ht[:P], reads=[ht], writes=[h2_bufs[ti]])
        if debug:
            k.dma("sp", dbg_h2[r0:r0 + P, :], ht[:P], reads=[ht], writes=[dbg_buf])


    sts = {0: b_front(0)}
    if NT > 1:
        sts[1] = b_front(1)
    b_mid(0, sts[0])
    for ti in range(NT):
        if ti + 2 < NT:
            sts[ti + 2] = b_front(ti + 2)
        if ti + 1 < NT:
            b_mid(ti + 1, sts[ti + 1])
        b_back(ti, sts.pop(ti))
    p0 = p0box[0]
    if p0 is not None:
        for _ in p0:
            pass
    k.phase_reset(MARK)
    load_gains([(3, 0), (4, 1)])
    wq_bf = k.sb("wq_bf", [128, 8, 2048], BF16)
    load_weight_bf(wq_bf, 0, peer_wq, 2048)
    weights_ready()
    skT = k.sb("skT", [128, 16, 128], BF16)
    iota16 = k.sb("iota16", [128, 16], F32)
    k.dma("sp", iota16[:], c_iota16, writes=[iota16])
    identsel = k.sb("identsel", [128, 128, 128], BF16)
    k.dma("sp", identsel[:].rearrange("p a b -> p (a b)"), c_identsel, writes=[identsel])
    MARK_C = k.mark()
    sk_f = k.sb("sk_f", [128, 16, 128], F32)
    k.dma("sp", sk_f[:], peer_sk.rearrange("a k d -> k a d"), writes=[sk_f])
    sk_bf = k.sb("sk_bf", [128, 16 * 128], BF16)
    k.op("dve", lambda e: e.tensor_copy(out=sk_bf[:], in_=sk_f[:].rearrange("p a d -> p (a d)")), [sk_f], [sk_bf])
    transpose_to(skT, sk_bf, 128, 16)
    k.phase_reset(MARK_C)

    ones_f = k.sb("ones_f", [128, 128], F32)
    k.op("dve", lambda e: e.memset(ones_f[:], 1.0), [], [ones_f])

    def routing(ti, res):
        sample = (ti == n_ptiles)
        P = PS if sample else 128
        r0 = ti * 128
        ht = k.ring("xt", [128, D], F32, n=2)
        k.dma("sp", ht[:P], h2_d[r0:r0 + P, :], reads=[h2_bufs[ti]], writes=[ht])
        fn_bf = k.ring("xn", [128, D], BF16, n=2)
        rmsnorm_bf(ht, P, 3, fn_bf)
        yield
        fnT = k.ring("xnT", [128, 8, 128], BF16, n=1)
        transpose_to(fnT, fn_bf, P, 8)
        yield
        xj = k.ring("n2_j", [128, D], BF16, n=1)
        xsq = k.ring("xsq", [128, 1], F32, n=1)
        k.op("act", lambda e: e.activation(out=xj[:P], in_=fn_bf[:P], func=AF.Square, accum_out=xsq[:P]), [fn_bf], [xj, xsq])
        dg = k.ring("xsq_dg", [128, 128], F32, n=1)
        k.op("dve", lambda e: e.tensor_scalar(out=dg[:P, :P], in0=ident_f[:P, :P], scalar1=xsq[:P, 0:1], scalar2=None,
                                              op0=ALU.mult), [ident_f, xsq], [dg])
        bx = nb()
        k.op("pe", lambda e: e.matmul(bank(bx)[:, 0:P], lhsT=ones_f[:P, :], rhs=dg[:P, :P], start=True, stop=True),
             [ones_f, dg], [pbuf[bx]])
        xsqB = k.ring("xsqB", [128, 128], F32, n=2)
        k.op("act", lambda e: e.copy(out=xsqB[:, :P], in_=bank(bx)[:, 0:P]), [pbuf[bx]], [xsqB])
        yield
        qpT = k.ring("qpT", [128, 16, 128], BF16, n=1)
        for g in range(4):
            bb = nb()
            for j in range(4):
                mm_feat(bb, j, P, wq_bf, (g * 4 + j) * 128, fnT)
            k.op("act", lambda e: e.copy(out=qpT[:, g * 4:g * 4 + 4, :P],
                                         in_=bank(bb).rearrange("p (c t) -> p c t", c=4)[:, :, :P]), [pbuf[bb]], [qpT])
            yield
        s_sb = k.ring("s_sb", [128, 16, 128], F32, n=1)
        for g in range(4):
            bb = nb()
            for j in range(4):
                hj = g * 4 + j
                k.op("pe", lambda e, j=j, hj=hj: e.matmul(bank(bb)[:P, j * 128:(j + 1) * 128], lhsT=qpT[:, hj, :P],
                                                          rhs=skT[:, hj, :], start=True, stop=True), [qpT, skT], [pbuf[bb]])
            k.op("act", lambda e: e.copy(out=s_sb[:P, g * 4:g * 4 + 4, :],
                                         in_=bank(bb).rearrange("p (c t) -> p c t", c=4)[:P]), [pbuf[bb]], [s_sb])
            yield
        v16 = k.ring("v16", [128, 16, 16], F32, n=1)
        i16 = k.ring("i16", [128, 16, 16], U32, n=1)
        for hj in range(16):
            for r in range(2):
                k.op("dve", lambda e: e.max(out=v16[:P, hj, r * 8:r * 8 + 8], in_=s_sb[:P, hj, :]), [s_sb], [v16])
                yield
                k.op("dve", lambda e: e.max_index(out=i16[:P, hj, r * 8:r * 8 + 8], in_max=v16[:P, hj, r * 8:r * 8 + 8],
                                                  in_values=s_sb[:P, hj, :]), [s_sb, v16], [i16])
                yield
                if r == 0:
                    k.op("dve", lambda e: e.match_replace(out=s_sb[:P, hj, :], in_to_replace=v16[:P, hj, 0:8],
                                                          in_values=s_sb[:P, hj, :], imm_value=-1e30), [v16], [s_sb])
                    yield
        i16f = k.ring("i16f", [128, 8, 2, 16], F32, n=1)
        k.op("dve", lambda e: e.tensor_copy(out=i16f[:P].rearrange("p h j k -> p (h j) k"), in_=i16[:P]), [i16], [i16f])
        v4 = v16[:P].rearrange("p (h j) k -> p h j k", j=2)
        cand = k.ring("cand", [128, 8, 16, 16], F32, n=1)
        k.op("dve", lambda e: e.tensor_tensor(out=cand[:P], in0=bc(v4[:, :, 0, :], [P, 8, 16, 16], 3),
                                              in1=bc(v4[:, :, 1, :], [P, 8, 16, 16], 2), op=ALU.add), [v16], [cand])
        yield
        tv = k.ring("tv", [128, 8, 16], F32, n=1)
        tp = k.ring("tp", [128, 8, 16], U32, n=1)
        for h in range(8):
            cf = cand[:P, h].rearrange("p a b -> p (a b)")
            for r in range(2):
                k.op("dve", lambda e: e.max(out=tv[:P, h, r * 8:r * 8 + 8], in_=cf), [cand], [tv])
                yield
                k.op("dve", lambda e: e.max_index(out=tp[:P, h, r * 8:r * 8 + 8], in_max=tv[:P, h, r * 8:r * 8 + 8],
                                                  in_values=cf), [cand, tv], [tp])
                yield
                if r == 0:
                    k.op("dve", lambda e: e.match_replace(out=cf, in_to_replace=tv[:P, h, 0:8], in_values=cf,
                                                          imm_value=-1e30), [tv], [cand])
                    yield
        ta = k.ring("ta", [128, 8, 16], U32, n=1)
        tb = k.ring("tb", [128, 8, 16], U32, n=1)
        k.op("dve", lambda e: e.tensor_single_scalar(out=ta[:P], in_=tp[:P], scalar=4, op=ALU.logical_shift_right), [tp], [ta])
        k.op("dve", lambda e: e.tensor_single_scalar(out=tb[:P], in_=tp[:P], scalar=15, op=ALU.bitwise_and), [tp], [tb])
        taf = k.ring("taf", [128, 8, 16], F32, n=1)
        tbf = k.ring("tbf", [128, 8, 16], F32, n=1)
        k.op("dve", lambda e: e.tensor_copy(out=taf[:P], in_=ta[:P]), [ta], [taf])
        k.op("dve", lambda e: e.tensor_copy(out=tbf[:P], in_=tb[:P]), [tb], [tbf])
        yield
        idxf = k.ring("idxf", [128, 8, 16], F32, n=1)
        sel = {}
        for nm, tf, jj in (("a", taf, 0), ("b", tbf, 1)):
            oh = k.ring("cand", [128, 8, 16, 16], F32, n=1)
            io = iota16[:P, :].unsqueeze(1).unsqueeze(1).broadcast_to([P, 8, 16, 16])
            k.op("dve", lambda e: e.tensor_tensor(out=oh[:P], in0=io, in1=bc(tf[:P], [P, 8, 16, 16], 3), op=ALU.is_equal),
                 [iota16, tf], [oh])
            yield
            k.op("dve", lambda e: e.tensor_tensor(out=oh[:P], in0=oh[:P], in1=bc(i16f[:P, :, jj, :], [P, 8, 16, 16], 2),
                                                  op=ALU.mult), [i16f], [oh])
            yield
            sl = k.ring("sel_" + nm, [128, 8, 16], F32, n=1)
            k.op("dve", lambda e: e.tensor_reduce(out=sl[:P], in_=oh[:P], axis=AX.X, op=ALU.add), [oh], [sl])
            sel[nm] = sl
            yield
        k.op("dve", lambda e: e.scalar_tensor_tensor(out=idxf[:P], in0=sel["a"][:P], scalar=128.0, in1=sel["b"][:P],
                                                     op0=ALU.mult, op1=ALU.add), [sel["a"], sel["b"]], [idxf])
        gsh = k.ring("gsh", [128, 8, 16], F32, n=1)
        k.op("dve", lambda e: e.tensor_tensor(out=gsh[:P], in0=tv[:P], in1=bc(tv[:P, :, 0], [P, 8, 16], 2), op=ALU.subtract),
             [tv], [gsh])
        k.op("act", lambda e: e.activation(out=gsh[:P], in_=gsh[:P], func=AF.Exp), [], [gsh])
        gsm = k.ring("gsm", [128, 8], F32, n=1)
        k.op("dve", lambda e: e.tensor_reduce(out=gsm[:P], in_=gsh[:P], axis=AX.X, op=ALU.add), [gsh], [gsm])
        grs = k.ring("grs", [128, 8], F32, n=1)
        k.op("dve", lambda e: e.reciprocal(out=grs[:P], in_=gsm[:P]), [gsm], [grs])
        gates = k.ring("gates", [128, 8, 16], F32, n=1)
        k.op("dve", lambda e: e.tensor_tensor(out=gates[:P], in0=gsh[:P], in1=bc(grs[:P, :], [P, 8, 16], 2), op=ALU.mult),
             [gsh, grs], [gates])
        yield
        if debug:
            dump("idxf", idxf[:P].rearrange("p h k -> p (h k)"), idxf, ti)
            dump("gates", gates[:P].rearrange("p h k -> p (h k)"), gates, ti)
        bt = nb()
        k.op("pe", lambda e: e.matmul(bank(bt)[:, 0:P], lhsT=idxf[:P].rearrange("p h k -> p (h k)"),
                                      rhs=ident_f[:P, :P], start=True, stop=True), [idxf, ident_f], [pbuf[bt]])
        k.op("pe", lambda e: e.matmul(bank(bt)[:, 128:128 + P], lhsT=gates[:P].rearrange("p h k -> p (h k)"),
                                      rhs=ident_f[:P, :P], start=True, stop=True), [gates, ident_f], [pbuf[bt]])
        idxT = k.ring("idxT", [128, 128], I32, n=2)
        gT = k.ring("gT", [128, 128], F32, n=2)
        idxTf = k.ring("idxTf", [128, 128], F32, n=1)
        k.op("act", lambda e: e.copy(out=idxTf[:, :P], in_=bank(bt)[:, 0:P]), [pbuf[bt]], [idxTf])
        k.op("act", lambda e: e.copy(out=gT[:, :P], in_=bank(bt)[:, 128:128 + P]), [pbuf[bt]], [gT])
        k.op("dve", lambda e: e.tensor_copy(out=idxT[:, :P], in_=idxTf[:, :P]), [idxTf], [idxT])
        res.update(ht=ht, fn_bf=fn_bf, idxT=idxT, gT=gT, P=P, r0=r0, xsqB=xsqB)
        yield

    def drain(gen):
        if gen is not None:
            for _ in gen:
                pass

    GA = 4
    results = [dict() for _ in range(NT + 1)]
    drain(routing(0, results[0]))
    for ti in range(NT):
        R = results[ti]
        ht, fn_bf, idxT, gT, P, r0, xsqB = R["ht"], R["fn_bf"], R["idxT"], R["gT"], R["P"], R["r0"], R["xsqB"]
        nxt = routing(ti + 1, results[ti + 1]) if ti + 1 < NT else None
        if debug:
            dump("idxT", idxT[:, :P], idxT, ti)
        bo0 = nb2()
        hold(bo0), hold(bo0 + 1)
        UVs, pairs, acols, gcols = {}, {}, {}, {}
        for i in range(-GA, P):
            tg = i + GA
            if 0 <= tg < P:
                UV = k.ring("UVg", [128, UVW], BF16, n=GA + 2)
                UVs[tg] = UV
                k.dma("pool", None, None, reads=[idxT, uv_buf], writes=[UV], fn=lambda e: e.indirect_dma_start(
                    out=UV[:], out_offset=None, in_=uv_d,
                    in_offset=bass.IndirectOffsetOnAxis(ap=idxT[:, tg:tg + 1], axis=0)))
            tb_ = i + 2
            if 0 <= tb_ < P:
                b0 = nb2()
                hold(b0), hold(b0 + 1)
                pairs[tb_] = b0
                UVb = UVs[tb_]
                for half in range(2):
                    k.op("pe", lambda e: e.matmul(bank(b0 + half)[:, :], lhsT=ident[:P, tb_:tb_ + 1].broadcast_to([P, 128]),
                                                  rhs=fn_bf[:P, half * 512:(half + 1) * 512], start=True, stop=False),
                         [ident, fn_bf], [pbuf[b0 + half]])
                    k.op("pe", lambda e: e.matmul(bank(b0 + half)[:, :], lhsT=ident[:, :],
                                                  rhs=UVb[:, half * 512:(half + 1) * 512], start=False, stop=True),
                         [ident, UVb], [pbuf[b0 + half]])
            t = i
            Wsel = None
            if 0 <= t < P:
                gcol = gcols.pop(t)
                wcol = k.ring("wcol", [128, 1], F32, n=4)
                k.op("dve", lambda e: e.tensor_tensor(out=wcol[:], in0=gcol[:], in1=gT[:, t:t + 1], op=ALU.mult),
                     [gcol, gT], [wcol])
                Wsel = k.ring("Wsel", [128, 128], BF16, n=4)
                k.op("dve", lambda e: e.tensor_scalar(out=Wsel[:, :P], in0=identsel[:, t, :P], scalar1=wcol[:, 0:1],
                                                      scalar2=None, op0=ALU.mult), [identsel, wcol], [Wsel])
            td = i + 1
            if 0 <= td < P:
                b0 = pairs.pop(td)
                UVd = UVs[td]
                negh = k.ring("negh", [128, 1], F32, n=4)
                k.op("dve", lambda e: e.tensor_scalar(out=negh[:], in0=UVd[:, 2 * D:2 * D + 2].bitcast(F32),
                                                      scalar1=xsqB[:, td:td + 1], scalar2=-0.5, op0=ALU.add, op1=ALU.mult),
                     [UVd, xsqB], [negh])
                junk = k.ring("pjunk", [128, D], BF16, n=2)
                acol = k.ring("acol", [128, 1], F32, n=4)
                k.op("act", lambda e: e.activation(out=junk[:], in_=pp[b0 // 2][:, :], func=AF.Square, accum_out=acol[:, 0:1]),
                     [pbuf[b0], pbuf[b0 + 1]], [junk, acol])
                release(b0), release(b0 + 1)
                gcol = k.ring("gcol", [128, 1], F32, n=4)
                k.op("act", lambda e: e.activation(out=gcol[:], in_=acol[:], func=AF.Gelu, scale=0.5, bias=negh[:, 0:1]),
                     [acol, negh], [gcol])
                gcols[td] = gcol
            if 0 <= t < P:
                UV = UVs.pop(t)
                for half in range(2):
                    k.op("pe", lambda e: e.matmul(bank(bo0 + half)[:P, :], lhsT=Wsel[:, :P],
                                                  rhs=UV[:, D + half * 512:D + (half + 1) * 512],
                                                  start=(t == 0), stop=(t == P - 1)), [Wsel, UV], [pbuf[bo0 + half]])
                for _ in range(2 if (t % 3 == 0 or P < 128) else 1):
                    if nxt is not None:
                        try:
                            next(nxt)
                        except StopIteration:
                            nxt = None
        drain(nxt)
        k.op("dve", lambda e: e.tensor_tensor(out=ht[:P], in0=pp[bo0 // 2][:P, :], in1=ht[:P], op=ALU.add),
             [pbuf[bo0], pbuf[bo0 + 1]], [ht])
        release(bo0), release(bo0 + 1)
        if debug:
            k.dma("sp", dbg_h3[r0:r0 + P, :], ht[:P], reads=[ht], writes=[dbg_buf])
        yt = k.ring("yt", [128, D], F32, n=1)
        rmsnorm_bf(ht, P, 4, None, out_f=yt)
        k.dma("sp", y_all[r0:r0 + P, :], yt[:P], reads=[yt], writes=[out_bufs[0]])

    k.finish(out_bufs + ([dbg_buf] if debug else []) + list(dumps.values()))
    return nc


def host_consts(n_ptiles=16):
    TP = n_ptiles * 128
    TT = TP + PS
    c = {}
    c["c_ident"] = np.eye(128, dtype=np.float32)
    tri = np.zeros((2, 128, 128), np.float32)
    utri = np.zeros((2, 128, 128), np.float32)
    s = np.arange(128)[:, None]
    t = np.arange(128)[None, :]
    tri[0] = (s <= t)
    utri[0] = (s > t)
    same = (s // DEC_S == t // DEC_S) & (s < PS) & (t < PS)
    tri[1] = (s <= t) & same
    utri[1] = (s > t) & same
    c["c_tri"], c["c_utri"] = tri, utri
    si = np.zeros((2, 128, 16), np.float32)
    si[0, :, 0] = 1.0
    for b in range(SB_PER_CORE):
        si[1, b * DEC_S:(b + 1) * DEC_S, b] = 1.0
    c["c_seqind"] = si
    siT = np.zeros((16, 64), np.float32)
    for b in range(SB_PER_CORE):
        siT[b, b * DEC_S:(b + 1) * DEC_S] = 1.0
    c["c_seqindT"] = np.ascontiguousarray(np.broadcast_to(siT.reshape(1, -1), (64, 16 * 64))).astype(ml_dtypes.bfloat16)
    half = 32
    inv = (10000.0 ** (-np.arange(half, dtype=np.float32) / half)).astype(np.float32)
    pos = np.concatenate([np.arange(TP), PAST + (np.arange(PS) % DEC_S)]).astype(np.float32)
    tl = np.concatenate([np.arange(TP) % 128, np.arange(PS) % DEC_S]).astype(np.float64)
    L = np.concatenate([np.full(TP, 128.0), np.full(PS, float(DEC_S))])
    ang = (pos[:, None] * inv[None, :]).astype(np.float32)
    cos = np.cos(ang).astype(np.float64)
    sin = np.sin(ang).astype(np.float64)
    lg = np.log1p(-(2.0 ** (-5.0 - np.arange(4, dtype=np.float64))))
    fq = np.exp(lg[None, :] * (tl[:, None] + 1.0))
    fk = np.exp(-lg[None, :] * (tl[:, None] + 1.0)) * 0.125
    fkd = np.exp(lg[None, :] * (L[:, None] - 1.0 - tl[:, None])) * 0.125
    rot = np.zeros((TT, 6, 4, 32), np.float64)
    for i, f in enumerate((fq, fk, fkd)):
        rot[:, 2 * i] = cos[:, None, :] * f[:, :, None]
        rot[:, 2 * i + 1] = sin[:, None, :] * f[:, :, None]
    c["c_rot"] = rot.reshape(TT, 768).astype(np.float32)
    dr = np.zeros((2, 64, 4), np.float64)
    dr[0] = np.exp(lg * 128.0)[None, :]
    dr[1] = np.exp(lg * float(DEC_S))[None, :]
    c["c_decret"] = dr.astype(np.float32)
    c["c_iota16"] = np.ascontiguousarray(np.broadcast_to(np.arange(16, dtype=np.float32)[None, :], (128, 16)))
    c["c_identsel"] = np.ascontiguousarray(np.broadcast_to(np.eye(128, dtype=np.float32).reshape(1, -1), (128, 128 * 128))).astype(ml_dtypes.bfloat16)
    return c


def make_in_maps(inp, n_ptiles=16, cores=8):
    f = lambda a: np.ascontiguousarray(np.asarray(a, dtype=np.float32))
    consts = host_consts(n_ptiles)
    TP = n_ptiles * 128
    shared = {
        "w_in": f(inp["w_in"][0]), "w_a2": f(inp["w_a2"][0]), "b_a": f(inp["b_a"][0]).reshape(1, 256),
        "gains": f(np.stack([inp["norm_mix"][0], inp["norm_xattn"][0], inp["norm_mem"][0], inp["norm_ffn"][0],
                             inp["norm_final"]])),
        "hgains": f(np.stack([inp["gla_head_norm"][0].reshape(512), inp["ret_head_norm"][0].reshape(512)])),
        "w_pa": f(inp["w_pa"][0]), "w_pb": f(inp["w_pb"][0]), "w_o": f(inp["w_o"][0]),
        "w_xq": f(inp["w_xq"][0]), "w_xk": f(inp["w_xk"][0]), "w_xv": f(inp["w_xv"][0]), "w_xo": f(inp["w_xo"][0]),
        "peer_wq": f(inp["peer_wq"][0]), "peer_sk": f(inp["peer_subkeys"][0]).reshape(16, 128, 128),
        "peer_u": f(inp["peer_u"][0]), "peer_v": f(inp["peer_v"][0]),
    }
    shared.update(consts)
    maps = []
    for c in range(cores):
        sb = slice(c * SB_PER_CORE, (c + 1) * SB_PER_CORE)
        m = dict(shared)
        m["x_all"] = f(np.concatenate([np.asarray(inp["x_prompt"][c])[:TP], np.asarray(inp["x_sample"][sb]).reshape(PS, D)], 0))
        m["mem"] = f(inp["mem_prompt"][c])
        m["st_gla"] = f(inp["state_gla"][0][sb])
        m["st_ret"] = f(inp["state_ret"][0][sb])
        m["ck"] = f(np.asarray(inp["cache_mem_k"][0][sb]).reshape(SB_PER_CORE, NMEM, D))
        m["cv"] = f(np.asarray(inp["cache_mem_v"][0][sb]).reshape(SB_PER_CORE, NMEM, D))
        maps.append(m)
    return maps


_NC_CACHE = {}


def kernel(**inp):
    if "nc" not in _NC_CACHE:
        _NC_CACHE["nc"] = build()
    nc = _NC_CACHE["nc"]
    maps = make_in_maps(inp)
    res = run_bass_kernel_spmd(nc, maps, core_ids=list(range(8)))
    R = res.results
    y_p = np.stack([r["y_all"][:SEQ] for r in R], 0)
    y_s = np.concatenate([r["y_all"][SEQ:].reshape(SB_PER_CORE, DEC_S, D) for r in R], 0)
    gla_p = np.stack([r["o_gla_p"] for r in R], 0)[None]
    ret_p = np.stack([r["o_ret_p"] for r in R], 0)[None]
    mk = np.stack([r["o_mk"].reshape(NMEM, 4, 256) for r in R], 0)[None]
    mv = np.stack([r["o_mv"].reshape(NMEM, 4, 256) for r in R], 0)[None]
    gla_s = np.concatenate([r["o_gla_s"] for r in R], 0)[None]
    ret_s = np.concatenate([r["o_ret_s"] for r in R], 0)[None]
    return tuple(np.ascontiguousarray(a, dtype=np.float32) for a in (y_p, y_s, gla_p, ret_p, mk, mv, gla_s, ret_s))
```

```python
import numpy as np
import ml_dtypes
import concourse.bass as bass
import concourse.mybir as mybir
from concourse.bass_utils import run_bass_kernel_spmd

F32 = mybir.dt.float32
BF16 = mybir.dt.bfloat16
I32 = mybir.dt.int32
U32 = mybir.dt.uint32
AF = mybir.ActivationFunctionType
ALU = mybir.AluOpType
AX = mybir.AxisListType

D = 1024
SEQ = 2048
NB = 8
DEC_B = 128
DEC_S = 4
PAST = 16384
INW = 5136
NMEM = 256
EPS = 1e-6
NEXP = 16384
SB_PER_CORE = DEC_B // 8
PS = SB_PER_CORE * DEC_S
C_GQ, C_GK, C_GV, C_GLR, C_GR, C_RQ, C_RK, C_RV, C_RG, C_ZA, C_ZB = (
    0, 256, 512, 1024, 1040, 1552, 1808, 2064, 2576, 3088, 4112)


class Buf:
    __slots__ = ("w", "r", "name")

    def __init__(self, name=""):
        self.w = None
        self.r = {}
        self.name = name


class T:
    def __init__(self, t, buf):
        self.t = t
        self.b = buf

    def __getitem__(self, k):
        return self.t[k]


class K:
    def __init__(self, nc):
        self.nc = nc
        self.E = {"pe": nc.tensor, "act": nc.scalar, "dve": nc.vector, "pool": nc.gpsimd, "sp": nc.sync}
        self.sems = {e: nc.alloc_semaphore("s_" + e) for e in self.E}
        self.cnt = {e: 0 for e in self.E}
        self.seen = {e: {} for e in self.E}
        self.slots = {}
        for q, n in (("sp", 12), ("pool", 8), ("act", 4)):
            self.slots[q] = []
            for i in range(n):
                nm = "d_%s%d" % (q, i)
                self.sems[nm] = nc.alloc_semaphore(nm)
                self.slots[q].append([nm, 0])
        self.slot_i = {q: 0 for q in self.slots}
        self.ncnt = 0
        self.rot = {}

    ARENA_WORDS = 47104

    def sb(self, name, shape, dt):
        if not hasattr(self, "arena"):
            self.arena = self.nc.alloc_sbuf_tensor("arena", [128, self.ARENA_WORDS], F32)
            self.top = 0
        n = 1
        for x in shape[1:]:
            n *= x
        esz = 2 if dt == BF16 else 4
        words = (n * esz + 3) // 4
        words = (words + 7) // 8 * 8
        assert self.top + words <= self.ARENA_WORDS, "arena overflow at %s (%d + %d)" % (name, self.top, words)
        ap = self.arena[0:shape[0], self.top:self.top + words]
        self.top += words
        if dt != F32:
            ap = ap.bitcast(dt)
        ap = ap[:, 0:n]
        if len(shape) > 2:
            names = " ".join("d%d" % i for i in range(len(shape) - 1))
            kw = {"d%d" % i: shape[i + 1] for i in range(len(shape) - 1)}
            ap = ap.rearrange("p (%s) -> p %s" % (names, names), **kw)
        return T(ap, Buf(name))

    def mark(self):
        return self.top

    def phase_reset(self, mark):
        tags = {}
        for e in self.E:
            if self.cnt[e] > 0:
                tags[e] = self.cnt[e]
        for q in self.slots:
            for nm, v in self.slots[q]:
                if v > 0:
                    tags[nm] = v
        for e in self.E:
            self._wait(e, {s_: v for s_, v in tags.items() if s_ != e})
        self.top = mark
        self.rot = {}

    def ring(self, name, shape, dt, n=2):
        if name not in self.rot:
            self.rot[name] = [[self.sb("%s_%d" % (name, i), shape, dt) for i in range(n)], 0]
        lst = self.rot[name]
        t = lst[0][lst[1] % len(lst[0])]
        lst[1] += 1
        return t

    def _wait(self, eng, tags):
        e = self.E[eng]
        for s, v in tags.items():
            if eng == "pe" and s == "pe":
                continue
            if s == eng and eng == "sp":
                continue
            if self.seen[eng].get(s, 0) < v:
                e.wait_ge(self.sems[s], v)
                self.seen[eng][s] = v

    @staticmethod
    def _need(reads, writes):
        tags = {}

        def upd(tag):
            if tag is None:
                return
            s, v = tag
            if tags.get(s, 0) < v:
                tags[s] = v

        for b in reads:
            upd(b.w)
        for b in writes:
            upd(b.w)
            for s, v in b.r.items():
                upd((s, v))
        return tags

    @staticmethod
    def _bufs(xs):
        out = []
        for x in xs:
            if isinstance(x, T):
                out.append(x.b)
            elif isinstance(x, Buf):
                out.append(x)
            elif isinstance(x, (list, tuple)):
                out.extend(K._bufs(x))
            elif x is None:
                pass
            else:
                raise TypeError(type(x))
        return out

    def _done(self, tag, reads, writes):
        s, v = tag
        for b in writes:
            b.w = tag
            b.r = {}
        for b in reads:
            if b in writes:
                continue
            if b.r.get(s, 0) < v:
                b.r[s] = v

    def op(self, eng, fn, reads=(), writes=()):
        reads = self._bufs(reads)
        writes = self._bufs(writes)
        self._wait(eng, self._need(reads, writes))
        ins = fn(self.E[eng])
        self.cnt[eng] += 1
        ins.then_inc(self.sems[eng], 1)
        self._done((eng, self.cnt[eng]), reads, writes)
        self.ncnt += 1
        return ins

    def dma(self, q, out, in_, reads=(), writes=(), fn=None):
        reads = self._bufs(reads)
        writes = self._bufs(writes)
        need = self._need(reads, writes)
        i = self.slot_i[q]
        self.slot_i[q] = (i + 1) % len(self.slots[q])
        slot = self.slots[q][i]
        if slot[1] > 0:
            if need.get(slot[0], 0) < slot[1]:
                need[slot[0]] = slot[1]
        self._wait(q, need)
        if fn is None:
            ins = self.E[q].dma_start(out=out, in_=in_)
        else:
            ins = fn(self.E[q])
        slot[1] += 16
        ins.then_inc(self.sems[slot[0]], 16)
        self._done((slot[0], slot[1]), reads, writes)
        self.ncnt += 1
        return ins

    def finish(self, bufs):
        self._wait("sp", self._need(self._bufs(bufs), []))


def bc(ap, shape, axis):
    return ap.unsqueeze(axis).broadcast_to(list(shape))


def build(n_ptiles=16, debug=False, peer_mode=1, cut=99):
    nc = bass.Bass("TRN2", target_bir_lowering=False)
    k = K(nc)
    NT = n_ptiles + 1
    TP = n_ptiles * 128
    TT = TP + PS

    def din(name, shape, dt=F32):
        return nc.dram_tensor(name, list(shape), dt, kind="ExternalInput").ap()

    def dout(name, shape, dt=F32):
        return nc.dram_tensor(name, list(shape), dt, kind="ExternalOutput").ap()

    x_all = din("x_all", [TT, D])
    mem = din("mem", [NMEM, D])
    st_gla = din("st_gla", [SB_PER_CORE, 4, 64, 128])
    st_ret = din("st_ret", [SB_PER_CORE, 4, 64, 128])
    ck_in = din("ck", [SB_PER_CORE, NMEM, D])
    cv_in = din("cv", [SB_PER_CORE, NMEM, D])
    w_in = din("w_in", [D, INW])
    w_a2 = din("w_a2", [16, 256])
    b_a = din("b_a", [1, 256])
    gains = din("gains", [5, D])
    hgains = din("hgains", [2, 512])
    w_pa = din("w_pa", [512, D])
    w_pb = din("w_pb", [512, D])
    w_o = din("w_o", [D, D])
    w_xq = din("w_xq", [D, D])
    w_xk = din("w_xk", [D, D])
    w_xv = din("w_xv", [D, D])
    w_xo = din("w_xo", [D, D])
    peer_wq = din("peer_wq", [D, 2048])
    peer_sk = din("peer_sk", [16, 128, 128])
    peer_u = din("peer_u", [NEXP, D])
    peer_v = din("peer_v", [NEXP, D])
    c_ident = din("c_ident", [128, 128])
    c_tri = din("c_tri", [2, 128, 128])
    c_utri = din("c_utri", [2, 128, 128])
    c_seqind = din("c_seqind", [2, 128, 16])
    c_seqindT = din("c_seqindT", [64, 16 * 64], BF16)
    c_rot = din("c_rot", [TT, 6 * 128])
    c_decret = din("c_decret", [2, 64, 4])
    c_iota16 = din("c_iota16", [128, 16])
    c_identsel = din("c_identsel", [128, 128 * 128], BF16)

    y_all = dout("y_all", [TT, D])
    o_gla_p = dout("o_gla_p", [4, 64, 128])
    o_ret_p = dout("o_ret_p", [4, 64, 128])
    o_mk = dout("o_mk", [NMEM, D])
    o_mv = dout("o_mv", [NMEM, D])
    o_gla_s = dout("o_gla_s", [SB_PER_CORE, 4, 64, 128])
    o_ret_s = dout("o_ret_s", [SB_PER_CORE, 4, 64, 128])
    out_bufs = [Buf("o%d" % i) for i in range(8)]
    if debug:
        dbg_h1 = dout("dbg_h1", [TT, D])
        dbg_h2 = dout("dbg_h2", [TT, D])
        dbg_h3 = dout("dbg_h3", [TT, D])
        dbg_buf = Buf("dbg")

    dumps = {}

    def dump(name, ap, dep, ti_=0, only=0):
        if not debug or ti_ != only or name in dumps:
            return
        shp = list(ap.shape)
        dtn = dout("dd_" + name, shp, ap.dtype)
        dumps[name] = Buf("dd_" + name)
        k.dma("sp", dtn, ap, reads=[dep], writes=[dumps[name]])

    h1_d = nc.dram_tensor("h1_scratch", [TT, D], F32, kind="Internal").ap()
    h1_bufs = [Buf("h1_%d" % i) for i in range(NT)]

    pp = [nc.alloc_psum_tensor("pp%d" % i, [128, 1024], F32) for i in range(4)]
    pbuf = [Buf("bank%d" % i) for i in range(8)]
    bank_i = [0]

    def bank(i):
        return pp[i // 2][:, (i % 2) * 512:(i % 2) * 512 + 512]

    held = set()

    def nb():
        while True:
            i = bank_i[0] % 8
            bank_i[0] += 1
            if i not in held:
                return i

    def hold(i):
        held.add(i)

    def release(i):
        held.discard(i)

    def nb2():
        while True:
            if bank_i[0] % 2:
                bank_i[0] += 1
            i = bank_i[0] % 8
            bank_i[0] += 2
            if i not in held and (i + 1) not in held:
                return i

    ident_f = k.sb("ident_f", [128, 128], F32)
    ident = k.sb("ident", [128, 128], BF16)
    tri = k.sb("tri", [128, 2, 128], F32)
    utri = k.sb("utri", [128, 2, 128], F32)
    seqind = k.sb("seqind", [128, 2, 16], F32)
    decret = k.sb("decret", [64, 2, 4], F32)
    ones_row = k.sb("ones_row", [1, 128], F32)
    hgain_t = k.sb("hgain_t", [128, 2, 512], F32)
    wa2_t = k.sb("wa2_t", [16, 256], F32)
    ba_t = k.sb("ba_t", [1, 256], F32)

    k.dma("sp", ident_f[:], c_ident, writes=[ident_f])
    k.dma("sp", tri[:], c_tri.rearrange("a p t -> p a t"), writes=[tri])
    k.dma("sp", utri[:], c_utri.rearrange("a p t -> p a t"), writes=[utri])
    k.dma("sp", seqind[:], c_seqind.rearrange("a p t -> p a t"), writes=[seqind])
    k.dma("sp", decret[:], c_decret.rearrange("a p t -> p a t"), writes=[decret])
    k.dma("sp", wa2_t[:], w_a2, writes=[wa2_t])
    k.dma("sp", ba_t[:], b_a, writes=[ba_t])
    gslot = {}
    gain_box = [None]

    def load_gains(pairs):
        gain_box[0] = k.sb("gain_t", [128, len(pairs), D], F32)
        gain_t = gain_box[0]
        for gi, slot in pairs:
            gslot[gi] = slot
            k.dma("sp", gain_t[:, slot, :], gains[gi].partition_broadcast(128), writes=[gain_t])
    for i in range(2):
        k.dma("sp", hgain_t[:, i, :], hgains[i].partition_broadcast(128), writes=[hgain_t])
    k.op("dve", lambda e: e.tensor_copy(out=ident[:], in_=ident_f[:]), [ident_f], [ident])
    k.op("pool", lambda e: e.memset(ones_row[:], 1.0), [], [ones_row])
    eps_t = k.sb("eps_t", [128, 1], F32)
    k.op("pool", lambda e: e.memset(eps_t[:], EPS), [], [eps_t])
    one_t = k.sb("one_t", [128, 1], F32)
    k.op("pool", lambda e: e.memset(one_t[:], 1.0), [], [one_t])

    wl_pending = []

    def weights_ready():
        tags = {nm: v for nm, v in k.slots["pool"] if v > 0}
        for e in ("pe", "act", "dve", "pool", "sp"):
            k._wait(e, dict(tags))
        del wl_pending[:]

    def load_weight_bf(dst, row0, src, ncols, col0=0, chunk=512):
        kcs = src.shape[0] // 128
        for kc in range(kcs):
            for c0 in range(0, ncols, 2048):
                cw = min(2048, ncols - c0)
                k.dma("pool", dst[:, row0 + kc, col0 + c0:col0 + c0 + cw], src[kc * 128:(kc + 1) * 128, c0:c0 + cw],
                      writes=[Buf("wld")])
        wl_pending.append(dst)

    def rmsnorm_bf(xt, P, gi, out_bf, out_f=None):
        junk = out_bf if out_bf is not None else k.ring("rms_junk", [128, D], BF16, n=1)
        ssq = k.ring("rms_ssq", [128, 1], F32, n=2)
        k.op("act", lambda e: e.activation(out=junk[:P], in_=xt[:P], func=AF.Square, accum_out=ssq[:P]),
             [xt], [junk, ssq])
        rstd = k.ring("rms_rstd", [128, 1], F32, n=2)
        k.op("act", lambda e: e.activation(out=rstd[:P], in_=ssq[:P], func=AF.Sqrt, scale=1.0 / D, bias=eps_t[:P, 0:1]),
             [ssq, eps_t], [rstd])
        rstd2 = k.ring("rms_rstd2", [128, 1], F32, n=2)
        k.op("dve", lambda e: e.reciprocal(out=rstd2[:P], in_=rstd[:P]), [rstd], [rstd2])
        if out_f is not None:
            k.op("dve", lambda e: e.scalar_tensor_tensor(out=out_f[:P], in0=xt[:P], scalar=rstd2[:P, 0:1],
                                                         in1=gain_box[0][:P, gslot[gi], :], op0=ALU.mult, op1=ALU.mult),
                 [xt, rstd2, gain_box[0]], [out_f])
            if out_bf is not None:
                k.op("pool", lambda e: e.tensor_copy(out=out_bf[:P], in_=out_f[:P]), [out_f], [out_bf])
        else:
            k.op("dve", lambda e: e.scalar_tensor_tensor(out=out_bf[:P], in0=xt[:P], scalar=rstd2[:P, 0:1],
                                                         in1=gain_box[0][:P, gslot[gi], :], op0=ALU.mult, op1=ALU.mult),
                 [xt, rstd2, gain_box[0]], [out_bf])

    def transpose_to(dstT, src_bf, P, nchunks, evac="act", src_col0=0, dcol0=0, deps=None):
        for c0 in range(0, nchunks, 8):
            n = min(8, nchunks - c0)
            bi = nb()
            pv = bank(bi).bitcast(BF16).rearrange("p (c t) -> p c t", c=8)
            for c in range(n):
                k.op("pe", lambda e, c=c: e.transpose(out=pv[:, c, :P],
                                                      in_=src_bf[:P, src_col0 + (c0 + c) * 128:src_col0 + (c0 + c + 1) * 128],
                                                      identity=ident[:P, :P]),
                     (deps if deps is not None else [src_bf]) + [ident], [pbuf[bi]])
            if evac == "act":
                k.op("act", lambda e: e.copy(out=dstT[:, c0:c0 + n, dcol0:dcol0 + P], in_=pv[:, :n, :P]), [pbuf[bi]], [dstT])
            else:
                k.op(evac, lambda e: e.tensor_copy(out=dstT[:, c0:c0 + n, dcol0:dcol0 + P], in_=pv[:, :n, :P]), [pbuf[bi]], [dstT])

    def mm_tok(psum_bank, P, lhsT_t, w_t, col0, ncols, kcs=8, tok0=0):
        for kc in range(kcs):
            k.op("pe", lambda e, kc=kc: e.matmul(bank(psum_bank)[:P, :ncols], lhsT=lhsT_t[:, kc, tok0:tok0 + P],
                                                 rhs=w_t[:, kc, col0:col0 + ncols],
                                                 start=(kc == 0), stop=(kc == kcs - 1)),
                 [lhsT_t, w_t], [pbuf[psum_bank]])

    def mm_feat(psum_bank, slot, P, w_t, col0, rhsT_t, kcs=8, mcols=128):
        pv = bank(psum_bank).rearrange("p (c t) -> p c t", c=4)
        for kc in range(kcs):
            k.op("pe", lambda e, kc=kc: e.matmul(pv[:mcols, slot, :P], lhsT=w_t[:, kc, col0:col0 + mcols],
                                                 rhs=rhsT_t[:, kc, :P],
                                                 start=(kc == 0), stop=(kc == kcs - 1)),
                 [w_t, rhsT_t], [pbuf[psum_bank]])

    MARK = k.mark()

    UVW = 2 * D + 64
    uv_d = nc.dram_tensor("uv_scratch", [NEXP, UVW], BF16, kind="Internal").ap()
    uv_buf = Buf("uv")

    def p0_gen():
        RB = 1024
        for (tab, c0) in ((peer_u, 0), (peer_v, D)):
            for rr in range(0, NEXP, RB):
                k.dma("pool", uv_d[rr:rr + RB, c0:c0 + D], tab[rr:rr + RB, :], writes=[Buf("uvst")])
                yield

    load_gains([(0, 0)])
    seqindT = k.sb("seqindT", [64, 16, 64], BF16)
    k.dma("sp", seqindT[:].rearrange("p a b -> p (a b)"), c_seqindT, writes=[seqindT])
    w_in_bf = k.sb("w_in_bf", [128, 8, INW], BF16)
    w_pa_bf = k.sb("w_pa_bf", [128, 4, D], BF16)
    w_pb_bf = k.sb("w_pb_bf", [128, 4, D], BF16)
    w_o_bf = k.sb("w_o_bf", [128, 8, D], BF16)
    load_weight_bf(w_in_bf, 0, w_in, INW)
    load_weight_bf(w_pa_bf, 0, w_pa, D)
    load_weight_bf(w_pb_bf, 0, w_pb, D)
    load_weight_bf(w_o_bf, 0, w_o, D)
    weights_ready()

    S_f = {m: k.sb("S_f_" + m, [64, 4, 128], F32) for m in ("a", "b")}
    S_bf = {m: k.sb("S_bf_" + m, [64, 4, 128], BF16) for m in ("a", "b")}

    def mixer_core(m, P, ti, q_e, k_e, k_d, v_bf, dec, sg, hg_i, oT_dst, first):
        sample = (ti == n_ptiles)
        mi = 1 if sample else 0
        NS = SB_PER_CORE if sample else 1
        qT = k.ring("qT", [64, 4, 128], BF16, n=1)
        kT = k.ring("kT", [64, 4, 128], BF16, n=1)
        for src, dst in ((q_e, qT), (k_e, kT)):
            bi = nb()
            pv = bank(bi).bitcast(BF16).rearrange("p (c t) -> p c t", c=8)
            for h in range(4):
                k.op("pe", lambda e, h=h: e.transpose(out=pv[:64, h, :P], in_=src[:P, h * 64:(h + 1) * 64],
                                                      identity=ident[:P, :P]), [src, ident], [pbuf[bi]])
            k.op("act", lambda e: e.copy(out=dst[:, :, :P], in_=pv[:64, 0:4, :P]), [pbuf[bi]], [dst])
        bi = nb()
        av = bank(bi).rearrange("p (c t) -> p c t", c=4)
        for h in range(4):
            k.op("pe", lambda e, h=h: e.matmul(av[:P, h, :P], lhsT=kT[:, h, :P], rhs=qT[:, h, :P],
                                               start=True, stop=True), [kT, qT], [pbuf[bi]])
        attT = k.ring("attT", [128, 4, 128], BF16, n=1)
        k.op("dve", lambda e: e.tensor_tensor(out=attT[:P, :, :P], in0=av[:P, :, :P],
                                              in1=bc(tri[:P, mi, :P], [P, 4, P], 1), op=ALU.mult),
             [pbuf[bi], tri], [attT])
        if not sample:
            bo = nb()
            hold(bo)
            ov = bank(bo).rearrange("p (c t) -> p c t", c=4)
            ov_dep = pbuf[bo]
            for h in range(4):
                k.op("pe", lambda e, h=h: e.matmul(ov[:P, h, :], lhsT=attT[:P, h, :P], rhs=v_bf[:P, h * 128:(h + 1) * 128],
                                                   start=True, stop=first), [attT, v_bf], [pbuf[bo]])
                if not first:
                    k.op("pe", lambda e, h=h: e.matmul(ov[:P, h, :], lhsT=qT[:, h, :P], rhs=S_bf[m][:, h, :],
                                                       start=False, stop=True), [qT, S_bf[m]], [pbuf[bo]])
            bs = nb()
            sv = bank(bs).rearrange("p (c t) -> p c t", c=4)
            for h in range(4):
                k.op("pe", lambda e, h=h: e.matmul(sv[:64, h, :], lhsT=k_d[:P, h * 64:(h + 1) * 64],
                                                   rhs=v_bf[:P, h * 128:(h + 1) * 128], start=True, stop=True),
                     [k_d, v_bf], [pbuf[bs]])
            if first:
                k.op("dve", lambda e: e.tensor_copy(out=S_f[m][:], in_=sv[:64, :, :]), [pbuf[bs]], [S_f[m]])
            else:
                for h in range(4):
                    k.op("dve", lambda e, h=h: e.scalar_tensor_tensor(
                        out=S_f[m][:, h, :], in0=S_f[m][:, h, :], scalar=dec[:, h, 0:1], in1=sv[:64, h, :],
                        op0=ALU.mult, op1=ALU.add), [dec, pbuf[bs]], [S_f[m]])
            k.op("pool", lambda e: e.tensor_copy(out=S_bf[m][:], in_=S_f[m][:]), [S_f[m]], [S_bf[m]])
            if ti == n_ptiles - 1:
                dst = o_gla_p if m == "a" else o_ret_p
                ob = out_bufs[2] if m == "a" else out_bufs[3]
                k.dma("sp", dst.rearrange("h k v -> k h v"), S_f[m][:], reads=[S_f[m]], writes=[ob])
        else:
            GS = 2
            bos = []
            for h in range(4):
                bb = nb()
                hold(bb)
                bos.append(bb)
                k.op("pe", lambda e, h=h, bb=bb: e.matmul(bank(bb)[:P, 0:128], lhsT=attT[:P, h, :P],
                                                          rhs=v_bf[:P, h * 128:(h + 1) * 128], start=True, stop=False),
                     [attT, v_bf], [pbuf[bb]])
            src = st_gla if m == "a" else st_ret
            dst = o_gla_s if m == "a" else o_ret_s
            ob = out_bufs[6] if m == "a" else out_bufs[7]
            for g in range(SB_PER_CORE // GS):
                S0 = k.ring("S0_f", [64, GS, 4, 128], F32, n=1)
                for j in range(GS):
                    k.dma("sp", S0[:, j, :, :], src[g * GS + j].rearrange("h k v -> k h v"), writes=[S0])
                qTm = k.ring("qTm", [64, 4, GS, 64], F32, n=1)
                k.op("dve", lambda e: e.tensor_tensor(out=qTm[:], in0=bc(qT[:, :, :P], [64, 4, GS, P], 2),
                                                      in1=bc(seqindT[:, g * GS:(g + 1) * GS, :], [64, 4, GS, P], 1),
                                                      op=ALU.mult), [qT, seqindT], [qTm])
                for h in range(4):
                    for j in range(GS):
                        last = (g == SB_PER_CORE // GS - 1 and j == GS - 1)
                        k.op("pe", lambda e, h=h, j=j: e.matmul(bank(bos[h])[:P, 0:128], lhsT=qTm[:, h, j, :],
                                                                rhs=S0[:, j, h, :], start=False, stop=last),
                             [qTm, S0], [pbuf[bos[h]]])
                kdm = k.ring("kdm", [64, GS, 256], BF16, n=1)
                k.op("dve", lambda e: e.tensor_tensor(out=kdm[:], in0=bc(k_d[:P, :], [P, GS, 256], 1),
                                                      in1=bc(seqind[:P, 1, g * GS:(g + 1) * GS], [P, GS, 256], 2),
                                                      op=ALU.mult), [k_d, seqind], [kdm])
                for j in range(GS):
                    bs = nb()
                    sv = bank(bs).rearrange("p (c t) -> p c t", c=4)
                    for h in range(4):
                        k.op("pe", lambda e, h=h, j=j: e.matmul(sv[:64, h, :], lhsT=kdm[:P, j, h * 64:(h + 1) * 64],
                                                                rhs=v_bf[:P, h * 128:(h + 1) * 128], start=True, stop=True),
                             [kdm, v_bf], [pbuf[bs]])
                    for h in range(4):
                        k.op("dve", lambda e, h=h, j=j: e.scalar_tensor_tensor(
                            out=S0[:, j, h, :], in0=S0[:, j, h, :], scalar=dec[:, h, g * GS + j:g * GS + j + 1],
                            in1=sv[:64, h, :], op0=ALU.mult, op1=ALU.add), [dec, pbuf[bs]], [S0])
                for j in range(GS):
                    k.dma("sp", dst[g * GS + j].rearrange("h k v -> k h v"), S0[:, j, :, :], reads=[S0], writes=[ob])
            o_sb = k.ring("o_sb", [64, 4, 128], F32, n=1)
            for h in range(4):
                k.op("act", lambda e, h=h: e.copy(out=o_sb[:P, h, :], in_=bank(bos[h])[:P, 0:128]), [pbuf[bos[h]]], [o_sb])
                release(bos[h])
            ov = o_sb
            ov_dep = o_sb
            bo = None
        sq = k.ring("hn_sq", [128, 4, 128], F32, n=1)
        k.op("act", lambda e: e.activation(out=sq[:P], in_=ov[:P], func=AF.Square), [ov_dep], [sq])
        ssq = k.ring("hn_ssq", [128, 4], F32)
        k.op("dve", lambda e: e.tensor_reduce(out=ssq[:P], in_=sq[:P], axis=AX.X, op=ALU.add), [sq], [ssq])
        r1 = k.ring("hn_r1", [128, 4], F32)
        k.op("act", lambda e: e.activation(out=r1[:P], in_=ssq[:P], func=AF.Sqrt, scale=1.0 / 128, bias=eps_t[:P, 0:1]),
             [ssq, eps_t], [r1])
        r2 = k.ring("hn_r2", [128, 4], F32)
        k.op("dve", lambda e: e.reciprocal(out=r2[:P], in_=r1[:P]), [r1], [r2])
        on = sq
        k.op("dve", lambda e: e.tensor_tensor(out=on[:P], in0=ov[:P], in1=bc(r2[:P, :], [P, 4, 128], 2), op=ALU.mult),
             [ov_dep, r2], [on])
        if bo is not None:
            release(bo)
        gg = sg
        k.op("pool", lambda e: e.tensor_tensor(out=gg[:P], in0=sg[:P], in1=hgain_t[:P, hg_i, :], op=ALU.mult),
             [hgain_t], [gg])
        ob_bf = k.ring("hn_obf", [128, 512], BF16, n=1)
        k.op("dve", lambda e: e.tensor_tensor(out=ob_bf[:P], in0=on[:P].rearrange("p a b -> p (a b)"), in1=gg[:P],
                                              op=ALU.mult), [on, gg], [ob_bf])
        transpose_to(oT_dst, ob_bf, P, 4)

    p0 = p0_gen()
    for ti in range(NT):
        sample = (ti == n_ptiles)
        P = PS if sample else 128
        mi = 1 if sample else 0
        NS = SB_PER_CORE if sample else 1
        r0 = ti * 128
        first = (ti == 0)
        for _ in range(2):
            if p0 is not None:
                try:
                    next(p0)
                except StopIteration:
                    p0 = None
        xt = k.ring("xt", [128, D], F32, n=1)
        k.dma("sp", xt[:P], x_all[r0:r0 + P, :], writes=[xt])
        rot_t = k.ring("rot_t", [128, 6, 4, 32], F32, n=1)
        k.dma("sp", rot_t[:P].rearrange("p a h i -> p (a h i)"), c_rot[r0:r0 + P, :], writes=[rot_t])
        xn = k.ring("xn", [128, D], BF16, n=1)
        rmsnorm_bf(xt, P, 0, xn)
        xnT = k.ring("xnT", [128, 8, 128], BF16, n=1)
        transpose_to(xnT, xn, P, 8)

        qk_a = k.ring("qk_ab", [128, 512], F32, n=1)
        b1 = nb()
        mm_tok(b1, P, xnT, w_in_bf, C_GQ, 512)
        k.op("act", lambda e: e.copy(out=qk_a[:P], in_=bank(b1)[:P, :]), [pbuf[b1]], [qk_a])
        v_a = k.ring("v_ab", [128, 512], BF16, n=1)
        b2 = nb()
        mm_tok(b2, P, xnT, w_in_bf, C_GV, 512)
        k.op("act", lambda e: e.copy(out=v_a[:P], in_=bank(b2)[:P, :]), [pbuf[b2]], [v_a])
        sg_a = k.ring("sg_ab", [128, 512], F32, n=1)
        b3 = nb()
        mm_tok(b3, P, xnT, w_in_bf, C_GR, 512)
        k.op("act", lambda e: e.activation(out=sg_a[:P], in_=bank(b3)[:P, :], func=AF.Silu), [pbuf[b3]], [sg_a])
        b7 = nb()
        mm_feat(b7, 0, P, w_in_bf, C_GLR, xnT, mcols=16)
        glrT = k.ring("glrT", [16, 128], F32, n=1)
        k.op("act", lambda e: e.copy(out=glrT[:, :P], in_=bank(b7)[:16, :P]), [pbuf[b7]], [glrT])

        b8 = nb()
        k.op("pe", lambda e: e.matmul(bank(b8)[:P, :256], lhsT=glrT[:, :P], rhs=wa2_t[:, :], start=True, stop=False),
             [glrT, wa2_t], [pbuf[b8]])
        k.op("pe", lambda e: e.matmul(bank(b8)[:P, :256], lhsT=ones_row[:, :P], rhs=ba_t[:, :], start=False, stop=True),
             [ones_row, ba_t], [pbuf[b8]])
        ez = k.ring("ez", [128, 256], F32, n=1)
        k.op("act", lambda e: e.activation(out=ez[:P], in_=bank(b8)[:P, :256], func=AF.Exp, scale=-1.0), [pbuf[b8]], [ez])
        nla = ez
        k.op("act", lambda e: e.activation(out=nla[:P], in_=ez[:P], func=AF.Ln, bias=one_t[:P, 0:1]), [one_t], [nla])
        nla2 = ez
        k.op("dve", lambda e: e.tensor_scalar(out=nla2[:P], in0=nla[:P], scalar1=1.0 / 16, scalar2=None, op0=ALU.mult),
             [], [nla2])
        b9 = nb()
        k.op("pe", lambda e: e.matmul(bank(b9)[:P, 0:256], lhsT=tri[:P, mi, :P], rhs=nla2[:P, :], start=True, stop=True),
             [tri, nla2], [pbuf[b9]])
        k.op("pe", lambda e: e.matmul(bank(b9)[:P, 256:512], lhsT=utri[:P, mi, :P], rhs=nla2[:P, :], start=True, stop=True),
             [utri, nla2], [pbuf[b9]])
        b10 = nb()
        dv_ = bank(b10).rearrange("p (c t) -> p c t", c=4)
        for h in range(4):
            k.op("pe", lambda e, h=h: e.matmul(dv_[:64, h, :max(NS, 2)], lhsT=nla2[:P, h * 64:(h + 1) * 64],
                                               rhs=seqind[:P, mi, :max(NS, 2)], start=True, stop=True),
                 [nla2, seqind], [pbuf[b10]])
        dec_a = k.ring("dec_a", [64, 4, 16], F32, n=1)
        k.op("act", lambda e: e.activation(out=dec_a[:, :, :NS], in_=dv_[:64, :, :NS], func=AF.Exp, scale=-1.0),
             [pbuf[b10]], [dec_a])
        qe_a = k.ring("qe_ab", [128, 256], BF16, n=1)
        ke_a = k.ring("ke_ab", [128, 256], BF16, n=1)
        kd_a = k.ring("kd_ab", [128, 256], BF16, n=1)
        eq = k.ring("e3", [128, 256], F32, n=1)
        k.op("act", lambda e: e.activation(out=eq[:P], in_=bank(b9)[:P, 0:256], func=AF.Exp, scale=-1.0), [pbuf[b9]], [eq])
        k.op("dve", lambda e: e.scalar_tensor_tensor(out=qe_a[:P], in0=eq[:P], scalar=0.125, in1=qk_a[:P, 0:256],
                                                     op0=ALU.mult, op1=ALU.mult), [eq, qk_a], [qe_a])
        ek = k.ring("e3", [128, 256], F32, n=1)
        k.op("act", lambda e: e.activation(out=ek[:P], in_=bank(b9)[:P, 0:256], func=AF.Exp, scale=1.0), [pbuf[b9]], [ek])
        k.op("dve", lambda e: e.tensor_tensor(out=ke_a[:P], in0=ek[:P], in1=qk_a[:P, 256:512], op=ALU.mult),
             [ek, qk_a], [ke_a])
        ekd = k.ring("e3", [128, 256], F32, n=1)
        k.op("act", lambda e: e.activation(out=ekd[:P], in_=bank(b9)[:P, 256:512], func=AF.Exp, scale=-1.0), [pbuf[b9]], [ekd])
        k.op("dve", lambda e: e.tensor_tensor(out=kd_a[:P], in0=ekd[:P], in1=qk_a[:P, 256:512], op=ALU.mult),
             [ekd, qk_a], [kd_a])
        oaT = k.ring("oaT", [128, 4, 128], BF16, n=1)
        mixer_core("a", P, ti, qe_a, ke_a, kd_a, v_a, dec_a, sg_a, 0, oaT, first)

        qk_b = k.ring("qk_ab", [128, 512], F32, n=1)
        b4 = nb()
        mm_tok(b4, P, xnT, w_in_bf, C_RQ, 512)
        k.op("act", lambda e: e.copy(out=qk_b[:P], in_=bank(b4)[:P, :]), [pbuf[b4]], [qk_b])
        v_b = k.ring("v_ab", [128, 512], BF16, n=1)
        b5 = nb()
        mm_tok(b5, P, xnT, w_in_bf, C_RV, 512)
        k.op("act", lambda e: e.copy(out=v_b[:P], in_=bank(b5)[:P, :]), [pbuf[b5]], [v_b])
        sg_b = k.ring("sg_ab", [128, 512], F32, n=1)
        b6 = nb()
        mm_tok(b6, P, xnT, w_in_bf, C_RG, 512)
        k.op("act", lambda e: e.activation(out=sg_b[:P], in_=bank(b6)[:P, :], func=AF.Silu), [pbuf[b6]], [sg_b])
        qe_b = k.ring("qe_ab", [128, 256], BF16, n=1)
        ke_b = k.ring("ke_ab", [128, 256], BF16, n=1)
        kd_b = k.ring("kd_ab", [128, 256], BF16, n=1)
        for (dst, col0, ci) in ((qe_b, 0, 0), (ke_b, 256, 2), (kd_b, 256, 4)):
            xv = qk_b[:P, col0:col0 + 256].rearrange("p (h i) -> p h i", h=4)
            x1, x2 = xv[:, :, 0:32], xv[:, :, 32:64]
            dv3 = dst[:P, :].rearrange("p (h i) -> p h i", h=4)
            cc, ss = rot_t[:P, ci], rot_t[:P, ci + 1]
            t1 = k.ring("rt1", [128, 4, 32], F32, n=1)
            t2 = k.ring("rt2", [128, 4, 32], F32, n=1)
            k.op("dve", lambda e: e.tensor_tensor(out=t1[:P], in0=x1, in1=cc, op=ALU.mult), [qk_b, rot_t], [t1])
            k.op("pool", lambda e: e.tensor_tensor(out=t2[:P], in0=x2, in1=ss, op=ALU.mult), [qk_b, rot_t], [t2])
            k.op("dve", lambda e: e.tensor_tensor(out=dv3[:, :, 0:32], in0=t1[:P], in1=t2[:P], op=ALU.subtract),
                 [t1, t2], [dst])
            t3 = k.ring("rt1", [128, 4, 32], F32, n=1)
            t4 = k.ring("rt2", [128, 4, 32], F32, n=1)
            k.op("dve", lambda e: e.tensor_tensor(out=t3[:P], in0=x1, in1=ss, op=ALU.mult), [qk_b, rot_t], [t3])
            k.op("pool", lambda e: e.tensor_tensor(out=t4[:P], in0=x2, in1=cc, op=ALU.mult), [qk_b, rot_t], [t4])
            k.op("dve", lambda e: e.tensor_tensor(out=dv3[:, :, 32:64], in0=t3[:P], in1=t4[:P], op=ALU.add),
                 [t3, t4], [dst])
        dec_b = k.ring("dec_b", [64, 4, 16], F32, n=1)
        k.op("dve", lambda e: e.tensor_copy(out=dec_b[:], in_=bc(decret[:, mi, :], [64, 4, 16], 2)), [decret], [dec_b])
        obT = k.ring("obT", [128, 4, 128], BF16, n=1)
        mixer_core("b", P, ti, qe_b, ke_b, kd_b, v_b, dec_b, sg_b, 1, obT, first)

        mT = k.ring("mT", [128, 8, 128], BF16, n=1)
        for half in range(2):
            bz_a, bz_b, bp_a, bp_b = nb(), nb(), nb(), nb()
            for j in range(4):
                dt_ = half * 4 + j
                mm_feat(bz_a, j, P, w_in_bf, C_ZA + dt_ * 128, xnT)
                mm_feat(bz_b, j, P, w_in_bf, C_ZB + dt_ * 128, xnT)
                mm_feat(bp_a, j, P, w_pa_bf, dt_ * 128, oaT, kcs=4)
                mm_feat(bp_b, j, P, w_pb_bf, dt_ * 128, obT, kcs=4)

            def v4(bi):
                return bank(bi).rearrange("p (c t) -> p c t", c=4)[:, :, :P]
            sza = k.ring("sza", [128, 4, 128], F32, n=1)
            szb = k.ring("szb", [128, 4, 128], F32, n=1)
            k.op("act", lambda e: e.activation(out=sza[:, :, :P], in_=v4(bz_a), func=AF.Sigmoid), [pbuf[bz_a]], [sza])
            k.op("act", lambda e: e.activation(out=szb[:, :, :P], in_=v4(bz_b), func=AF.Sigmoid), [pbuf[bz_b]], [szb])
            ma = sza
            mb = szb
            k.op("dve", lambda e: e.tensor_tensor(out=ma[:, :, :P], in0=sza[:, :, :P], in1=v4(bp_a), op=ALU.mult),
                 [pbuf[bp_a]], [ma])
            k.op("dve", lambda e: e.tensor_tensor(out=mb[:, :, :P], in0=szb[:, :, :P], in1=v4(bp_b), op=ALU.mult),
                 [pbuf[bp_b]], [mb])
            k.op("pool", lambda e: e.tensor_tensor(out=mT[:, half * 4:half * 4 + 4, :P], in0=ma[:, :, :P],
                                                   in1=mb[:, :, :P], op=ALU.add), [ma, mb], [mT])
        h1 = xt
        for half in range(2):
            bh = nb()
            mm_tok(bh, P, mT, w_o_bf, half * 512, 512)
            k.op("dve", lambda e: e.tensor_tensor(out=h1[:P, half * 512:(half + 1) * 512], in0=bank(bh)[:P, :],
                                                  in1=xt[:P, half * 512:(half + 1) * 512], op=ALU.add),
                 [pbuf[bh]], [h1])
        k.dma("sp", h1_d[r0:r0 + P, :], h1[:P], reads=[h1], writes=[h1_bufs[ti]])
        if debug:
            k.dma("sp", dbg_h1[r0:r0 + P, :], h1[:P], reads=[h1], writes=[dbg_buf])


    if p0 is not None:
        for _ in p0:
            pass
    k.phase_reset(MARK)
    load_gains([(1, 0), (2, 1)])
    h2_d = nc.dram_tensor("h2_scratch", [TT, D], F32, kind="Internal").ap()
    h2_bufs = [Buf("h2_%d" % i) for i in range(NT)]
    w_xq_bf = k.sb("w_xq_bf", [128, 8, D], BF16)
    w_xk_bf = k.sb("w_xk_bf", [128, 8, D], BF16)
    w_xv_bf = k.sb("w_xv_bf", [128, 8, D], BF16)
    w_xo_bf = k.sb("w_xo_bf", [128, 8, D], BF16)
    load_weight_bf(w_xk_bf, 0, w_xk, D)
    load_weight_bf(w_xv_bf, 0, w_xv, D)
    load_weight_bf(w_xq_bf, 0, w_xq, D)
    load_weight_bf(w_xo_bf, 0, w_xo, D)
    weights_ready()
    mnT = k.sb("mnT", [128, 8, 256], BF16)
    mkT = k.sb("mkT", [128, 8, 256], BF16)
    mv_bf = k.sb("mv_bf", [128, 2, D], BF16)
    for mc in range(2):
        mt = k.ring("xtB", [128, D], F32, n=3)
        k.dma("sp", mt[:], mem[mc * 128:(mc + 1) * 128, :], writes=[mt])
        mn = k.ring("xn", [128, D], BF16, n=1)
        rmsnorm_bf(mt, 128, 2, mn)
        transpose_to(mnT, mn, 128, 8, dcol0=mc * 128)
    for mc in range(2):
        for (wt, dst, ob, isv) in ((w_xk_bf, o_mk, out_bufs[4], False), (w_xv_bf, o_mv, out_bufs[5], True)):
            kv_f = k.ring("kv_f", [128, D], F32, n=2)
            for half in range(2):
                bb = nb()
                mm_tok(bb, 128, mnT, wt, half * 512, 512, tok0=mc * 128)
                k.op("act", lambda e: e.copy(out=kv_f[:, half * 512:(half + 1) * 512], in_=bank(bb)[:, :]),
                     [pbuf[bb]], [kv_f])
            k.dma("sp", dst[mc * 128:(mc + 1) * 128, :], kv_f[:], reads=[kv_f], writes=[ob])
            if isv:
                k.op("dve", lambda e: e.tensor_copy(out=mv_bf[:, mc, :], in_=kv_f[:]), [kv_f], [mv_bf])
    for j in range(8):
        if j % 2 == 0:
            bb = nb()
        for kc in range(8):
            k.op("pe", lambda e, kc=kc: e.matmul(bank(bb)[:, (j % 2) * 256:(j % 2) * 256 + 256],
                                                 lhsT=w_xk_bf[:, kc, j * 128:(j + 1) * 128], rhs=mnT[:, kc, :],
                                                 start=(kc == 0), stop=(kc == 7)), [w_xk_bf, mnT], [pbuf[bb]])
        if j % 2 == 1:
            k.op("act", lambda e: e.copy(out=mkT[:, j - 1:j + 1, :],
                                         in_=bank(bb).rearrange("p (a m) -> p a m", a=2)), [pbuf[bb]], [mkT])

    def softmax_rows(scv, deps, P, nring=1):
        mx = k.ring("sm_mx", [128, 4], F32)
        k.op("dve", lambda e: e.tensor_reduce(out=mx[:P], in_=scv, axis=AX.X, op=ALU.max), deps, [mx])
        sh = k.ring("sm_sh", [128, 4, 256], F32, n=1)
        k.op("dve", lambda e: e.tensor_tensor(out=sh[:P], in0=scv, in1=bc(mx[:P, :], [P, 4, 256], 2), op=ALU.subtract),
             deps + [mx], [sh])
        k.op("act", lambda e: e.activation(out=sh[:P], in_=sh[:P], func=AF.Exp), [], [sh])
        sm = k.ring("sm_sm", [128, 4], F32)
        k.op("dve", lambda e: e.tensor_reduce(out=sm[:P], in_=sh[:P], axis=AX.X, op=ALU.add), [sh], [sm])
        rs = k.ring("sm_rs", [128, 4], F32)
        k.op("dve", lambda e: e.reciprocal(out=rs[:P], in_=sm[:P]), [sm], [rs])
        pb = k.ring("sm_pb%d" % nring, [128, 4, 256], BF16, n=nring)
        k.op("dve", lambda e: e.tensor_tensor(out=pb[:P], in0=sh[:P], in1=bc(rs[:P, :], [P, 4, 256], 2), op=ALU.mult),
             [sh, rs], [pb])
        return pb

    def n2_gen():
        n2_all = k.sb("n2_all", [128, NEXP // 128], F32)
        for c in range(NEXP // 128):
            ub = k.ring("n2_u", [128, D], BF16, n=3)
            k.dma("pool", ub[:], uv_d[c * 128:(c + 1) * 128, 0:D], writes=[ub])
            nj = k.ring("n2_j", [128, D], BF16, n=1)
            k.op("act", lambda e: e.activation(out=nj[:], in_=ub[:], func=AF.Square, accum_out=n2_all[:, c:c + 1]),
                 [ub], [nj, n2_all])
            yield
        NC_ = NEXP // 128
        m_ = k.sb("n2_m", [128, NC_], F32)
        k.op("dve", lambda e: e.tensor_scalar(out=m_[:], in0=n2_all[:], scalar1=-0.5, scalar2=None, op0=ALU.mult),
             [n2_all], [m_])
        hl = k.sb("n2_hl", [128, NC_, 64], BF16)
        k.op("pool", lambda e: e.memset(hl[:], 0.0), [], [hl])
        k.op("dve", lambda e: e.tensor_copy(out=hl[:, :, 0], in_=m_[:]), [m_], [hl])
        hif = k.sb("n2_hif", [128, NC_], F32)
        k.op("dve", lambda e: e.tensor_copy(out=hif[:], in_=hl[:, :, 0]), [hl], [hif])
        k.op("dve", lambda e: e.tensor_tensor(out=hl[:, :, 1], in0=m_[:], in1=hif[:], op=ALU.subtract), [m_, hif], [hl])
        n2_dst = uv_d[:, 2 * D:UVW].rearrange("(c p) o -> p c o", p=128)
        for q4 in range(4):
            cs = slice(q4 * (NC_ // 4), (q4 + 1) * (NC_ // 4))
            k.dma("pool", n2_dst[:, cs, :], hl[:, cs, :], reads=[hl], writes=[Buf("uvn2")])
        yield

    p0box = [n2_gen()]
    p0_steps = -(-(NEXP // 128 + 1) // n_ptiles)

    def b_front(ti):
        sample = (ti == n_ptiles)
        P = PS if sample else 128
        r0 = ti * 128
        for _ in range(p0_steps):
            if p0box[0] is not None:
                try:
                    next(p0box[0])
                except StopIteration:
                    p0box[0] = None
        ht = k.ring("xtB", [128, D], F32, n=3)
        k.dma("sp", ht[:P], h1_d[r0:r0 + P, :], reads=[h1_bufs[ti]], writes=[ht])
        hn = k.ring("xn", [128, D], BF16, n=1)
        rmsnorm_bf(ht, P, 1, hn)
        hnT = k.ring("xnT", [128, 8, 128], BF16, n=1)
        transpose_to(hnT, hn, P, 8)
        qT = k.ring("xqT", [128, 8, 128], BF16, n=3)
        for half in range(2):
            bb = nb()
            for j in range(4):
                mm_feat(bb, j, P, w_xq_bf, (half * 4 + j) * 128, hnT)
            k.op("act", lambda e: e.mul(out=qT[:, half * 4:half * 4 + 4, :P],
                                        in_=bank(bb).rearrange("p (c t) -> p c t", c=4)[:, :, :P], mul=1.0 / 16),
                 [pbuf[bb]], [qT])
        return dict(ht=ht, qT=qT, P=P, r0=r0, sample=sample)

    def b_mid(ti, st):
        ht, qT, P, r0, sample = st["ht"], st["qT"], st["P"], st["r0"], st["sample"]
        if not sample:
            b0 = nb2()
            hold(b0), hold(b0 + 1)
            scv = pp[b0 // 2][:P, :].rearrange("p (h m) -> p h m", h=4)
            for h in range(4):
                for c in range(2):
                    k.op("pe", lambda e, h=h, c=c: e.matmul(scv[:, h, :], lhsT=qT[:, h * 2 + c, :P], rhs=mkT[:, h * 2 + c, :],
                                                            start=(c == 0), stop=(c == 1)), [qT, mkT], [pbuf[b0 + h // 2]])
            st["pb"] = softmax_rows(scv, [pbuf[b0], pbuf[b0 + 1]], P, nring=2)
            release(b0), release(b0 + 1)

    def b_back(ti, st):
        ht, qT, P, r0, sample = st["ht"], st["qT"], st["P"], st["r0"], st["sample"]
        oT = k.ring("xoT", [128, 8, 128], BF16, n=1)
        if not sample:
            pb = st["pb"]
            pT = k.ring("xpT", [128, 8, 128], BF16, n=1)
            transpose_to(pT, pb[:P].rearrange("p h m -> p (h m)"), P, 8, deps=[pb])
            for half in range(2):
                bb = nb()
                pv = bank(bb).rearrange("p (c t) -> p c t", c=4)
                for jj in range(4):
                    j = half * 4 + jj
                    h, c = j // 2, j % 2
                    for mc in range(2):
                        k.op("pe", lambda e, mc=mc: e.matmul(pv[:, jj, :P], lhsT=mv_bf[:, mc, h * 256 + c * 128:h * 256 + c * 128 + 128],
                                                             rhs=pT[:, h * 2 + mc, :P], start=(mc == 0), stop=(mc == 1)),
                             [mv_bf, pT], [pbuf[bb]])
                k.op("act", lambda e: e.copy(out=oT[:, half * 4:half * 4 + 4, :P], in_=pv[:, :, :P]), [pbuf[bb]], [oT])
        else:
            for b in range(SB_PER_CORE):
                Kb_bf = k.ring("Kb_bf", [128, 2, D], BF16, n=2)
                k.dma("pool", Kb_bf[:], ck_in[b].rearrange("(mc p) d -> p mc d", p=128), writes=[Kb_bf])
                KbT = k.ring("KbT", [128, 8, 256], BF16, n=1)
                for mc in range(2):
                    transpose_to(KbT, Kb_bf[:, mc, :], 128, 8, dcol0=mc * 128, deps=[Kb_bf])
                Vb_bf = k.ring("Vb_bf", [128, 2, D], BF16, n=2)
                k.dma("pool", Vb_bf[:], cv_in[b].rearrange("(mc p) d -> p mc d", p=128), writes=[Vb_bf])
                b0 = nb2()
                hold(b0), hold(b0 + 1)
                scv = pp[b0 // 2][:DEC_S, :].rearrange("p (h m) -> p h m", h=4)
                for h in range(4):
                    for c in range(2):
                        k.op("pe", lambda e, h=h, c=c: e.matmul(scv[:, h, :], lhsT=qT[:, h * 2 + c, b * DEC_S:(b + 1) * DEC_S],
                                                                rhs=KbT[:, h * 2 + c, :], start=(c == 0), stop=(c == 1)),
                             [qT, KbT], [pbuf[b0 + h // 2]])
                pb = softmax_rows(scv, [pbuf[b0], pbuf[b0 + 1]], DEC_S)
                release(b0), release(b0 + 1)
                pTb = k.ring("pTb", [128, 8, DEC_S], BF16, n=1)
                bt = nb()
                ptv = bank(bt).bitcast(BF16).rearrange("p (c t) -> p c t", c=8)
                for j in range(8):
                    k.op("pe", lambda e, j=j: e.transpose(out=ptv[:, j, :DEC_S], in_=pb[:DEC_S, j // 2, (j % 2) * 128:(j % 2) * 128 + 128],
                                                          identity=ident[:DEC_S, :DEC_S]), [pb, ident], [pbuf[bt]])
                k.op("act", lambda e: e.copy(out=pTb[:], in_=ptv[:, :, :DEC_S]), [pbuf[bt]], [pTb])
                for half in range(2):
                    bb = nb()
                    pv = bank(bb).rearrange("p (c t) -> p c t", c=4)
                    for jj in range(4):
                        j = half * 4 + jj
                        h, c = j // 2, j % 2
                        for mc in range(2):
                            k.op("pe", lambda e, mc=mc: e.matmul(pv[:, jj, :DEC_S],
                                                                 lhsT=Vb_bf[:, mc, h * 256 + c * 128:h * 256 + c * 128 + 128],
                                                                 rhs=pTb[:, h * 2 + mc, :], start=(mc == 0), stop=(mc == 1)),
                                 [Vb_bf, pTb], [pbuf[bb]])
                    k.op("act", lambda e: e.copy(out=oT[:, half * 4:half * 4 + 4, b * DEC_S:(b + 1) * DEC_S],
                                                 in_=pv[:, :, :DEC_S]), [pbuf[bb]], [oT])
        for half in range(2):
            bh = nb()
            mm_tok(bh, P, oT, w_xo_bf, half * 512, 512)
            k.op("dve", lambda e: e.tensor_tensor(out=ht[:P, half * 512:(half + 1) * 512], in0=bank(bh)[:P, :],
                                                  in1=ht[:P, half * 512:(half + 1) * 512], op=ALU.add),
                 [pbuf[bh]], [ht])
        k.dma("sp", h2_d[r0:r0 + P, :], ht[:P], reads=[ht], writes=[h2_bufs[ti]])
        if debug:
            k.dma("sp", dbg_h2[r0:r0 + P, :], ht[:P], reads=[ht], writes=[dbg_buf])


    sts = {0: b_front(0)}
    if NT > 1:
        sts[1] = b_front(1)
    b_mid(0, sts[0])
    for ti in range(NT):
        if ti + 2 < NT:
            sts[ti + 2] = b_front(ti + 2)
        if ti + 1 < NT:
            b_mid(ti + 1, sts[ti + 1])
        b_back(ti, sts.pop(ti))
    p0 = p0box[0]
    if p0 is not None:
        for _ in p0:
            pass
    k.phase_reset(MARK)
    load_gains([(3, 0), (4, 1)])
    wq_bf = k.sb("wq_bf", [128, 8, 2048], BF16)
    load_weight_bf(wq_bf, 0, peer_wq, 2048)
    weights_ready()
    skT = k.sb("skT", [128, 16, 128], BF16)
    iota16 = k.sb("iota16", [128, 16], F32)
    k.dma("sp", iota16[:], c_iota16, writes=[iota16])
    identsel = k.sb("identsel", [128, 128, 128], BF16)
    k.dma("sp", identsel[:].rearrange("p a b -> p (a b)"), c_identsel, writes=[identsel])
    MARK_C = k.mark()
    sk_f = k.sb("sk_f", [128, 16, 128], F32)
    k.dma("sp", sk_f[:], peer_sk.rearrange("a k d -> k a d"), writes=[sk_f])
    sk_bf = k.sb("sk_bf", [128, 16 * 128], BF16)
    k.op("dve", lambda e: e.tensor_copy(out=sk_bf[:], in_=sk_f[:].rearrange("p a d -> p (a d)")), [sk_f], [sk_bf])
    transpose_to(skT, sk_bf, 128, 16)
    k.phase_reset(MARK_C)

    ones_f = k.sb("ones_f", [128, 128], F32)
    k.op("dve", lambda e: e.memset(ones_f[:], 1.0), [], [ones_f])

    def routing(ti, res):
        sample = (ti == n_ptiles)
        P = PS if sample else 128
        r0 = ti * 128
        ht = k.ring("xt", [128, D], F32, n=2)
        k.dma("sp", ht[:P], h2_d[r0:r0 + P, :], reads=[h2_bufs[ti]], writes=[ht])
        fn_bf = k.ring("xn", [128, D], BF16, n=2)
        rmsnorm_bf(ht, P, 3, fn_bf)
        yield
        fnT = k.ring("xnT", [128, 8, 128], BF16, n=1)
        transpose_to(fnT, fn_bf, P, 8)
        yield
        xj = k.ring("n2_j", [128, D], BF16, n=1)
        xsq = k.ring("xsq", [128, 1], F32, n=1)
        k.op("act", lambda e: e.activation(out=xj[:P], in_=fn_bf[:P], func=AF.Square, accum_out=xsq[:P]), [fn_bf], [xj, xsq])
        dg = k.ring("xsq_dg", [128, 128], F32, n=1)
        k.op("dve", lambda e: e.tensor_scalar(out=dg[:P, :P], in0=ident_f[:P, :P], scalar1=xsq[:P, 0:1], scalar2=-0.5,
                                              op0=ALU.mult, op1=ALU.mult), [ident_f, xsq], [dg])
        bx = nb()
        k.op("pe", lambda e: e.matmul(bank(bx)[:, 0:P], lhsT=ones_f[:P, :], rhs=dg[:P, :P], start=True, stop=True),
             [ones_f, dg], [pbuf[bx]])
        xsqB = k.ring("xsqB", [128, 128], F32, n=2)
        k.op("act", lambda e: e.copy(out=xsqB[:, :P], in_=bank(bx)[:, 0:P]), [pbuf[bx]], [xsqB])
        yield
        qpT = k.ring("qpT", [128, 16, 128], BF16, n=1)
        for g in range(4):
            bb = nb()
            for j in range(4):
                mm_feat(bb, j, P, wq_bf, (g * 4 + j) * 128, fnT)
            k.op("act", lambda e: e.copy(out=qpT[:, g * 4:g * 4 + 4, :P],
                                         in_=bank(bb).rearrange("p (c t) -> p c t", c=4)[:, :, :P]), [pbuf[bb]], [qpT])
            yield
        s_sb = k.ring("s_sb", [128, 16, 128], F32, n=1)
        for g in range(4):
            bb = nb()
            for j in range(4):
                hj = g * 4 + j
                k.op("pe", lambda e, j=j, hj=hj: e.matmul(bank(bb)[:P, j * 128:(j + 1) * 128], lhsT=qpT[:, hj, :P],
                                                          rhs=skT[:, hj, :], start=True, stop=True), [qpT, skT], [pbuf[bb]])
            k.op("act", lambda e: e.copy(out=s_sb[:P, g * 4:g * 4 + 4, :],
                                         in_=bank(bb).rearrange("p (c t) -> p c t", c=4)[:P]), [pbuf[bb]], [s_sb])
            yield
        v16 = k.ring("v16", [128, 16, 16], F32, n=1)
        i16 = k.ring("i16", [128, 16, 16], U32, n=1)
        for hj in range(16):
            for r in range(2):
                k.op("dve", lambda e: e.max(out=v16[:P, hj, r * 8:r * 8 + 8], in_=s_sb[:P, hj, :]), [s_sb], [v16])
                yield
                k.op("dve", lambda e: e.max_index(out=i16[:P, hj, r * 8:r * 8 + 8], in_max=v16[:P, hj, r * 8:r * 8 + 8],
                                                  in_values=s_sb[:P, hj, :]), [s_sb, v16], [i16])
                yield
                if r == 0:
                    k.op("dve", lambda e: e.match_replace(out=s_sb[:P, hj, :], in_to_replace=v16[:P, hj, 0:8],
                                                          in_values=s_sb[:P, hj, :], imm_value=-1e30), [v16], [s_sb])
                    yield
        i16f = k.ring("i16f", [128, 8, 2, 16], F32, n=1)
        k.op("dve", lambda e: e.tensor_copy(out=i16f[:P].rearrange("p h j k -> p (h j) k"), in_=i16[:P]), [i16], [i16f])
        v4 = v16[:P].rearrange("p (h j) k -> p h j k", j=2)
        cand = k.ring("cand", [128, 8, 16, 16], F32, n=1)
        k.op("dve", lambda e: e.tensor_tensor(out=cand[:P], in0=bc(v4[:, :, 0, :], [P, 8, 16, 16], 3),
                                              in1=bc(v4[:, :, 1, :], [P, 8, 16, 16], 2), op=ALU.add), [v16], [cand])
        yield
        tv = k.ring("tv", [128, 8, 16], F32, n=1)
        tp = k.ring("tp", [128, 8, 16], U32, n=1)
        for h in range(8):
            cf = cand[:P, h].rearrange("p a b -> p (a b)")
            for r in range(2):
                k.op("dve", lambda e: e.max(out=tv[:P, h, r * 8:r * 8 + 8], in_=cf), [cand], [tv])
                yield
                k.op("dve", lambda e: e.max_index(out=tp[:P, h, r * 8:r * 8 + 8], in_max=tv[:P, h, r * 8:r * 8 + 8],
                                                  in_values=cf), [cand, tv], [tp])
                yield
                if r == 0:
                    k.op("dve", lambda e: e.match_replace(out=cf, in_to_replace=tv[:P, h, 0:8], in_values=cf,
                                                          imm_value=-1e30), [tv], [cand])
                    yield
        ta = k.ring("ta", [128, 8, 16], U32, n=1)
        tb = k.ring("tb", [128, 8, 16], U32, n=1)
        k.op("dve", lambda e: e.tensor_single_scalar(out=ta[:P], in_=tp[:P], scalar=4, op=ALU.logical_shift_right), [tp], [ta])
        k.op("dve", lambda e: e.tensor_single_scalar(out=tb[:P], in_=tp[:P], scalar=15, op=ALU.bitwise_and), [tp], [tb])
        taf = k.ring("taf", [128, 8, 16], F32, n=1)
        tbf = k.ring("tbf", [128, 8, 16], F32, n=1)
        k.op("dve", lambda e: e.tensor_copy(out=taf[:P], in_=ta[:P]), [ta], [taf])
        k.op("dve", lambda e: e.tensor_copy(out=tbf[:P], in_=tb[:P]), [tb], [tbf])
        yield
        idxf = k.ring("idxf", [128, 8, 16], F32, n=1)
        sel = {}
        for nm, tf, jj in (("a", taf, 0), ("b", tbf, 1)):
            oh = k.ring("cand", [128, 8, 16, 16], F32, n=1)
            io = iota16[:P, :].unsqueeze(1).unsqueeze(1).broadcast_to([P, 8, 16, 16])
            k.op("dve", lambda e: e.tensor_tensor(out=oh[:P], in0=io, in1=bc(tf[:P], [P, 8, 16, 16], 3), op=ALU.is_equal),
                 [iota16, tf], [oh])
            yield
            k.op("dve", lambda e: e.tensor_tensor(out=oh[:P], in0=oh[:P], in1=bc(i16f[:P, :, jj, :], [P, 8, 16, 16], 2),
                                                  op=ALU.mult), [i16f], [oh])
            yield
            sl = k.ring("sel_" + nm, [128, 8, 16], F32, n=1)
            k.op("dve", lambda e: e.tensor_reduce(out=sl[:P], in_=oh[:P], axis=AX.X, op=ALU.add), [oh], [sl])
            sel[nm] = sl
            yield
        k.op("dve", lambda e: e.scalar_tensor_tensor(out=idxf[:P], in0=sel["a"][:P], scalar=128.0, in1=sel["b"][:P],
                                                     op0=ALU.mult, op1=ALU.add), [sel["a"], sel["b"]], [idxf])
        gsh = k.ring("gsh", [128, 8, 16], F32, n=1)
        k.op("dve", lambda e: e.tensor_tensor(out=gsh[:P], in0=tv[:P], in1=bc(tv[:P, :, 0], [P, 8, 16], 2), op=ALU.subtract),
             [tv], [gsh])
        k.op("act", lambda e: e.activation(out=gsh[:P], in_=gsh[:P], func=AF.Exp), [], [gsh])
        gsm = k.ring("gsm", [128, 8], F32, n=1)
        k.op("dve", lambda e: e.tensor_reduce(out=gsm[:P], in_=gsh[:P], axis=AX.X, op=ALU.add), [gsh], [gsm])
        grs = k.ring("grs", [128, 8], F32, n=1)
        k.op("dve", lambda e: e.reciprocal(out=grs[:P], in_=gsm[:P]), [gsm], [grs])
        gates = k.ring("gates", [128, 8, 16], F32, n=1)
        k.op("dve", lambda e: e.tensor_tensor(out=gates[:P], in0=gsh[:P], in1=bc(grs[:P, :], [P, 8, 16], 2), op=ALU.mult),
             [gsh, grs], [gates])
        yield
        if debug:
            dump("idxf", idxf[:P].rearrange("p h k -> p (h k)"), idxf, ti)
            dump("gates", gates[:P].rearrange("p h k -> p (h k)"), gates, ti)
        bt = nb()
        k.op("pe", lambda e: e.matmul(bank(bt)[:, 0:P], lhsT=idxf[:P].rearrange("p h k -> p (h k)"),
                                      rhs=ident_f[:P, :P], start=True, stop=True), [idxf, ident_f], [pbuf[bt]])
        k.op("pe", lambda e: e.matmul(bank(bt)[:, 128:128 + P], lhsT=gates[:P].rearrange("p h k -> p (h k)"),
                                      rhs=ident_f[:P, :P], start=True, stop=True), [gates, ident_f], [pbuf[bt]])
        idxT = k.ring("idxT", [128, 128], I32, n=2)
        gT = k.ring("gT", [128, 128], F32, n=2)
        idxTf = k.ring("idxTf", [128, 128], F32, n=1)
        k.op("act", lambda e: e.copy(out=idxTf[:, :P], in_=bank(bt)[:, 0:P]), [pbuf[bt]], [idxTf])
        k.op("act", lambda e: e.copy(out=gT[:, :P], in_=bank(bt)[:, 128:128 + P]), [pbuf[bt]], [gT])
        k.op("dve", lambda e: e.tensor_copy(out=idxT[:, :P], in_=idxTf[:, :P]), [idxTf], [idxT])
        res.update(ht=ht, fn_bf=fn_bf, idxT=idxT, gT=gT, P=P, r0=r0, xsqB=xsqB)
        yield

    def drain(gen):
        if gen is not None:
            for _ in gen:
                pass

    GA = 4
    results = [dict() for _ in range(NT + 1)]
    drain(routing(0, results[0]))
    for ti in range(NT):
        R = results[ti]
        ht, fn_bf, idxT, gT, P, r0, xsqB = R["ht"], R["fn_bf"], R["idxT"], R["gT"], R["P"], R["r0"], R["xsqB"]
        nxt = routing(ti + 1, results[ti + 1]) if ti + 1 < NT else None
        if debug:
            dump("idxT", idxT[:, :P], idxT, ti)
        bo0 = nb2()
        hold(bo0), hold(bo0 + 1)
        UVs, pairs, acols, gcols = {}, {}, {}, {}
        for i in range(-GA, P):
            tg = i + GA
            if 0 <= tg < P:
                UV = k.ring("UVg", [128, UVW], BF16, n=GA + 2)
                UVs[tg] = UV
                k.dma("pool", None, None, reads=[idxT, uv_buf], writes=[UV], fn=lambda e: e.indirect_dma_start(
                    out=UV[:], out_offset=None, in_=uv_d,
                    in_offset=bass.IndirectOffsetOnAxis(ap=idxT[:, tg:tg + 1], axis=0)))
            tb_ = i + 2
            if 0 <= tb_ < P:
                b0 = nb2()
                hold(b0), hold(b0 + 1)
                pairs[tb_] = b0
                UVb = UVs[tb_]
                for half in range(2):
                    k.op("pe", lambda e: e.matmul(bank(b0 + half)[:, :], lhsT=ident[:P, tb_:tb_ + 1].broadcast_to([P, 128]),
                                                  rhs=fn_bf[:P, half * 512:(half + 1) * 512], start=True, stop=False),
                         [ident, fn_bf], [pbuf[b0 + half]])
                    k.op("pe", lambda e: e.matmul(bank(b0 + half)[:, :], lhsT=ident[:, :],
                                                  rhs=UVb[:, half * 512:(half + 1) * 512], start=False, stop=True),
                         [ident, UVb], [pbuf[b0 + half]])
            t = i
            Wsel = None
            if 0 <= t < P:
                gcol = gcols.pop(t)
                wcol = k.ring("wcol", [128, 1], F32, n=4)
                k.op("dve", lambda e: e.tensor_tensor(out=wcol[:], in0=gcol[:], in1=gT[:, t:t + 1], op=ALU.mult),
                     [gcol, gT], [wcol])
                Wsel = k.ring("Wsel", [128, 128], BF16, n=4)
                k.op("dve", lambda e: e.tensor_scalar(out=Wsel[:, :P], in0=identsel[:, t, :P], scalar1=wcol[:, 0:1],
                                                      scalar2=None, op0=ALU.mult), [identsel, wcol], [Wsel])
            td = i + 1
            if 0 <= td < P:
                b0 = pairs.pop(td)
                UVd = UVs[td]
                negh = k.ring("negh", [128, 1], F32, n=4)
                k.op("dve", lambda e: e.scalar_tensor_tensor(out=negh[:], in0=UVd[:, 2 * D:2 * D + 1],
                                                             scalar=xsqB[:, td:td + 1], in1=UVd[:, 2 * D + 1:2 * D + 2],
                                                             op0=ALU.add, op1=ALU.add), [UVd, xsqB], [negh])
                junk = k.ring("pjunk", [128, D], BF16, n=2)
                acol = k.ring("acol", [128, 1], F32, n=4)
                k.op("act", lambda e: e.activation(out=junk[:], in_=pp[b0 // 2][:, :], func=AF.Square, accum_out=acol[:, 0:1]),
                     [pbuf[b0], pbuf[b0 + 1]], [junk, acol])
                release(b0), release(b0 + 1)
                gcol = k.ring("gcol", [128, 1], F32, n=4)
                k.op("act", lambda e: e.activation(out=gcol[:], in_=acol[:], func=AF.Gelu, scale=0.5, bias=negh[:, 0:1]),
                     [acol, negh], [gcol])
                gcols[td] = gcol
            if 0 <= t < P:
                UV = UVs.pop(t)
                for half in range(2):
                    k.op("pe", lambda e: e.matmul(bank(bo0 + half)[:P, :], lhsT=Wsel[:, :P],
                                                  rhs=UV[:, D + half * 512:D + (half + 1) * 512],
                                                  start=(t == 0), stop=(t == P - 1)), [Wsel, UV], [pbuf[bo0 + half]])
                for _ in range(2 if (t % 3 == 0 or P < 128) else 1):
                    if nxt is not None:
                        try:
                            next(nxt)
                        except StopIteration:
                            nxt = None
        drain(nxt)
        k.op("dve", lambda e: e.tensor_tensor(out=ht[:P], in0=pp[bo0 // 2][:P, :], in1=ht[:P], op=ALU.add),
             [pbuf[bo0], pbuf[bo0 + 1]], [ht])
        release(bo0), release(bo0 + 1)
        if debug:
            k.dma("sp", dbg_h3[r0:r0 + P, :], ht[:P], reads=[ht], writes=[dbg_buf])
        yt = k.ring("yt", [128, D], F32, n=1)
        rmsnorm_bf(ht, P, 4, None, out_f=yt)
        k.dma("sp", y_all[r0:r0 + P, :], yt[:P], reads=[yt], writes=[out_bufs[0]])

    k.finish(out_bufs + ([dbg_buf] if debug else []) + list(dumps.values()))
    return nc


def host_consts(n_ptiles=16):
    TP = n_ptiles * 128
    TT = TP + PS
    c = {}
    c["c_ident"] = np.eye(128, dtype=np.float32)
    tri = np.zeros((2, 128, 128), np.float32)
    utri = np.zeros((2, 128, 128), np.float32)
    s = np.arange(128)[:, None]
    t = np.arange(128)[None, :]
    tri[0] = (s <= t)
    utri[0] = (s > t)
    same = (s // DEC_S == t // DEC_S) & (s < PS) & (t < PS)
    tri[1] = (s <= t) & same
    utri[1] = (s > t) & same
    c["c_tri"], c["c_utri"] = tri, utri
    si = np.zeros((2, 128, 16), np.float32)
    si[0, :, 0] = 1.0
    for b in range(SB_PER_CORE):
        si[1, b * DEC_S:(b + 1) * DEC_S, b] = 1.0
    c["c_seqind"] = si
    siT = np.zeros((16, 64), np.float32)
    for b in range(SB_PER_CORE):
        siT[b, b * DEC_S:(b + 1) * DEC_S] = 1.0
    c["c_seqindT"] = np.ascontiguousarray(np.broadcast_to(siT.reshape(1, -1), (64, 16 * 64))).astype(ml_dtypes.bfloat16)
    half = 32
    inv = (10000.0 ** (-np.arange(half, dtype=np.float32) / half)).astype(np.float32)
    pos = np.concatenate([np.arange(TP), PAST + (np.arange(PS) % DEC_S)]).astype(np.float32)
    tl = np.concatenate([np.arange(TP) % 128, np.arange(PS) % DEC_S]).astype(np.float64)
    L = np.concatenate([np.full(TP, 128.0), np.full(PS, float(DEC_S))])
    ang = (pos[:, None] * inv[None, :]).astype(np.float32)
    cos = np.cos(ang).astype(np.float64)
    sin = np.sin(ang).astype(np.float64)
    lg = np.log1p(-(2.0 ** (-5.0 - np.arange(4, dtype=np.float64))))
    fq = np.exp(lg[None, :] * (tl[:, None] + 1.0))
    fk = np.exp(-lg[None, :] * (tl[:, None] + 1.0)) * 0.125
    fkd = np.exp(lg[None, :] * (L[:, None] - 1.0 - tl[:, None])) * 0.125
    rot = np.zeros((TT, 6, 4, 32), np.float64)
    for i, f in enumerate((fq, fk, fkd)):
        rot[:, 2 * i] = cos[:, None, :] * f[:, :, None]
        rot[:, 2 * i + 1] = sin[:, None, :] * f[:, :, None]
    c["c_rot"] = rot.reshape(TT, 768).astype(np.float32)
    dr = np.zeros((2, 64, 4), np.float64)
    dr[0] = np.exp(lg * 128.0)[None, :]
    dr[1] = np.exp(lg * float(DEC_S))[None, :]
    c["c_decret"] = dr.astype(np.float32)
    c["c_iota16"] = np.ascontiguousarray(np.broadcast_to(np.arange(16, dtype=np.float32)[None, :], (128, 16)))
    c["c_identsel"] = np.ascontiguousarray(np.broadcast_to(np.eye(128, dtype=np.float32).reshape(1, -1), (128, 128 * 128))).astype(ml_dtypes.bfloat16)
    return c


def make_in_maps(inp, n_ptiles=16, cores=8):
    f = lambda a: np.ascontiguousarray(np.asarray(a, dtype=np.float32))
    consts = host_consts(n_ptiles)
    TP = n_ptiles * 128
    shared = {
        "w_in": f(inp["w_in"][0]), "w_a2": f(inp["w_a2"][0]), "b_a": f(inp["b_a"][0]).reshape(1, 256),
        "gains": f(np.stack([inp["norm_mix"][0], inp["norm_xattn"][0], inp["norm_mem"][0], inp["norm_ffn"][0],
                             inp["norm_final"]])),
        "hgains": f(np.stack([inp["gla_head_norm"][0].reshape(512), inp["ret_head_norm"][0].reshape(512)])),
        "w_pa": f(inp["w_pa"][0]), "w_pb": f(inp["w_pb"][0]), "w_o": f(inp["w_o"][0]),
        "w_xq": f(inp["w_xq"][0]), "w_xk": f(inp["w_xk"][0]), "w_xv": f(inp["w_xv"][0]), "w_xo": f(inp["w_xo"][0]),
        "peer_wq": f(inp["peer_wq"][0]), "peer_sk": f(inp["peer_subkeys"][0]).reshape(16, 128, 128),
        "peer_u": f(inp["peer_u"][0]), "peer_v": f(inp["peer_v"][0]),
    }
    shared.update(consts)
    maps = []
    for c in range(cores):
        sb = slice(c * SB_PER_CORE, (c + 1) * SB_PER_CORE)
        m = dict(shared)
        m["x_all"] = f(np.concatenate([np.asarray(inp["x_prompt"][c])[:TP], np.asarray(inp["x_sample"][sb]).reshape(PS, D)], 0))
        m["mem"] = f(inp["mem_prompt"][c])
        m["st_gla"] = f(inp["state_gla"][0][sb])
        m["st_ret"] = f(inp["state_ret"][0][sb])
        m["ck"] = f(np.asarray(inp["cache_mem_k"][0][sb]).reshape(SB_PER_CORE, NMEM, D))
        m["cv"] = f(np.asarray(inp["cache_mem_v"][0][sb]).reshape(SB_PER_CORE, NMEM, D))
        maps.append(m)
    return maps


_NC_CACHE = {}


def kernel(**inp):
    if "nc" not in _NC_CACHE:
        _NC_CACHE["nc"] = build()
    nc = _NC_CACHE["nc"]
    maps = make_in_maps(inp)
    res = run_bass_kernel_spmd(nc, maps, core_ids=list(range(8)))
    R = res.results
    y_p = np.stack([r["y_all"][:SEQ] for r in R], 0)
    y_s = np.concatenate([r["y_all"][SEQ:].reshape(SB_PER_CORE, DEC_S, D) for r in R], 0)
    gla_p = np.stack([r["o_gla_p"] for r in R], 0)[None]
    ret_p = np.stack([r["o_ret_p"] for r in R], 0)[None]
    mk = np.stack([r["o_mk"].reshape(NMEM, 4, 256) for r in R], 0)[None]
    mv = np.stack([r["o_mv"].reshape(NMEM, 4, 256) for r in R], 0)[None]
    gla_s = np.concatenate([r["o_gla_s"] for r in R], 0)[None]
    ret_s = np.concatenate([r["o_ret_s"] for r in R], 0)[None]
    return tuple(np.ascontiguousarray(a, dtype=np.float32) for a in (y_p, y_s, gla_p, ret_p, mk, mv, gla_s, ret_s))
```
